# Optimizing a Trainium2 kernel written in Bass

```python
import math
import jax, jax.numpy as jnp
from jax import lax
import numpy as np

D_MODEL = 1024
BATCH = 8
SEQ = 2048
DEPTH = 1
DEC_BATCH = 16
DEC_SEQ = 32
PAST_LEN = 1024

CHUNK = 64
Q_BLOCK = 128
SSD_HEADS = 16
SSD_HEAD_DIM = 64
SSD_D_INNER = SSD_HEADS * SSD_HEAD_DIM
SSD_GROUPS = 2
SSD_HEADS_PER_GROUP = SSD_HEADS // SSD_GROUPS
SSD_STATE = 128
CONV_W = 4
SSD_CONV_DIM = SSD_D_INNER + 2 * SSD_GROUPS * SSD_STATE
MLA_HEADS = 8
Q_LORA = 512
KV_LORA = 512
QK_NOPE = 128
QK_ROPE = 64
V_HEAD = 128
MLA_WIDTH = MLA_HEADS * V_HEAD
ROPE_THETA = 10000.0
D_MIX = SSD_D_INNER + MLA_WIDTH
D_IN_PROJ = SSD_D_INNER + SSD_CONV_DIM + SSD_HEADS + Q_LORA + KV_LORA + QK_ROPE
D_FF = 2816
ALPHA = (2 * DEPTH) ** 0.25
BETA = (8 * DEPTH) ** -0.25
EPS = 1e-5

kernel_name = "hybrid_ssd_mla_streaming_step"


def layer_norm(x, g, b):
    xf = x.astype(jnp.float32)
    mu = jnp.mean(xf, -1, keepdims=True)
    var = jnp.mean(jnp.square(xf - mu), -1, keepdims=True)
    return ((xf - mu) * lax.rsqrt(var + EPS) * g + b).astype(x.dtype)


def rms_norm(x, g):
    xf = x.astype(jnp.float32)
    return (xf * lax.rsqrt(jnp.mean(jnp.square(xf), -1, keepdims=True) + EPS) * g).astype(x.dtype)


def swiglu(x, w_gate, w_up, w_down):
    return (jax.nn.silu(x @ w_gate) * (x @ w_up)) @ w_down


def rope_cos_sin(pos):
    inv = ROPE_THETA ** (-jnp.arange(0, QK_ROPE, 2, dtype=jnp.float32) / QK_ROPE)
    ang = pos.astype(jnp.float32)[:, None] * inv[None, :]
    return jnp.cos(ang), jnp.sin(ang)


def rotate(x, cos, sin):
    x1, x2 = jnp.split(x.astype(jnp.float32), 2, axis=-1)
    return jnp.concatenate([x1 * cos - x2 * sin, x1 * sin + x2 * cos], -1).astype(x.dtype)


def causal_conv(xpad, w, b):
    L = xpad.shape[1] - (CONV_W - 1)
    out = b
    for k in range(CONV_W):
        out = out + xpad[:, k:k + L] * w[k]
    return out


def ssd_scan(xdt, dA, Bm, Cm, h0, chunk):
    b, L, H, P = xdt.shape
    G, Hg, N = SSD_GROUPS, SSD_HEADS_PER_GROUP, SSD_STATE
    nc = L // chunk
    dt_ = xdt.dtype
    x = xdt.reshape(b, nc, chunk, G, Hg, P)
    Bc = Bm.reshape(b, nc, chunk, G, N)
    Cc = Cm.reshape(b, nc, chunk, G, N)
    a_cs = jnp.cumsum(dA.reshape(b, nc, chunk, H), axis=2)
    seg = a_cs[:, :, :, None, :] - a_cs[:, :, None, :, :]
    tri = jnp.tril(jnp.ones((chunk, chunk), bool))[None, None, :, :, None]
    Lm = jnp.exp(jnp.where(tri, seg, -jnp.inf)).astype(dt_).reshape(b, nc, chunk, chunk, G, Hg)
    cb = jnp.einsum('bcign,bcjgn->bcijg', Cc, Bc)
    y_diag = jnp.einsum('bcijg,bcijgh,bcjghp->bcighp', cb, Lm, x)
    decay_states = jnp.exp(a_cs[:, :, -1:, :] - a_cs).astype(dt_).reshape(b, nc, chunk, G, Hg)
    states = jnp.einsum('bcjgn,bcjgh,bcjghp->bcghpn', Bc, decay_states, x)
    chunk_decay = jnp.exp(a_cs[:, :, -1, :]).astype(dt_).reshape(b, nc, G, Hg)

    def step(h, inp):
        dec, st = inp
        return dec[..., None, None] * h + st, h

    h_init = h0.astype(states.dtype).reshape(b, G, Hg, P, N)
    h_final, h_prev = lax.scan(step, h_init, (jnp.moveaxis(chunk_decay, 1, 0), jnp.moveaxis(states, 1, 0)))
    h_prev = jnp.moveaxis(h_prev, 0, 1)
    in_decay = jnp.exp(a_cs).astype(dt_).reshape(b, nc, chunk, G, Hg)
    y_off = jnp.einsum('bcign,bcigh,bcghpn->bcighp', Cc, in_decay, h_prev)
    y = (y_diag + y_off).reshape(b, L, H, P)
    return y, h_final.reshape(b, H, P, N)


def mla_attend(q_nope, q_rope, k_nope, k_rope, v, q_pos, k_pos):
    s = jnp.einsum('bqhd,bkhd->bhqk', q_nope, k_nope) + jnp.einsum('bqhd,bkd->bhqk', q_rope, k_rope)
    s = s.astype(jnp.float32) * ((QK_NOPE + QK_ROPE) ** -0.5)
    mask = (k_pos // CHUNK)[None, :] <= (q_pos // CHUNK)[:, None]
    s = jnp.where(mask, s, -1e30)
    p = jax.nn.softmax(s, axis=-1).astype(v.dtype)
    return jnp.einsum('bhqk,bkhd->bqhd', p, v)


def token_mix(h, conv_prev, h0, lat_prev, kr_prev, pos0,
              w_in, conv_w, conv_b, dt_bias, a_log, d_skip, ssd_norm_g,
              q_norm_g, w_uq, kv_norm_g, w_ukv, w_out):
    b, L, _ = h.shape
    proj = h @ w_in
    idx = [SSD_D_INNER, SSD_D_INNER + SSD_CONV_DIM, SSD_D_INNER + SSD_CONV_DIM + SSD_HEADS,
           SSD_D_INNER + SSD_CONV_DIM + SSD_HEADS + Q_LORA,
           SSD_D_INNER + SSD_CONV_DIM + SSD_HEADS + Q_LORA + KV_LORA]
    z, xbc, dt_raw, c_q, c_kv, k_r = jnp.split(proj, idx, axis=-1)

    xpad = jnp.concatenate([conv_prev.astype(xbc.dtype), xbc], axis=1)
    new_conv = xpad[:, -(CONV_W - 1):]
    xbc_c = jax.nn.silu(causal_conv(xpad, conv_w, conv_b))
    xs, Bm, Cm = jnp.split(xbc_c, [SSD_D_INNER, SSD_D_INNER + SSD_GROUPS * SSD_STATE], axis=-1)
    xs = xs.reshape(b, L, SSD_HEADS, SSD_HEAD_DIM)
    Bm = Bm.reshape(b, L, SSD_GROUPS, SSD_STATE)
    Cm = Cm.reshape(b, L, SSD_GROUPS, SSD_STATE)
    dt = jax.nn.softplus(dt_raw.astype(jnp.float32) + dt_bias)
    A = -jnp.exp(a_log.astype(jnp.float32))
    chunk = min(CHUNK, L)
    y, h_new = ssd_scan(xs * dt[..., None].astype(xs.dtype), dt * A, Bm, Cm, h0, chunk)
    y = y + d_skip[:, None] * xs
    y = y.reshape(b, L, SSD_D_INNER) * jax.nn.silu(z)
    y = rms_norm(y.reshape(b, L, SSD_GROUPS, SSD_D_INNER // SSD_GROUPS),
                 ssd_norm_g.reshape(SSD_GROUPS, SSD_D_INNER // SSD_GROUPS)).reshape(b, L, SSD_D_INNER)

    q_pos = pos0 + jnp.arange(L)
    k_pos = jnp.arange(pos0 + L)
    cos_q, sin_q = rope_cos_sin(q_pos)
    q = (rms_norm(c_q, q_norm_g) @ w_uq).reshape(b, L, MLA_HEADS, QK_NOPE + QK_ROPE)
    q_nope, q_rope = q[..., :QK_NOPE], rotate(q[..., QK_NOPE:], cos_q[:, None, :], sin_q[:, None, :])
    lat_new = rms_norm(c_kv, kv_norm_g)
    kr_new = rotate(k_r, cos_q, sin_q)
    all_lat = jnp.concatenate([lat_prev.astype(lat_new.dtype), lat_new], axis=1)
    all_kr = jnp.concatenate([kr_prev.astype(kr_new.dtype), kr_new], axis=1)
    K = all_lat.shape[1]
    kv = (all_lat @ w_ukv).reshape(b, K, MLA_HEADS, QK_NOPE + V_HEAD)
    k_nope, v = kv[..., :QK_NOPE], kv[..., QK_NOPE:]
    if L > Q_BLOCK and L % Q_BLOCK == 0:
        nb = L // Q_BLOCK
        qn_b = q_nope.reshape(b, nb, Q_BLOCK, MLA_HEADS, QK_NOPE).swapaxes(0, 1)
        qr_b = q_rope.reshape(b, nb, Q_BLOCK, MLA_HEADS, QK_ROPE).swapaxes(0, 1)
        qp_b = q_pos.reshape(nb, Q_BLOCK)
        o = lax.map(lambda a: mla_attend(a[0], a[1], k_nope, all_kr, v, a[2], k_pos), (qn_b, qr_b, qp_b))
        o = o.swapaxes(0, 1).reshape(b, L, MLA_WIDTH)
    else:
        o = mla_attend(q_nope, q_rope, k_nope, all_kr, v, q_pos, k_pos).reshape(b, L, MLA_WIDTH)

    out = jnp.concatenate([y, o], axis=-1) @ w_out
    return out, new_conv, h_new, lat_new, kr_new


def layer(x, conv_prev, h0, lat_prev, kr_prev, pos0,
          ln1_g, ln1_b, ffn1_w_gate, ffn1_w_up, ffn1_w_down,
          w_in, conv_w, conv_b, dt_bias, a_log, d_skip, ssd_norm_g,
          q_norm_g, w_uq, kv_norm_g, w_ukv, w_out, ln2_g, ln2_b,
          ffn2_w_gate, ffn2_w_up, ffn2_w_down, ln3_g, ln3_b):
    x = layer_norm(ALPHA * x + 0.5 * swiglu(x, ffn1_w_gate, ffn1_w_up, ffn1_w_down), ln1_g, ln1_b)
    mix, new_conv, h_new, lat_new, kr_new = token_mix(
        x, conv_prev, h0, lat_prev, kr_prev, pos0, w_in, conv_w, conv_b, dt_bias, a_log, d_skip,
        ssd_norm_g, q_norm_g, w_uq, kv_norm_g, w_ukv, w_out)
    x = layer_norm(ALPHA * x + mix, ln2_g, ln2_b)
    x = layer_norm(ALPHA * x + 0.5 * swiglu(x, ffn2_w_gate, ffn2_w_up, ffn2_w_down), ln3_g, ln3_b)
    return x, new_conv, h_new, lat_new, kr_new


def setup_inputs(seed: int = 0) -> dict:
    key = jax.random.key(seed)
    ks = jax.random.split(key, 40)
    f32 = jnp.float32

    def nrm(k, shape, scale):
        return jax.random.normal(k, shape, f32) * scale

    def gain(k, shape):
        return 1.0 + 0.02 * jax.random.normal(k, shape, f32)

    dt0 = jnp.exp(jax.random.uniform(ks[20], (DEPTH, SSD_HEADS), f32, math.log(1e-3), math.log(1e-1)))
    return {
        "x_prompt": nrm(ks[0], (BATCH, SEQ, D_MODEL), 1.0),
        "x_sample": nrm(ks[1], (DEC_BATCH, DEC_SEQ, D_MODEL), 1.0),
        "cache_latent": nrm(ks[2], (DEPTH, DEC_BATCH, PAST_LEN, KV_LORA), 1.0),
        "cache_k_rope": nrm(ks[3], (DEPTH, DEC_BATCH, PAST_LEN, QK_ROPE), 1.0),
        "state_conv": nrm(ks[4], (DEPTH, DEC_BATCH, CONV_W - 1, SSD_CONV_DIM), 1.0),
        "state_ssm": nrm(ks[5], (DEPTH, DEC_BATCH, SSD_HEADS, SSD_HEAD_DIM, SSD_STATE), 0.1),
        "ln1_g": gain(ks[6], (DEPTH, D_MODEL)),
        "ln1_b": nrm(ks[7], (DEPTH, D_MODEL), 0.02),
        "ffn1_w_gate": nrm(ks[8], (DEPTH, D_MODEL, D_FF), D_MODEL ** -0.5),
        "ffn1_w_up": nrm(ks[9], (DEPTH, D_MODEL, D_FF), D_MODEL ** -0.5),
        "ffn1_w_down": nrm(ks[10], (DEPTH, D_FF, D_MODEL), BETA * D_FF ** -0.5),
        "w_in": nrm(ks[11], (DEPTH, D_MODEL, D_IN_PROJ), D_MODEL ** -0.5),
        "conv_w": nrm(ks[12], (DEPTH, CONV_W, SSD_CONV_DIM), CONV_W ** -0.5),
        "conv_b": nrm(ks[13], (DEPTH, SSD_CONV_DIM), 0.02),
        "dt_bias": dt0 + jnp.log(-jnp.expm1(-dt0)),
        "a_log": jnp.log(jax.random.uniform(ks[14], (DEPTH, SSD_HEADS), f32, 1.0, 16.0)),
        "d_skip": gain(ks[15], (DEPTH, SSD_HEADS)),
        "ssd_norm_g": gain(ks[16], (DEPTH, SSD_D_INNER)),
        "q_norm_g": gain(ks[17], (DEPTH, Q_LORA)),
        "w_uq": nrm(ks[18], (DEPTH, Q_LORA, MLA_HEADS * (QK_NOPE + QK_ROPE)), Q_LORA ** -0.5),
        "kv_norm_g": gain(ks[19], (DEPTH, KV_LORA)),
        "w_ukv": nrm(ks[21], (DEPTH, KV_LORA, MLA_HEADS * (QK_NOPE + V_HEAD)), KV_LORA ** -0.5),
        "w_out": nrm(ks[22], (DEPTH, D_MIX, D_MODEL), BETA * D_MIX ** -0.5),
        "ln2_g": gain(ks[23], (DEPTH, D_MODEL)),
        "ln2_b": nrm(ks[24], (DEPTH, D_MODEL), 0.02),
        "ffn2_w_gate": nrm(ks[25], (DEPTH, D_MODEL, D_FF), D_MODEL ** -0.5),
        "ffn2_w_up": nrm(ks[26], (DEPTH, D_MODEL, D_FF), D_MODEL ** -0.5),
        "ffn2_w_down": nrm(ks[27], (DEPTH, D_FF, D_MODEL), BETA * D_FF ** -0.5),
        "ln3_g": gain(ks[28], (DEPTH, D_MODEL)),
        "ln3_b": nrm(ks[29], (DEPTH, D_MODEL), 0.02),
    }


def reference(x_prompt, x_sample, cache_latent, cache_k_rope, state_conv, state_ssm,
              ln1_g, ln1_b, ffn1_w_gate, ffn1_w_up, ffn1_w_down,
              w_in, conv_w, conv_b, dt_bias, a_log, d_skip, ssd_norm_g,
              q_norm_g, w_uq, kv_norm_g, w_ukv, w_out, ln2_g, ln2_b,
              ffn2_w_gate, ffn2_w_up, ffn2_w_down, ln3_g, ln3_b):
    xp, xs = x_prompt, x_sample
    bp = xp.shape[0]
    lat_p, kr_p, conv_p, ssm_p = [], [], [], []
    lat_s, kr_s, conv_s, ssm_s = [], [], [], []
    for l in range(DEPTH):
        lw = (ln1_g[l], ln1_b[l], ffn1_w_gate[l], ffn1_w_up[l], ffn1_w_down[l],
              w_in[l], conv_w[l], conv_b[l], dt_bias[l], a_log[l], d_skip[l], ssd_norm_g[l],
              q_norm_g[l], w_uq[l], kv_norm_g[l], w_ukv[l], w_out[l], ln2_g[l], ln2_b[l],
              ffn2_w_gate[l], ffn2_w_up[l], ffn2_w_down[l], ln3_g[l], ln3_b[l])
        xp, c1, h1, la1, kr1 = layer(
            xp, jnp.zeros((bp, CONV_W - 1, SSD_CONV_DIM), xp.dtype),
            jnp.zeros((bp, SSD_HEADS, SSD_HEAD_DIM, SSD_STATE), xp.dtype),
            jnp.zeros((bp, 0, KV_LORA), xp.dtype), jnp.zeros((bp, 0, QK_ROPE), xp.dtype), 0, *lw)
        xs, c2, h2, la2, kr2 = layer(
            xs, state_conv[l], state_ssm[l], cache_latent[l], cache_k_rope[l], PAST_LEN, *lw)
        lat_p.append(la1); kr_p.append(kr1); conv_p.append(c1); ssm_p.append(h1)
        lat_s.append(la2); kr_s.append(kr2); conv_s.append(c2); ssm_s.append(h2)
    return (xp, xs,
            jnp.stack(lat_p), jnp.stack(kr_p), jnp.stack(conv_p), jnp.stack(ssm_p),
            jnp.stack(lat_s), jnp.stack(kr_s), jnp.stack(conv_s), jnp.stack(ssm_s))
```

```python
import math
from contextlib import ExitStack
import numpy as np
import concourse.bass as bass
import concourse.mybir as mybir
from concourse.bass_utils import run_bass_kernel_spmd

F32 = mybir.dt.float32
BF16 = mybir.dt.bfloat16
AF = mybir.ActivationFunctionType
ALU = mybir.AluOpType
AX = mybir.AxisListType

D = 1024
DFF = 2816
NFF = DFF // 128
NPROMPT = 2048
NSAMP = 64
NT = NPROMPT + NSAMP
PAST = 1024
ALPHA = 2.0 ** 0.25
EPS = 1e-5
DINP = 3664
SUBT = [(128 * s, 128) for s in range(16)] + [(2048, 64)]
TILES = [(512 * t, 512) for t in range(4)] + [(2048, 64)]
SUB_OF_TILE = [[4 * t + i for i in range(4)] for t in range(4)] + [[16]]
ENGS = ("pe", "act", "dve", "pool", "sp")
STOP_B1 = False
MLA_LEVEL = 9
USE_XBAR = False
SSD_W = (1, 2)
ATT_W = (1, 1, 1)
SSD_K = 8
NHEADS_DBG = 8


class T:
    __slots__ = ("name", "w", "r", "after", "excl")

    def __init__(self, name, after=None, excl=False):
        self.name = name
        self.w = None
        self.r = []
        self.after = after
        self.excl = excl


class Op:
    __slots__ = ("eng", "fn", "reads", "writes", "dma", "sig", "deps", "sem", "val", "idx")

    def __init__(self, eng, fn, reads, writes, dma):
        self.eng = eng
        self.fn = fn
        self.reads = reads
        self.writes = writes
        self.dma = dma
        self.sig = dma
        self.deps = []
        self.sem = None
        self.val = 0


class Prog:
    def __init__(self, nc, n_dma_sems=16):
        self.nc = nc
        self.ops = []
        self.n_dma_sems = n_dma_sems
        self.xbar = T("xbar")

    def op(self, eng, fn, reads=(), writes=()):
        o = Op(eng, fn, list(reads), list(writes), False)
        self.ops.append(o)
        return o

    def dma(self, eng, fn, reads=(), writes=()):
        o = Op(eng, fn, list(reads) + [self.xbar], list(writes), True)
        self.ops.append(o)
        return o

    def dmat(self, eng, fn, reads=(), writes=()):
        o = Op(eng, fn, list(reads), list(writes) + [self.xbar], True)
        self.ops.append(o)
        return o

    def build(self):
        nc = self.nc
        for i, x in enumerate(self.ops):
            x.idx = i
            deps = {}
            for t in x.reads + x.writes:
                if t.after is not None:
                    for a in t.after:
                        if a.w is not None:
                            deps[a.w.idx] = a.w
                        for r in a.r:
                            deps[r.idx] = r
                    t.after = None
            for t in x.reads:
                if t.w is not None:
                    deps[t.w.idx] = t.w
                if t.excl:
                    for r in t.r:
                        if r.eng != x.eng:
                            deps[r.idx] = r
            for t in x.writes:
                if t.w is not None:
                    deps[t.w.idx] = t.w
                for r in t.r:
                    if r.eng == x.eng == "pe" and not r.dma and not x.dma:
                        continue
                    deps[r.idx] = r
            deps.pop(i, None)
            for y in deps.values():
                if y.eng == "pe" and x.eng == "pe" and not y.dma and not x.dma:
                    continue
                y.sig = True
                x.deps.append(y)
            for t in x.reads:
                t.r.append(x)
            for t in x.writes:
                t.w = x
                t.r = []
        with ExitStack() as es:
            esem = {e: es.enter_context(nc.semaphore(f"s_{e}")) for e in ENGS}
            dq = ("sp", "pool", "act")
            dsem = {e: [es.enter_context(nc.semaphore(f"d_{e}{k}")) for k in range(self.n_dma_sems)] for e in dq}
            cnt = {e: 0 for e in ENGS}
            dcnt = {e: [0] * self.n_dma_sems for e in dq}
            drr = {e: 0 for e in dq}
            prev_on_sem = {}
            for x in self.ops:
                if x.dma:
                    k = drr[x.eng]
                    drr[x.eng] = (k + 1) % self.n_dma_sems
                    dcnt[x.eng][k] += 16
                    x.sem = dsem[x.eng][k]
                    x.val = dcnt[x.eng][k]
                    key = (x.eng, k)
                    if key in prev_on_sem:
                        x.deps.append(prev_on_sem[key])
                    prev_on_sem[key] = x
                elif x.sig:
                    cnt[x.eng] += 1
                    x.sem = esem[x.eng]
                    x.val = cnt[x.eng]
            self.stats = dict(cnt=dict(cnt), dmax={e: max(dcnt[e]) for e in dq}, nops=len(self.ops))
            per_eng = {e: [o for o in self.ops if o.eng == e] for e in ENGS}
            finals = [(esem[e], cnt[e]) for e in ENGS if cnt[e]]
            for e in dq:
                finals += [(dsem[e][k], dcnt[e][k]) for k in range(self.n_dma_sems) if dcnt[e][k]]

            def emit(e, eng):
                seen = {}
                for x in per_eng[e]:
                    need = {}
                    for y in x.deps:
                        sid = id(y.sem)
                        if seen.get(sid, 0) < y.val and need.get(sid, (None, 0))[1] < y.val:
                            need[sid] = (y.sem, y.val)
                    for sid, (s, v) in need.items():
                        eng.wait_ge(s, v)
                        seen[sid] = v
                    inst = x.fn(eng)
                    if x.sig:
                        inst.then_inc(x.sem, 16 if x.dma else 1)
                if e == "sp":
                    for s, v in finals:
                        if seen.get(id(s), 0) < v:
                            eng.wait_ge(s, v)

            with nc.Block() as block:
                @block.tensor
                def _(eng):
                    emit("pe", eng)

                @block.scalar
                def _(eng):
                    emit("act", eng)

                @block.vector
                def _(eng):
                    emit("dve", eng)

                @block.gpsimd
                def _(eng):
                    emit("pool", eng)

                @block.sync
                def _(eng):
                    emit("sp", eng)


class Arena:
    def __init__(self, nc, lo, hi):
        self.nc = nc
        self.lo = lo
        self.hi = hi
        self.cur = lo
        self.live = []
        self.dead = []
        self.n = 0

    def alloc(self, name, shape, dtype):
        esz = 2 if dtype == BF16 else 4
        nbytes = int(np.prod(shape[1:])) * esz
        off = (self.cur + 63) // 64 * 64
        assert off + nbytes <= self.hi, f"SBUF overflow at {name}: {off + nbytes} > {self.hi}"
        self.cur = off + nbytes
        self.n += 1
        h = self.nc.alloc_sbuf_tensor_at(f"{name}_{self.n}", list(shape), dtype, offset=off)
        rec = (off, off + nbytes, [])
        self.live.append(rec)
        return h, rec

    def newT(self, rec, name):
        lo, hi, lst = rec
        after = []
        for (dlo, dhi, dl) in self.dead:
            if dlo < hi and lo < dhi:
                after += dl
        t = T(name, after if after else None)
        lst.append(t)
        return t

    def mark(self):
        return (self.cur, len(self.live))

    def reset(self, m):
        cur, n = m
        self.dead += self.live[n:]
        del self.live[n:]
        self.cur = cur


def build_program(debug=False):
    nc = bass.Bass("TRN2", target_bir_lowering=False)

    def din(name, shape, dt=F32):
        return nc.dram_tensor(name, list(shape), dt, kind="ExternalInput").ap()

    def dout(name, shape, dt=F32):
        return nc.dram_tensor(name, list(shape), dt, kind="ExternalOutput").ap()

    def dscr(name, shape, dt=F32):
        return nc.dram_tensor(name, list(shape), dt, kind="ExternalOutput" if debug else "Internal").ap()

    xin = din("xin", [NT, D])
    w = {}
    for nm, shp in [("ffn1_w_gate", [D, DFF]), ("ffn1_w_up", [D, DFF]), ("ffn1_w_down", [DFF, D]),
                    ("ffn2_w_gate", [D, DFF]), ("ffn2_w_up", [D, DFF]), ("ffn2_w_down", [DFF, D]),
                    ("w_in", [D, DINP]), ("w_uq", [512, 1536]), ("w_ukv", [512, 2048]), ("w_out", [2048, D])]:
        w[nm] = din(nm, shp)
    lnp = {nm: din(nm, [128, D]) for nm in ["ln1_g", "ln1_b", "ln2_g", "ln2_b", "ln3_g", "ln3_b"]}
    ident_d = din("ident", [128, 128])

    cwp_d = din("cwp", [128, 12, 4])
    cbp_d = din("cbp", [128, 12])
    dtb_d = din("dtb", [128, 16])
    alog_d = din("alog", [128, 16])
    dsk_d = din("dsk", [128, 16])
    gss_d = din("gss", [128, D])
    csts_d = din("csts", [128, 8, 128])
    nmp_d = din("nmp", [128, 512])
    nms_d = din("nms", [128, 256])
    idl_d = din("idl", [128, 2, 64])
    nmlp_d = din("nmlp", [128, 256])
    nmls_d = din("nmls", [128, 128])
    sconv_d = din("sconv", [2, 3, 1536])
    sssm_d = din("sssm", [2, 16, 64, 128])

    gq_d = din("gq", [128, 512])
    gkv_d = din("gkv", [128, 512])
    clat_d = din("clat", [2, PAST, 512])
    ckr_d = din("ckr", [2, PAST, 64])
    cs4_d = din("cs4", [NT, 128])
    cosT_d = din("cosT", [64, NT])
    sinT_d = din("sinT", [64, NT])
    maskq_d = din("maskq", [128, 128])

    y_out = dout("y", [NT, D])
    lat_o = dout("lat_o", [NT, 512])
    kr_o = dout("kr_o", [NT, 64])
    x2d = dscr("x2d", [NT, D])
    conv_o = dout("conv_o", [3, 3, 1536])
    ssm_o = dout("ssm_o", [3, 16, 64, 128])
    x1d = dscr("x1d", [NT, D])
    r1d = dscr("r1d", [NT, D])
    dbg_y = dout("dbg_y", [NT, D]) if debug else None
    dbg_yn = dout("dbg_yn", [NT, D], BF16) if debug else None

    P = Prog(nc)
    A = Arena(nc, 16512, 229344)

    psA = nc.alloc_psum_tensor("psA", [128, 2048], F32)
    psB = nc.alloc_psum_tensor("psB", [128, 2048], F32)

    def bank(i):
        t = psA if i < 4 else psB
        j = i % 4
        return t[:, j * 512:(j + 1) * 512]

    bankT = [T(f"bank{i}", excl=True) for i in range(8)]

    ident_f, r = A.alloc("ident_f", [128, 128], F32)
    t_identf = A.newT(r, "ident_f")
    ident_b, r = A.alloc("ident_b", [128, 128], BF16)
    t_identb = A.newT(r, "ident_b")
    P.dma("sp", lambda e: e.dma_start(out=ident_f[:], in_=ident_d[:, :]), writes=[t_identf])
    P.dma("pool", lambda e: e.dma_start(out=ident_b[:], in_=ident_d[:, :]), writes=[t_identb])

    xT, r = A.alloc("xT", [128, 8, NT], BF16)
    xT_rec = r
    xT_T = [A.newT(r, f"xT{s}") for s in range(17)]

    def emit_x_sub(s, xb):
        t0, m = SUBT[s]
        h, th = xb[s % 2]
        P.dma("pool", lambda e: e.dma_start(out=h[0:m, :], in_=xin[t0:t0 + m, :]), writes=[th])
        bk = 6 + (s % 2)
        pv = bank(bk).bitcast(BF16)

        def tr(e):
            for k in range(8):
                i = e.transpose(out=pv[:, k * 128:k * 128 + m], in_=h[0:m, k * 128:(k + 1) * 128], identity=ident_b[0:m, 0:m])
            return i
        P.op("pe", tr, reads=[th, t_identb], writes=[bankT[bk]])
        src = pv.rearrange("p (k t) -> p k t", k=8)[:, :, 0:m]
        if s % 2 == 0:
            P.op("act", lambda e: e.activation(out=xT[:, :, t0:t0 + m], in_=src, func=AF.Copy), reads=[bankT[bk]], writes=[xT_T[s]])
        else:
            P.op("dve", lambda e: e.tensor_copy(out=xT[:, :, t0:t0 + m], in_=src), reads=[bankT[bk]], writes=[xT_T[s]])


    def interleave_g(gens):
        gens = [g for g in gens if g is not None]
        while gens:
            for g in list(gens):
                try:
                    next(g)
                except StopIteration:
                    gens.remove(g)

    def ln_tail(s, t0, m, yv, t_yv, bufs, lng, t_lng, lnb, t_lnb, out_d, out_T, make_xT, extra_w=()):
        xo, t_xo = bufs["xo"][s % 2]
        xob, t_xob = bufs["xob"][s % 2]
        st, t_st = bufs["st"][s % 2]
        mv, t_mv = bufs["mv"][s % 2]
        rs, t_rs = bufs["rs"][s % 2]
        for half in range(2):
            P.op("dve", lambda e, half=half: e.bn_stats(out=st[0:m, half * 6:(half + 1) * 6], in_=yv[0:m, half * 512:(half + 1) * 512]),
                 reads=[t_yv], writes=[t_st])
        P.op("dve", lambda e: e.bn_aggr(out=mv[0:m, :], in_=st[0:m, :]), reads=[t_st], writes=[t_mv])
        yield
        P.op("act", lambda e: e.activation(out=rs[0:m, 0:1], in_=mv[0:m, 1:2], func=AF.Sqrt, bias=eps_t[0:m, :]),
             reads=[t_mv, t_eps], writes=[t_rs])
        yield
        P.op("dve", lambda e: e.reciprocal(out=rs[0:m, 1:2], in_=rs[0:m, 0:1]), reads=[t_rs], writes=[t_rs])
        P.op("dve", lambda e: e.tensor_scalar(out=yv[0:m, :], in0=yv[0:m, :], scalar1=mv[0:m, 0:1], scalar2=rs[0:m, 1:2],
                                              op0=ALU.subtract, op1=ALU.mult), reads=[t_yv, t_mv, t_rs], writes=[t_yv])
        P.op("dve", lambda e: e.tensor_tensor(out=yv[0:m, :], in0=yv[0:m, :], in1=lng[0:m, :], op=ALU.mult), reads=[t_yv, t_lng], writes=[t_yv])
        P.op("dve", lambda e: e.tensor_tensor(out=xo[0:m, :], in0=yv[0:m, :], in1=lnb[0:m, :], op=ALU.add), reads=[t_yv, t_lnb], writes=[t_xo])
        yield
        P.dma("sp", lambda e: e.dma_start(out=out_d[t0:t0 + m, :], in_=xo[0:m, :]), reads=[t_xo], writes=([out_T[s]] if out_T else []))
        if make_xT:
            P.op("act", lambda e: e.activation(out=xob[0:m, :], in_=xo[0:m, :], func=AF.Copy), reads=[t_xo], writes=[t_xob])
            bk = 6 + (s % 2)
            pv = bank(bk).bitcast(BF16)

            def tr(e):
                for k in range(8):
                    i = e.transpose(out=pv[:, k * 128:k * 128 + m], in_=xob[0:m, k * 128:(k + 1) * 128], identity=ident_b[0:m, 0:m])
                return i
            yield
            P.op("pe", tr, reads=[t_xob, t_identb], writes=[bankT[bk]])
            yield
            src = pv.rearrange("p (k t) -> p k t", k=8)[:, :, 0:m]
            P.op("act", lambda e: e.activation(out=xT[:, :, t0:t0 + m], in_=src, func=AF.Copy),
                 reads=[bankT[bk]], writes=[xT_T[s]] + list(extra_w))
        yield

    def ffn_phase(tag, wg_d, wu_d, wd_d, g_d, b_d, res_d, res_T, out_d, out_T, make_xT, pre_x=False):
        mk = A.mark()
        lng, r = A.alloc("lng", [128, D], F32)
        t_lng = A.newT(r, "lng")
        lnb, r = A.alloc("lnb", [128, D], F32)
        t_lnb = A.newT(r, "lnb")
        P.dma("sp", lambda e: e.dma_start(out=lng[:], in_=g_d[:, :]), writes=[t_lng])
        P.dma("sp", lambda e: e.dma_start(out=lnb[:], in_=b_d[:, :]), writes=[t_lnb])
        hT, r = A.alloc("hT", [128, NFF, NT], BF16)
        hT_T = [[A.newT(r, f"hT{c}_{t}") for t in range(5)] for c in range(NFF)]
        wd, r = A.alloc("wd", [128, NFF, D], BF16)
        wd_T = [A.newT(r, f"wd{c}") for c in range(NFF)]
        mk2 = A.mark()
        wgb, wub, sgb = [], [], []
        for i in range(2):
            h, r = A.alloc(f"wg{i}", [128, 8, 256], BF16)
            wgb.append((h, A.newT(r, f"wg{i}")))
            h, r = A.alloc(f"wu{i}", [128, 8, 256], BF16)
            wub.append((h, A.newT(r, f"wu{i}")))
        for i in range(2):
            h, r = A.alloc(f"sg{i}", [128, 512], F32)
            sgb.append((h, A.newT(r, f"sg{i}")))
        wg_v = wg_d.rearrange("(ko ki) f -> ki ko f", ki=128)
        wu_v = wu_d.rearrange("(ko ki) f -> ki ko f", ki=128)
        wd_v = wd_d.rearrange("(c p) d -> p c d", p=128)

        def load_block(b):
            hg, tg = wgb[b % 2]
            hu, tu = wub[b % 2]
            P.dma("pool", lambda e: e.dma_start(out=hg[:], in_=wg_v[:, :, b * 256:(b + 1) * 256]), writes=[tg])
            P.dma("pool", lambda e: e.dma_start(out=hu[:], in_=wu_v[:, :, b * 256:(b + 1) * 256]), writes=[tu])

        xb = []
        if pre_x:
            for i in range(2):
                h, r = A.alloc(f"xb{i}", [128, D], BF16)
                xb.append((h, A.newT(r, f"xb{i}")))
            for s_ in SUB_OF_TILE[0]:
                emit_x_sub(s_, xb)
        load_block(0)
        it = 0
        for b in range(NFF // 2):
            if b + 1 < NFF // 2:
                load_block(b + 1)
            for c in (2 * b, 2 * b + 1):
                P.dma("pool", lambda e, c=c: e.dma_start(out=wd[:, c, :], in_=wd_v[:, c, :]), writes=[wd_T[c]])
            hg, tg = wgb[b % 2]
            hu, tu = wub[b % 2]
            for hc in range(2):
                c = 2 * b + hc
                for t, (t0, n) in enumerate(TILES):
                    bg, bu = (it % 2), 2 + (it % 2)
                    sg, tsg = sgb[it % 2]
                    it += 1
                    if pre_x and b == 0 and hc == 0 and t > 0:
                        for s_ in SUB_OF_TILE[t]:
                            emit_x_sub(s_, xb)
                    xr = [xT_T[s] for s in SUB_OF_TILE[t]]

                    def mm(e, wt, bk, t0=t0, n=n, hc=hc):
                        for k in range(8):
                            i = e.matmul(bank(bk)[:, 0:n], lhsT=wt[:, k, hc * 128:(hc + 1) * 128], rhs=xT[:, k, t0:t0 + n],
                                         start=(k == 0), stop=(k == 7))
                        return i
                    P.op("pe", lambda e, mm=mm, hg=hg, bg=bg: mm(e, hg, bg), reads=xr + [tg], writes=[bankT[bg]])
                    P.op("pe", lambda e, mm=mm, hu=hu, bu=bu: mm(e, hu, bu), reads=xr + [tu], writes=[bankT[bu]])
                    P.op("act", lambda e, sg=sg, bg=bg, n=n: e.activation(out=sg[:, 0:n], in_=bank(bg)[:, 0:n], func=AF.Silu),
                         reads=[bankT[bg]], writes=[tsg])
                    P.op("dve", lambda e, sg=sg, bu=bu, n=n, c=c, t0=t0: e.tensor_tensor(
                        out=hT[:, c, t0:t0 + n], in0=sg[:, 0:n], in1=bank(bu)[:, 0:n], op=ALU.mult),
                        reads=[tsg, bankT[bu]], writes=[hT_T[c][t]])
        A.reset(mk2)
        bufs = {}
        for nm, shape, dt, nb in [("xres", [128, D], F32, 2), ("yv", [128, D], F32, 2),
                                  ("xo", [128, D], F32, 2), ("xob", [128, D], BF16, 2),
                                  ("st", [128, 12], F32, 2), ("mv", [128, 2], F32, 2), ("rs", [128, 2], F32, 2)]:
            bufs[nm] = []
            for i in range(nb):
                h, r = A.alloc(f"{nm}{i}", shape, dt)
                bufs[nm].append((h, A.newT(r, f"{nm}{i}")))
        def down(s):
            t0, m = SUBT[s]
            t = min(s // 4, 4)
            xres, t_xres = bufs["xres"][s % 2]
            xa, t_xa = xres, t_xres
            yv, t_yv = bufs["yv"][s % 2]
            P.dma("sp", lambda e: e.dma_start(out=xres[0:m, :], in_=res_d[t0:t0 + m, :]),
                  reads=([res_T[s]] if res_T else []), writes=[t_xres])
            P.op("act", lambda e: e.activation(out=xa[0:m, :], in_=xres[0:m, :], func=AF.Copy, scale=ALPHA), reads=[t_xres], writes=[t_xa])
            for half in range(2):
                bk = 4 * (s % 2) + half

                def mmd(e, bk=bk, half=half):
                    for c in range(NFF):
                        i = e.matmul(bank(bk)[0:m, :], lhsT=hT[:, c, t0:t0 + m], rhs=wd[:, c, half * 512:(half + 1) * 512],
                                     start=(c == 0), stop=(c == NFF - 1))
                    return i
                P.op("pe", mmd, reads=[hT_T[c][t] for c in range(NFF)] + wd_T, writes=[bankT[bk]])
                yield
                P.op("dve", lambda e, bk=bk, half=half: e.scalar_tensor_tensor(
                    out=yv[0:m, half * 512:(half + 1) * 512], in0=bank(bk)[0:m, :], scalar=0.5,
                    in1=xa[0:m, half * 512:(half + 1) * 512], op0=ALU.mult, op1=ALU.add),
                    reads=[bankT[bk], t_xa], writes=[t_yv])
                yield

        def tail(s):
            t0, m = SUBT[s]
            yv, t_yv = bufs["yv"][s % 2]
            return ln_tail(s, t0, m, yv, t_yv, bufs, lng, t_lng, lnb, t_lnb, out_d, out_T, make_xT)

        interleave_g([down(0)])
        for s in range(17):
            interleave_g([down(s + 1) if s + 1 < 17 else None, tail(s)])
        A.reset(mk)

    def ncdma(e, out, in_, k):
        with nc.allow_non_contiguous_dma(reason="small strided state transfer"):
            return e.dma_start(out=out[:, :, k:k + 1], in_=in_[:, :, k:k + 1])

    def al(name, shape, dt=F32):
        h, r = A.alloc(name, shape, dt)
        return h, A.newT(r, name)

    def ssd_phase():
        mk = A.mark()
        w_in_v = w["w_in"].rearrange("(ko ki) f -> ki ko f", ki=128)
        w_out_v = w["w_out"].rearrange("(ko ki) f -> ki ko f", ki=128)
        wz, r = A.alloc("wz", [128, 8, 1024], BF16)
        t_wz = [A.newT(r, f"wz{j}") for j in range(2)]
        wx, r = A.alloc("wx", [128, 8, 1536], BF16)
        t_wx = [A.newT(r, f"wx{j}") for j in range(3)]
        wdt, t_wdt = al("wdt", [128, 8, 16], BF16)
        wot, r = A.alloc("wot", [128, 8, 1024], BF16)
        t_wot = [A.newT(r, f"wot{j}") for j in range(2)]
        for j in range(3):
            P.dma("pool", lambda e, j=j: e.dma_start(out=wx[:, :, j * 512:(j + 1) * 512], in_=w_in_v[:, :, 1024 + j * 512:1024 + (j + 1) * 512]), writes=[t_wx[j]])
        for j in range(2):
            P.dma("pool", lambda e, j=j: e.dma_start(out=wz[:, :, j * 512:(j + 1) * 512], in_=w_in_v[:, :, j * 512:(j + 1) * 512]), writes=[t_wz[j]])
        P.dma("pool", lambda e: e.dma_start(out=wdt[:], in_=w_in_v[:, :, 2560:2576]), writes=[t_wdt])
        for j in range(2):
            P.dma("pool", lambda e, j=j: e.dma_start(out=wot[:, :, j * 512:(j + 1) * 512], in_=w_out_v[:, 0:8, j * 512:(j + 1) * 512]), writes=[t_wot[j]])
        cw, t_cw = al("cw", [128, 12, 4])
        cb, t_cb = al("cb", [128, 12])
        dtb, t_dtb = al("dtb", [128, 16])
        Abc, t_Abc = al("Abc", [128, 16])
        dsk, t_dsk = al("dsk", [128, 16])
        gss, t_gss = al("gss", [128, D])
        cst, t_cst = al("cst", [128, 8, 128])
        ones_f, t_ones = al("ones_f", [128, 128])
        one_t, t_one = al("one_t", [128, 1])
        nmp, t_nmp = al("nmp", [128, 512], BF16)
        nms, t_nms = al("nms", [128, 256], BF16)
        idl, t_idl = al("idl", [128, 2, 64])
        nmlp, t_nmlp = al("nmlp", [128, 256], BF16)
        nmls, t_nmls = al("nmls", [128, 128], BF16)
        P.dma("sp", lambda e: e.dma_start(out=idl[:], in_=idl_d), writes=[t_idl])
        P.dma("pool", lambda e: e.dma_start(out=nmlp[:], in_=nmlp_d[:, :]), writes=[t_nmlp])
        P.dma("pool", lambda e: e.dma_start(out=nmls[:], in_=nmls_d[:, :]), writes=[t_nmls])
        for (h, th, d_) in [(cw, t_cw, cwp_d), (cb, t_cb, cbp_d), (dtb, t_dtb, dtb_d), (Abc, t_Abc, alog_d), (dsk, t_dsk, dsk_d),
                            (gss, t_gss, gss_d), (cst, t_cst, csts_d)]:
            P.dma("sp", lambda e, h=h, d_=d_: e.dma_start(out=h[:], in_=d_), writes=[th])
        P.dma("pool", lambda e: e.dma_start(out=nmp[:], in_=nmp_d[:, :]), writes=[t_nmp])
        P.dma("pool", lambda e: e.dma_start(out=nms[:], in_=nms_d[:, :]), writes=[t_nms])
        P.op("pool", lambda e: e.memset(ones_f[:], 1.0), writes=[t_ones])
        P.op("pool", lambda e: e.memset(one_t[:], 1.0), writes=[t_one])
        P.op("act", lambda e: e.activation(out=Abc[:], in_=Abc[:], func=AF.Exp), reads=[t_Abc], writes=[t_Abc])
        P.op("dve", lambda e: e.tensor_scalar(out=Abc[:], in0=Abc[:], scalar1=-1.0, scalar2=None, op0=ALU.mult), reads=[t_Abc], writes=[t_Abc])
        S, t_S = al("S", [128, D])
        Sb, t_Sb = al("Sb", [128, D], BF16)
        P.op("pool", lambda e: e.memset(S[:], 0.0), writes=[t_S])
        P.op("pool", lambda e: e.memset(Sb[:], 0.0), writes=[t_Sb])
        Ss = [al(f"Ss{j}", [128, D]) for j in range(2)]
        Sbs = [al(f"Sbs{j}", [128, D], BF16) for j in range(2)]
        sio, t_sio = al("sio", [128, 8, 128])
        ssm_v = lambda ap: ap.rearrange("(b h2) p n -> (h2 p) b n", h2=2)

        def state_in(j):
            P.dma("sp", lambda e: e.dma_start(out=sio[:], in_=ssm_v(sssm_d[j])), writes=[t_sio])

            def tr(e):
                for b in range(8):
                    i = e.transpose(out=bank(4 + b // 4)[:, (b % 4) * 128:(b % 4 + 1) * 128], in_=sio[:, b, :], identity=ident_f[:])
                return i
            P.op("pe", tr, reads=[t_sio, t_identf], writes=[bankT[4], bankT[5]])
            P.op("act", lambda e: e.activation(out=Ss[j][0][:], in_=psB[:, 0:1024], func=AF.Copy), reads=[bankT[4], bankT[5]], writes=[Ss[j][1]])
            P.op("dve", lambda e: e.tensor_copy(out=Sbs[j][0][:], in_=psB[:, 0:1024]), reads=[bankT[4], bankT[5]], writes=[Sbs[j][1]])

        def state_out(Sx, t_Sx, idx):
            def tr(e):
                for b in range(8):
                    i = e.transpose(out=bank(4 + b // 4)[:, (b % 4) * 128:(b % 4 + 1) * 128], in_=Sx[:, b * 128:(b + 1) * 128], identity=ident_f[:])
                return i
            P.op("pe", tr, reads=[t_Sx, t_identf], writes=[bankT[4], bankT[5]])
            P.op("act", lambda e: e.activation(out=sio[:], in_=psB[:, 0:1024].rearrange("p (b n) -> p b n", b=8), func=AF.Copy),
                 reads=[bankT[4], bankT[5]], writes=[t_sio])
            P.dma("sp", lambda e: e.dma_start(out=ssm_v(ssm_o[idx]), in_=sio[:]), reads=[t_sio])

        state_in(0)
        state_in(1)
        xbcb = []
        for i in range(2):
            h, r = A.alloc(f"xbc{i}", [128, 12, 131], F32)
            xbcb.append((h, A.newT(r, f"xh{i}"), [A.newT(r, f"xq{i}_{q}") for q in range(3)]))
        acc, t_acc = al("acc", [128, 12, 128])
        t_accc = [T(f"acc{ch}") for ch in range(12)]
        bcb2 = [al(f"bcb{i}", [128, 4, 128], BF16) for i in range(2)]
        Btok2 = [al(f"Btok{i}", [128, 2, 128], BF16) for i in range(2)]
        sm = {nm: al("sm_" + nm, [128, 16]) for nm in ["dtv", "ab", "ll", "dt", "dA", "acs", "dd", "decst", "dtd", "lndt", "acsl"]}
        indec2 = [al(f"indec{i}", [128, 16]) for i in range(2)]
        cd2 = [al(f"cd{i}", [128, 2, 16]) for i in range(2)]
        x_dt, t_xdt = al("x_dt", [128, D], BF16)
        xd2 = [al(f"xd{i}", [128, D], BF16) for i in range(2)]
        xs_sb, t_xs = al("xs_sb", [128, D])
        xsD, t_xsD = al("xsD", [128, D], BF16)
        dg, t_dg = al("dg", [128, 16, 128])
        t_dgq = [T(f"dgq{q}", after=[t_dg]) for q in range(4)]
        Mm, t_M = al("Mm", [128, 16, 128], BF16)
        ysb, t_y = al("ysb", [128, D])
        sz2 = [al(f"sz{i}", [128, D]) for i in range(2)]
        ssq, t_ssq = al("ssq", [128, 4])
        yn, t_yn = al("yn", [128, D], BF16)
        ynT, t_ynT = al("ynT", [128, 8, 128], BF16)
        r1s, t_r1s = ysb, t_y
        junk = yn[:].bitcast(F32)
        t_junk = t_yn
        conv_v = lambda ap: ap.rearrange("k (c p) -> p c k", p=128)
        g_ = lambda nm: sm[nm][0]
        t_ = lambda nm: sm[nm][1]
        pv2 = bank(2).bitcast(BF16)
        pv6 = bank(6).bitcast(BF16)

        def prepA(s):
            t0, m = SUBT[s]
            p = s % 2
            samp = (s == 16)
            co = 4 if samp else 0
            CH = 32 if samp else 64
            vi = 1 if samp else 0
            xbc, t_xh, t_xq = xbcb[p]
            pxbc, t_pxh, t_pxq = xbcb[1 - p]
            bcb, t_bcb = bcb2[p]
            Btok, t_Btok = Btok2[p]
            indec, t_indec = indec2[p]
            cd, t_cd = cd2[p]
            xd, t_xd = xd2[p]
            sz, t_sz = sz2[p]
            xr = [xT_T[s]]
            segs = [(0, 32, 0), (35, 32, 32)] if samp else [(0, 128, 0)]
            if samp:
                for j in range(2):
                    for k in range(3):
                        P.dma("sp", lambda e, j=j, k=k: ncdma(e, xbc[:, :, j * 35:j * 35 + 3], conv_v(sconv_d[j]), k), writes=[t_xh])
            elif s == 0:
                P.op("pool", lambda e: e.memset(xbc[:, :, 0:3], 0.0), writes=[t_xh])
            else:
                P.op("pool", lambda e: e.tensor_copy(out=xbc[:, :, 0:3], in_=pxbc[:, :, 128:131]), reads=[t_pxq[2], t_pxq[1], t_pxq[0]], writes=[t_xh])
            yield
            for q in range(3):
                bq = 2 + (q % 2)

                def mmx(e, q=q, bq=bq):
                    for j in range(4):
                        ch = 4 * q + j
                        for k in range(8):
                            i = e.matmul(bank(bq)[:, j * 128:j * 128 + m], lhsT=wx[:, k, ch * 128:(ch + 1) * 128], rhs=xT[:, k, t0:t0 + m],
                                         start=(k == 0), stop=(k == 7), skip_group_check=True)
                    return i
                P.op("pe", mmx, reads=xr + t_wx, writes=[bankT[bq]])
                yield
                bv = bank(bq).rearrange("p (j t) -> p j t", j=4)
                if samp:
                    for j in range(2):
                        P.op("act", lambda e, q=q, j=j, bv=bv: e.activation(
                            out=xbc[:, 4 * q:4 * q + 4, j * 35 + 3:j * 35 + 35], in_=bv[:, :, j * 32:(j + 1) * 32], func=AF.Copy),
                            reads=[bankT[bq]], writes=[t_xq[q]])
                else:
                    P.op("act", lambda e, q=q, bv=bv: e.activation(out=xbc[:, 4 * q:4 * q + 4, 3:131], in_=bv[:, :, :], func=AF.Copy),
                         reads=[bankT[bq]], writes=[t_xq[q]])
                yield
            if s == 15:
                for k in range(3):
                    P.dma("sp", lambda e, k=k: ncdma(e, conv_v(conv_o[0]), xbc[:, :, 128:131], k), reads=t_xq)
            if samp:
                for j in range(2):
                    for k in range(3):
                        P.dma("sp", lambda e, j=j, k=k: ncdma(e, conv_v(conv_o[1 + j]), xbc[:, :, j * 35 + 32:j * 35 + 35], k), reads=t_xq)
            def mmdt(e):
                for k in range(8):
                    i = e.matmul(bank(3)[0:m, 0:16], lhsT=xT[:, k, t0:t0 + m], rhs=wdt[:, k, :], start=(k == 0), stop=(k == 7))
                return i
            P.op("pe", mmdt, reads=xr + [t_wdt], writes=[bankT[3]])
            yield
            P.op("dve", lambda e: e.tensor_tensor(out=g_("dtv")[0:m, :], in0=bank(3)[0:m, 0:16], in1=dtb[0:m, :], op=ALU.add),
                 reads=[bankT[3], t_dtb], writes=[t_("dtv")])
            yield
            P.op("act", lambda e: e.activation(out=g_("ab")[0:m, :], in_=g_("dtv")[0:m, :], func=AF.Abs), reads=[t_("dtv")], writes=[t_("ab")])
            P.op("act", lambda e: e.activation(out=g_("ab")[0:m, :], in_=g_("ab")[0:m, :], func=AF.Exp, scale=-1.0), reads=[t_("ab")], writes=[t_("ab")])
            P.op("act", lambda e: e.activation(out=g_("ll")[0:m, :], in_=g_("ab")[0:m, :], func=AF.Ln, bias=one_t[0:m, :]),
                 reads=[t_("ab"), t_one], writes=[t_("ll")])
            yield
            def conv_op(k, ch):
                if True:
                    for (c0, n, o0) in segs:
                        if k == 0:
                            P.op("dve", lambda e, ch=ch, c0=c0, n=n, o0=o0: e.tensor_scalar(
                                out=acc[:, ch, o0:o0 + n], in0=xbc[:, ch, c0:c0 + n], scalar1=cw[:, ch, 0:1], scalar2=cb[:, ch:ch + 1],
                                op0=ALU.mult, op1=ALU.add), reads=[t_xh, t_xq[ch // 4], t_cw, t_cb], writes=[t_accc[ch]])
                        else:
                            P.op("dve", lambda e, ch=ch, k=k, c0=c0, n=n, o0=o0: e.scalar_tensor_tensor(
                                out=acc[:, ch, o0:o0 + n], in0=xbc[:, ch, c0 + k:c0 + k + n], scalar=cw[:, ch, k:k + 1], in1=acc[:, ch, o0:o0 + n],
                                op0=ALU.mult, op1=ALU.add), reads=[t_xh, t_xq[ch // 4], t_cw, t_accc[ch]], writes=[t_accc[ch]])
            for k in range(2):
                for ch in range(12):
                    conv_op(k, ch)
                    if ch % 4 == 3:
                        yield
            if True:
                if True:
                    P.op("dve", lambda e: e.scalar_tensor_tensor(out=g_("dt")[0:m, :], in0=g_("dtv")[0:m, :], scalar=0.0, in1=g_("ll")[0:m, :],
                                                                 op0=ALU.max, op1=ALU.add), reads=[t_("dtv"), t_("ll")], writes=[t_("dt")])
                    P.op("dve", lambda e: e.tensor_tensor(out=g_("dA")[0:m, :], in0=g_("dt")[0:m, :], in1=Abc[0:m, :], op=ALU.mult),
                         reads=[t_("dt"), t_Abc], writes=[t_("dA")])
                    yield
                if True:
                    def mmacs(e):
                        e.matmul(bank(3)[0:m, 0:16], lhsT=cst[0:m, co + 0, 0:m], rhs=g_("dA")[0:m, :], start=True, stop=True, skip_group_check=True)
                        e.matmul(bank(3)[0:m, 16:32], lhsT=cst[0:m, co + 1, 0:m], rhs=g_("dA")[0:m, :], start=True, stop=True, skip_group_check=True)
                        e.matmul(bank(3)[:, 32:48], lhsT=cst[0:m, co + 2, :], rhs=g_("dA")[0:m, :], start=True, stop=True, skip_group_check=True)
                        return e.matmul(bank(3)[:, 48:64], lhsT=cst[0:m, co + 3, :], rhs=g_("dA")[0:m, :], start=True, stop=True, skip_group_check=True)
                    P.op("pe", mmacs, reads=[t_("dA"), t_cst], writes=[bankT[3]])
                    yield
                    P.op("act", lambda e: e.activation(out=g_("acs")[0:m, :], in_=bank(3)[0:m, 0:16], func=AF.Copy), reads=[bankT[3]], writes=[t_("acs")])
                    P.op("act", lambda e: e.activation(out=cd[:, :, :], in_=bank(3)[:, 32:64].rearrange("p (c h) -> p c h", c=2), func=AF.Exp),
                         reads=[bankT[3]], writes=[t_cd])
                    yield
                    P.op("dve", lambda e: e.tensor_tensor(out=g_("dd")[0:m, :], in0=bank(3)[0:m, 16:32], in1=g_("acs")[0:m, :], op=ALU.subtract),
                         reads=[bankT[3], t_("acs")], writes=[t_("dd")])
                    yield
                    P.op("act", lambda e: e.activation(out=g_("lndt")[0:m, :], in_=g_("dt")[0:m, :], func=AF.Ln), reads=[t_("dt")], writes=[t_("lndt")])
                    P.op("act", lambda e: e.activation(out=indec[0:m, :], in_=g_("acs")[0:m, :], func=AF.Exp), reads=[t_("acs")], writes=[t_indec])
                    P.op("act", lambda e: e.activation(out=g_("decst")[0:m, :], in_=g_("dd")[0:m, :], func=AF.Exp), reads=[t_("dd")], writes=[t_("decst")])
                    yield
                    P.op("dve", lambda e: e.tensor_tensor(out=g_("dtd")[0:m, :], in0=g_("dt")[0:m, :], in1=g_("decst")[0:m, :], op=ALU.mult),
                         reads=[t_("dt"), t_("decst")], writes=[t_("dtd")])
                    P.op("dve", lambda e: e.tensor_tensor(out=g_("acsl")[0:m, :], in0=g_("acs")[0:m, :], in1=g_("lndt")[0:m, :], op=ALU.subtract),
                         reads=[t_("acs"), t_("lndt")], writes=[t_("acsl")])
                    P.op("dve", lambda e: e.tensor_tensor(
                        out=dg[0:m, :, 0:CH], in0=idl[0:m, vi, 0:CH].unsqueeze(1).to_broadcast([m, 16, CH]),
                        in1=g_("acs")[0:m, :].unsqueeze(2).to_broadcast([m, 16, CH]), op=ALU.mult), reads=[t_idl, t_("acs")], writes=t_dgq)
                    yield
            nm_ = nmls if samp else nmlp
            t_nm = t_nmls if samp else t_nmlp
            rest_conv = [(k, ch) for k in (2, 3) for ch in range(12)]
            for q in range(4):
                bq = q % 2

                def mmR(e, q=q, bq=bq):
                    e.matmul(bank(bq)[0:m, 0:4 * CH], lhsT=cst[0:m, co + 1, 0:m], rhs=dg[0:m, 4 * q:4 * q + 4, 0:CH], start=True, stop=False)
                    return e.matmul(bank(bq)[0:m, 0:4 * CH], lhsT=ident_b[0:m, 0:m], rhs=nm_[0:m, 0:4 * CH], start=False, stop=True)
                P.op("pe", mmR, reads=[t_dgq[q], t_cst, t_identb, t_nm], writes=[bankT[bq]])
                yield
                for (k, ch) in rest_conv[6 * q:6 * q + 6]:
                    conv_op(k, ch)
                yield
                Rv = bank(bq)[0:m, 0:4 * CH].rearrange("p (h i) -> p h i", h=4)
                P.op("dve", lambda e, q=q, Rv=Rv: e.tensor_tensor(
                    out=dg[0:m, 4 * q:4 * q + 4, 0:CH], in0=Rv, in1=g_("acsl")[0:m, 4 * q:4 * q + 4].unsqueeze(2).to_broadcast([m, 4, CH]), op=ALU.subtract),
                    reads=[bankT[bq], t_("acsl")], writes=[t_dgq[q]])
                yield
            P.op("act", lambda e: e.activation(out=dg[0:m, :, 0:CH], in_=dg[0:m, :, 0:CH], func=AF.Exp), reads=t_dgq, writes=t_dgq)
            yield
            for half in range(2):
                def mmz(e, half=half):
                    for k in range(8):
                        i = e.matmul(bank(half)[0:m, :], lhsT=xT[:, k, t0:t0 + m], rhs=wz[:, k, half * 512:(half + 1) * 512], start=(k == 0), stop=(k == 7))
                    return i
                P.op("pe", mmz, reads=xr + [t_wz[half]], writes=[bankT[half]])
            yield
            P.op("act", lambda e: e.activation(out=sz[0:m, :], in_=psA[0:m, 0:1024], func=AF.Silu), reads=[bankT[0], bankT[1]], writes=[t_sz])
            P.op("act", lambda e: e.activation(out=bcb[:, :, 0:m], in_=acc[:, 8:12, 0:m], func=AF.Silu), reads=t_accc[8:12], writes=[t_bcb])
            P.op("act", lambda e: e.activation(out=acc[:, 0:8, 0:m], in_=acc[:, 0:8, 0:m], func=AF.Silu), reads=t_accc[0:8], writes=[t_acc] + t_accc[0:8])
            yield
            def trb(e):
                for g in range(2):
                    i = e.transpose(out=pv2[0:m, g * 128:(g + 1) * 128], in_=bcb[:, g, 0:m], identity=ident_b[:])
                return i
            P.op("pe", trb, reads=[t_bcb, t_identb], writes=[bankT[2]])

            def mmcb(e):
                for c in range(2):
                    for g in range(2):
                        i = e.matmul(bank(3)[c * CH:(c + 1) * CH, 128 + g * 64:128 + g * 64 + CH], lhsT=bcb[:, g, c * CH:(c + 1) * CH],
                                     rhs=bcb[:, 2 + g, c * CH:(c + 1) * CH], start=True, stop=True, skip_group_check=True)
                return i
            P.op("pe", mmcb, reads=[t_bcb], writes=[bankT[3]])

            def trx(e):
                for cc in range(8):
                    i = e.transpose(out=bank(cc // 4)[0:m, (cc % 4) * 128:(cc % 4 + 1) * 128], in_=acc[:, cc, 0:m], identity=ident_f[:])
                return i
            P.op("pe", trx, reads=[t_acc, t_identf], writes=[bankT[0], bankT[1]])
            yield
            P.op("act", lambda e: e.activation(out=Btok[0:m, :, :], in_=pv2[0:m, 0:256].rearrange("p (g n) -> p g n", g=2), func=AF.Copy),
                 reads=[bankT[2]], writes=[t_Btok])
            P.op("act", lambda e: e.activation(out=x_dt[0:m, :], in_=psA[0:m, 0:1024], func=AF.Copy), reads=[bankT[0], bankT[1]], writes=[t_xdt])
            yield
            for g in range(2):
                bvx = bank(g)[0:m, :].rearrange("p (h d) -> p h d", h=8)
                P.op("dve", lambda e, g=g, bvx=bvx: e.tensor_tensor(
                    out=xd[0:m, g * 512:(g + 1) * 512].rearrange("p (h d) -> p h d", h=8), in0=bvx,
                    in1=g_("dtd")[0:m, g * 8:(g + 1) * 8].unsqueeze(2).to_broadcast([m, 8, 64]), op=ALU.mult),
                    reads=[bankT[g], t_("dtd")], writes=[t_xd])
            yield
            P.op("act", lambda e: e.activation(out=xs_sb[0:m, :], in_=psA[0:m, 0:1024], func=AF.Copy), reads=[bankT[0], bankT[1]], writes=[t_xs])
            yield
            P.op("pool", lambda e: e.tensor_tensor(
                out=xsD[0:m, :].rearrange("p (h d) -> p h d", h=16), in0=xs_sb[0:m, :].rearrange("p (h d) -> p h d", h=16),
                in1=dsk[0:m, :].unsqueeze(2).to_broadcast([m, 16, 64]), op=ALU.mult), reads=[t_xs, t_dsk], writes=[t_xsD])
            cbv = bank(3)[0:m, 128:256].rearrange("p (g i) -> p g i", g=2)[:, :, 0:CH]
            P.op("dve", lambda e: e.tensor_tensor(
                out=Mm[0:m, :, 0:CH].rearrange("p (g h) i -> p g h i", g=2), in0=dg[0:m, :, 0:CH].rearrange("p (g h) i -> p g h i", g=2),
                in1=cbv.unsqueeze(2).to_broadcast([m, 2, 8, CH]), op=ALU.mult), reads=t_dgq + [bankT[3]], writes=[t_M])
            yield

        def runBC(s):
            t0, m = SUBT[s]
            p = s % 2
            samp = (s == 16)
            CH = 32 if samp else 64
            bcb, t_bcb = bcb2[p]
            Btok, t_Btok = Btok2[p]
            indec, t_indec = indec2[p]
            cd, t_cd = cd2[p]
            xd, t_xd = xd2[p]
            sz, t_sz = sz2[p]
            xres, t_xres = sz, t_sz
            for g in range(2):
                def mmy(e, g=g):
                    e.matmul(bank(4 + g)[0:m, :], lhsT=ident_b[0:m, 0:m], rhs=xsD[0:m, g * 512:(g + 1) * 512], start=True, stop=False, skip_group_check=True)
                    for hh in range(8):
                        h = g * 8 + hh
                        for c in range(2):
                            i = e.matmul(bank(4 + g)[c * CH:(c + 1) * CH, hh * 64:(hh + 1) * 64], lhsT=Mm[c * CH:(c + 1) * CH, h, 0:CH],
                                         rhs=x_dt[c * CH:(c + 1) * CH, h * 64:(h + 1) * 64], start=False, stop=(hh == 7 and c == 1), skip_group_check=True)
                    return i
                P.op("pe", mmy, reads=[t_identb, t_xsD, t_M, t_xdt], writes=[bankT[4 + g]])
            yield
            P.op("act", lambda e: e.activation(out=ysb[0:m, :], in_=psB[0:m, 0:1024], func=AF.Copy), reads=[bankT[4], bankT[5]], writes=[t_y])
            yield
            pending_out = []
            for c in range(2):
                r0, r1 = c * CH, (c + 1) * CH
                if samp:
                    (Sx, t_Sx), (Sbx, t_Sbx) = Ss[c], Sbs[c]
                else:
                    (Sx, t_Sx), (Sbx, t_Sbx) = (S, t_S), (Sb, t_Sb)
                for g in range(2):
                    P.op("pe", lambda e, g=g, r0=r0, r1=r1, Sbx=Sbx: e.matmul(
                        bank(6 + g)[r0:r1, :], lhsT=bcb[:, 2 + g, r0:r1], rhs=Sbx[:, g * 512:(g + 1) * 512], start=True, stop=True, skip_group_check=True),
                        reads=[t_bcb, t_Sbx], writes=[bankT[6 + g]])
                    P.op("pe", lambda e, g=g, r0=r0, r1=r1: e.matmul(
                        bank(4 + g)[:, :], lhsT=Btok[r0:r1, g, :], rhs=xd[r0:r1, g * 512:(g + 1) * 512], start=True, stop=True),
                        reads=[t_Btok, t_xd], writes=[bankT[4 + g]])
                yield
                P.op("pool", lambda e, c=c, Sx=Sx: e.tensor_tensor(
                    out=Sx[:, :].rearrange("p (h d) -> p h d", h=16), in0=Sx[:, :].rearrange("p (h d) -> p h d", h=16),
                    in1=cd[:, c, :].unsqueeze(2).to_broadcast([128, 16, 64]), op=ALU.mult), reads=[t_Sx, t_cd], writes=[t_Sx])
                P.op("dve", lambda e, Sx=Sx: e.tensor_tensor(out=Sx[:, :], in0=Sx[:, :], in1=psB[:, 0:1024], op=ALU.add),
                     reads=[t_Sx, bankT[4], bankT[5]], writes=[t_Sx])
                yield
                P.op("act", lambda e, Sx=Sx, Sbx=Sbx: e.activation(out=Sbx[:, :], in_=Sx[:, :], func=AF.Copy), reads=[t_Sx], writes=[t_Sbx])
                yield
                if samp:
                    pending_out.append((Sx, t_Sx, 1 + c))
                elif s == 15 and c == 1:
                    pending_out.append((Sx, t_Sx, 0))
            for g in range(2):
                P.op("dve", lambda e, g=g: e.tensor_tensor(
                    out=junk[0:m, 0:512].rearrange("p (h d) -> p h d", h=8), in0=bank(6 + g)[0:m, :].rearrange("p (h d) -> p h d", h=8),
                    in1=indec[0:m, g * 8:(g + 1) * 8].unsqueeze(2).to_broadcast([m, 8, 64]), op=ALU.mult),
                    reads=[bankT[6 + g], t_indec], writes=[t_junk])
                P.op("pool", lambda e, g=g: e.tensor_tensor(out=ysb[0:m, g * 512:(g + 1) * 512], in0=ysb[0:m, g * 512:(g + 1) * 512], in1=junk[0:m, 0:512], op=ALU.add),
                     reads=[t_junk, t_y], writes=[t_y])
                yield
            for po in pending_out:
                state_out(*po)
                yield
            if debug:
                P.dma("sp", lambda e: e.dma_start(out=dbg_y[t0:t0 + m, :], in_=ysb[0:m, :]), reads=[t_y])
            P.op("pool", lambda e: e.tensor_tensor(out=ysb[0:m, :], in0=ysb[0:m, :], in1=sz[0:m, :], op=ALU.mult), reads=[t_y, t_sz], writes=[t_y])
            yield
            for g in range(2):
                P.op("act", lambda e, g=g: e.activation(out=junk[0:m, 0:512], in_=ysb[0:m, g * 512:(g + 1) * 512], func=AF.Square,
                                                        accum_out=ssq[0:m, g:g + 1]), reads=[t_y], writes=[t_junk, t_ssq])
            P.op("act", lambda e: e.activation(out=ssq[0:m, 2:4], in_=ssq[0:m, 0:2], func=AF.Ln, scale=1.0 / 512.0, bias=eps_t[0:m, :]),
                 reads=[t_ssq, t_eps], writes=[t_ssq])
            P.op("act", lambda e: e.activation(out=ssq[0:m, 0:2], in_=ssq[0:m, 2:4], func=AF.Exp, scale=-0.5), reads=[t_ssq], writes=[t_ssq])
            yield
            for g in range(2):
                P.op("dve", lambda e, g=g: e.scalar_tensor_tensor(
                    out=yn[0:m, g * 512:(g + 1) * 512], in0=ysb[0:m, g * 512:(g + 1) * 512], scalar=ssq[0:m, g:g + 1],
                    in1=gss[0:m, g * 512:(g + 1) * 512], op0=ALU.mult, op1=ALU.mult), reads=[t_y, t_ssq, t_gss], writes=[t_yn])
            yield
            if debug:
                P.dma("sp", lambda e: e.dma_start(out=dbg_yn[t0:t0 + m, :], in_=yn[0:m, :]), reads=[t_yn])

            def tryn(e):
                for k in range(8):
                    i = e.transpose(out=pv6[:, k * 128:k * 128 + m], in_=yn[0:m, k * 128:(k + 1) * 128], identity=ident_b[0:m, 0:m])
                return i
            P.op("pe", tryn, reads=[t_yn, t_identb], writes=[bankT[6]])
            P.dma("sp", lambda e: e.dma_start(out=xres[0:m, :], in_=x1d[t0:t0 + m, :]), reads=[x1d_T[s], t_sz], writes=[t_xres])
            yield
            P.op("act", lambda e: e.activation(out=ynT[:, :, 0:m], in_=pv6.rearrange("p (k t) -> p k t", k=8)[:, :, 0:m], func=AF.Copy),
                 reads=[bankT[6]], writes=[t_ynT])
            yield
            for half in range(2):
                def mmo(e, half=half):
                    for k in range(8):
                        i = e.matmul(bank(4 + half)[0:m, :], lhsT=ynT[:, k, 0:m], rhs=wot[:, k, half * 512:(half + 1) * 512], start=(k == 0), stop=(k == 7))
                    return i
                P.op("pe", mmo, reads=[t_ynT, t_wot[half]], writes=[bankT[4 + half]])
            yield
            P.op("dve", lambda e: e.scalar_tensor_tensor(out=r1s[0:m, :], in0=xres[0:m, :], scalar=ALPHA, in1=psB[0:m, 0:1024],
                                                         op0=ALU.mult, op1=ALU.add), reads=[t_xres, bankT[4], bankT[5]], writes=[t_r1s])
            yield
            P.dma("sp", lambda e: e.dma_start(out=r1d[t0:t0 + m, :], in_=r1s[0:m, :]), reads=[t_r1s], writes=[r1d_T[s]])
            yield

        def interleave_w(pairs):
            pairs = [[g, w] for g, w in pairs if g is not None]
            while pairs:
                for pr in list(pairs):
                    for _ in range(pr[1]):
                        try:
                            next(pr[0])
                        except StopIteration:
                            pairs.remove(pr)
                            break

        def interleave2(gens):
            gens = [g for g in gens if g is not None]
            while gens:
                for g in list(gens):
                    try:
                        next(g)
                    except StopIteration:
                        gens.remove(g)

        interleave2([prepA(0)])
        for s in range(17):
            gA = prepA(s + 1) if s + 1 < 17 else None
            if gA is not None:
                for _ in range(SSD_K):
                    next(gA)
            interleave_w([(gA, SSD_W[1]), (runBC(s), SSD_W[0])])
        A.reset(mk)

    SCALE = 192.0 ** -0.5

    def mla_phase():
        mk = A.mark()
        w_in_v = w["w_in"].rearrange("(ko ki) f -> ki ko f", ki=128)
        oT, r = A.alloc("oT", [128, 8, NT], BF16)
        oT_T = [[A.newT(r, f"oT{h}_{s}") for s in range(17)] for h in range(8)]
        cqnT, r = A.alloc("cqnT", [128, 4, NT], BF16)
        cqnT_T = [A.newT(r, f"cqnT{s}") for s in range(17)]
        latTp, r = A.alloc("latTp", [128, 4, NPROMPT], BF16)
        latTp_T = [A.newT(r, f"latTp{s}") for s in range(16)]
        latTs, r = A.alloc("latTs", [128, 4, 2, 1056], BF16)
        latTs_T = [[A.newT(r, f"latTs{j}_{a}") for a in range(9)] for j in range(2)]
        krTp, r = A.alloc("krTp", [128, NPROMPT], BF16)
        krTp_T = [A.newT(r, f"krTp{s}") for s in range(16)]
        krTs, r = A.alloc("krTs", [128, 2, 1056], BF16)
        krTs_T = [[A.newT(r, f"krTs{j}_{a}") for a in range(9)] for j in range(2)]
        mk_a = A.mark()
        wq, t_wq = al("wq_in", [128, 8, 512], BF16)
        wkv, t_wkv = al("wkv_in", [128, 8, 512], BF16)
        wkr, t_wkr = al("wkr_in", [128, 8, 64], BF16)
        P.dma("pool", lambda e: e.dma_start(out=wq[:], in_=w_in_v[:, :, 2576:3088]), writes=[t_wq])
        P.dma("pool", lambda e: e.dma_start(out=wkv[:], in_=w_in_v[:, :, 3088:3600]), writes=[t_wkv])
        P.dma("pool", lambda e: e.dma_start(out=wkr[:], in_=w_in_v[:, :, 3600:3664]), writes=[t_wkr])
        gq, t_gq = al("gq", [128, 512])
        gkv, t_gkv = al("gkv", [128, 512])
        P.dma("sp", lambda e: e.dma_start(out=gq[:], in_=gq_d[:, :]), writes=[t_gq])
        P.dma("sp", lambda e: e.dma_start(out=gkv[:], in_=gkv_d[:, :]), writes=[t_gkv])
        ctok, t_ctok = al("ctok", [128, 8, 512], BF16)
        ckr, t_ckr = al("ckr", [128, 8, 64], BF16)
        for j in range(2):
            P.dma("pool", lambda e, j=j: e.dma_start(out=ctok[:], in_=clat_d[j].rearrange("(a p) f -> p a f", p=128)), writes=[t_ctok])
            P.dma("pool", lambda e, j=j: e.dma_start(out=ckr[:], in_=ckr_d[j].rearrange("(a p) f -> p a f", p=128)), writes=[t_ckr])
            for a in range(8):
                bk = 4 + (a % 2)
                pv = bank(bk).bitcast(BF16)

                def trc(e, a=a, pv=pv):
                    for c in range(4):
                        i = e.transpose(out=pv[:, c * 128:(c + 1) * 128], in_=ctok[:, a, c * 128:(c + 1) * 128], identity=ident_b[:])
                    return e.transpose(out=pv[0:64, 512:640], in_=ckr[:, a, :], identity=ident_b[:])
                P.op("pe", trc, reads=[t_ctok, t_ckr, t_identb], writes=[bankT[bk]])
                P.op("act", lambda e, a=a, j=j, pv=pv: e.activation(out=latTs[:, :, j, a * 128:(a + 1) * 128],
                                                                  in_=pv[:, 0:512].rearrange("p (c t) -> p c t", c=4), func=AF.Copy),
                     reads=[bankT[bk]], writes=[latTs_T[j][a]])
                P.op("dve", lambda e, a=a, j=j, pv=pv: e.tensor_copy(out=krTs[0:64, j, a * 128:(a + 1) * 128], in_=pv[0:64, 512:640]),
                     reads=[bankT[bk]], writes=[krTs_T[j][a]])
        sq, t_sq = al("sq", [128, 4])
        cqn2 = [al(f"cqn{i}", [128, 512], BF16) for i in range(2)]
        latf = [al(f"latf{i}", [128, 512]) for i in range(2)]
        latb2 = [al(f"latb{i}", [128, 512], BF16) for i in range(2)]
        cs4 = [al(f"cs4_{i}", [128, 128]) for i in range(2)]
        rt, t_rt = al("rt", [128, 128])
        krf = [al(f"krf{i}", [128, 64]) for i in range(2)]
        krb2 = [al(f"krb{i}", [128, 64], BF16) for i in range(2)]
        jk, t_jk = al("jk", [128, 512])

        sq2 = [al(f"sq2_{i}", [128, 4]) for i in range(2)]

        def b2a_bk(s, i):
            return (0, 1, 2)[i] if s % 2 == 0 else (5, 6, 7)[i]

        def b2a_part1a(s):
            t0, m = SUBT[s]
            xr = [xT_T[s]]
            csh, t_cs = cs4[s % 2]
            sq, t_sq = sq2[s % 2]
            P.dma("sp", lambda e: e.dma_start(out=csh[0:m, :], in_=cs4_d[t0:t0 + m, :]), writes=[t_cs])
            for (ib, wt, t_wt, nn) in [(0, wq, t_wq, 512), (1, wkv, t_wkv, 512), (2, wkr, t_wkr, 64)]:
                bk = b2a_bk(s, ib)

                def mmp(e, bk=bk, wt=wt, nn=nn):
                    for k in range(8):
                        i = e.matmul(bank(bk)[0:m, 0:nn], lhsT=xT[:, k, t0:t0 + m], rhs=wt[:, k, :], start=(k == 0), stop=(k == 7))
                    return i
                P.op("pe", mmp, reads=xr + [t_wt], writes=[bankT[bk]])
                yield
            for i_ in range(2):
                bk = b2a_bk(s, i_)
                P.op("act", lambda e, i_=i_, bk=bk: e.activation(out=jk[0:m, :], in_=bank(bk)[0:m, :], func=AF.Square, accum_out=sq[0:m, i_:i_ + 1]),
                     reads=[bankT[bk]], writes=[t_jk, t_sq])
                yield
            P.op("act", lambda e: e.activation(out=sq[0:m, 2:4], in_=sq[0:m, 0:2], func=AF.Sqrt, scale=1.0 / 512.0, bias=eps_t[0:m, :]),
                 reads=[t_sq, t_eps], writes=[t_sq])
            yield

        def b2a_part1b(s):
            t0, m = SUBT[s]
            cqn, t_cqn = cqn2[s % 2]
            latfh, t_latf = latf[s % 2]
            csh, t_cs = cs4[s % 2]
            krfh, t_krf = krf[s % 2]
            sq, t_sq = sq2[s % 2]
            b0, b1, b2_ = b2a_bk(s, 0), b2a_bk(s, 1), b2a_bk(s, 2)
            P.op("dve", lambda e: e.tensor_tensor(out=rt[0:m, 0:64], in0=bank(b2_)[0:m, 0:64], in1=csh[0:m, 0:64], op=ALU.mult),
                 reads=[bankT[b2_], t_cs], writes=[t_rt])
            P.op("dve", lambda e: e.tensor_tensor(out=rt[0:m, 64:128], in0=bank(b2_)[0:m, 0:64], in1=csh[0:m, 64:128], op=ALU.mult),
                 reads=[bankT[b2_], t_cs], writes=[t_rt])
            yield
            P.op("dve", lambda e: e.tensor_tensor(out=krfh[0:m, 0:32], in0=rt[0:m, 0:32], in1=rt[0:m, 32:64], op=ALU.subtract), reads=[t_rt], writes=[t_krf])
            P.op("dve", lambda e: e.tensor_tensor(out=krfh[0:m, 32:64], in0=rt[0:m, 64:96], in1=rt[0:m, 96:128], op=ALU.add), reads=[t_rt], writes=[t_krf])
            yield
            P.op("dve", lambda e: e.reciprocal(out=sq[0:m, 0:2], in_=sq[0:m, 2:4]), reads=[t_sq], writes=[t_sq])
            P.op("dve", lambda e: e.scalar_tensor_tensor(out=cqn[0:m, :], in0=bank(b0)[0:m, :], scalar=sq[0:m, 0:1], in1=gq[0:m, :],
                                                         op0=ALU.mult, op1=ALU.mult), reads=[bankT[b0], t_sq, t_gq], writes=[t_cqn])
            yield
            P.op("dve", lambda e: e.scalar_tensor_tensor(out=latfh[0:m, :], in0=bank(b1)[0:m, :], scalar=sq[0:m, 1:2], in1=gkv[0:m, :],
                                                         op0=ALU.mult, op1=ALU.mult), reads=[bankT[b1], t_sq, t_gkv], writes=[t_latf])
            yield

        def b2a_part2(s):
            t0, m = SUBT[s]
            cqn, t_cqn = cqn2[s % 2]
            latfh, t_latf = latf[s % 2]
            latb, t_latb = latb2[s % 2]
            krfh, t_krf = krf[s % 2]
            krb, t_krb = krb2[s % 2]
            P.dma("sp", lambda e: e.dma_start(out=lat_o[t0:t0 + m, :], in_=latfh[0:m, :]), reads=[t_latf])
            P.dma("sp", lambda e: e.dma_start(out=kr_o[t0:t0 + m, :], in_=krfh[0:m, :]), reads=[t_krf])
            P.op("act", lambda e: e.activation(out=latb[0:m, :], in_=latfh[0:m, :], func=AF.Copy), reads=[t_latf], writes=[t_latb])
            P.op("act", lambda e: e.activation(out=krb[0:m, :], in_=krfh[0:m, :], func=AF.Copy), reads=[t_krf], writes=[t_krb])
            yield
            bk = 4
            pv = bank(bk).bitcast(BF16)

            def trq(e):
                for c in range(4):
                    i = e.transpose(out=pv[:, c * 128:c * 128 + m], in_=cqn[0:m, c * 128:(c + 1) * 128], identity=ident_b[0:m, 0:m])
                for c in range(4):
                    i = e.transpose(out=pv[:, 512 + c * 128:512 + c * 128 + m], in_=latb[0:m, c * 128:(c + 1) * 128], identity=ident_b[0:m, 0:m])
                return i
            P.op("pe", trq, reads=[t_cqn, t_latb, t_identb], writes=[bankT[bk]])
            pv3 = bank(3).bitcast(BF16)
            P.op("pe", lambda e: e.transpose(out=pv3[0:64, 0:m], in_=krb[0:m, :], identity=ident_b[0:m, 0:m]),
                 reads=[t_krb, t_identb], writes=[bankT[3]])
            yield
            pvv = pv.rearrange("p (c t) -> p c t", c=8)
            P.op("act", lambda e: e.activation(out=cqnT[:, :, t0:t0 + m], in_=pvv[:, 0:4, 0:m], func=AF.Copy), reads=[bankT[bk]], writes=[cqnT_T[s]])
            yield
            if s < 16:
                P.op("dve", lambda e: e.tensor_copy(out=latTp[:, :, t0:t0 + m], in_=pvv[:, 4:8, 0:m]), reads=[bankT[bk]], writes=[latTp_T[s]])
                P.op("dve", lambda e: e.tensor_copy(out=krTp[0:64, t0:t0 + m], in_=pv3[0:64, 0:m]), reads=[bankT[3]], writes=[krTp_T[s]])
            else:
                for j in range(2):
                    P.op("dve", lambda e, j=j: e.tensor_copy(out=latTs[:, :, j, 1024:1056], in_=pvv[:, 4:8, j * 32:(j + 1) * 32]),
                         reads=[bankT[bk]], writes=[latTs_T[j][8]])
                    P.op("dve", lambda e, j=j: e.tensor_copy(out=krTs[0:64, j, 1024:1056], in_=pv3[0:64, j * 32:(j + 1) * 32]),
                         reads=[bankT[3]], writes=[krTs_T[j][8]])
            yield

        interleave_g([b2a_part1a(0)])
        interleave_g([b2a_part1a(1), b2a_part1b(0)])
        for s in range(17):
            interleave_g([b2a_part1a(s + 2) if s + 2 < 17 else None, b2a_part1b(s + 1) if s + 1 < 17 else None, b2a_part2(s)])
        A.reset(mk_a)
        if MLA_LEVEL <= 1:
            return mk, oT, oT_T, []
        w_uq_v = w["w_uq"].rearrange("(c p) (h t) -> p c h t", p=128, t=192)
        wuq, r = A.alloc("wuq", [128, 4, 8, 192], BF16)
        t_wuq = [A.newT(r, f"wuq{c}") for c in range(4)]
        wuqr, r = A.alloc("wuqr", [128, 4, 8, 64], BF16)
        t_wuqr = A.newT(r, "wuqr")
        wukv, r = A.alloc("wukv", [128, 4, 2048], BF16)
        t_wukv = [A.newT(r, f"wukv{c}") for c in range(4)]
        w_ukv_v = w["w_ukv"].rearrange("(c p) f -> p c f", p=128)
        for c in range(4):
            P.dma("pool", lambda e, c=c: e.dma_start(out=wuq[:, c, :, :], in_=w_uq_v[:, c, :, :]), writes=[t_wuq[c]])
            P.dma("pool", lambda e, c=c: e.dma_start(out=wukv[:, c, :], in_=w_ukv_v[:, c, :]), writes=[t_wukv[c]])
            P.dma("pool", lambda e, c=c: e.dma_start(out=wuqr[:, c, :, 0:32], in_=w_uq_v[:, c, :, 160:192]), writes=[t_wuqr])
            P.dma("pool", lambda e, c=c: e.dma_start(out=wuqr[:, c, :, 32:64], in_=w_uq_v[:, c, :, 128:160]), writes=[t_wuqr])
        P.op("dve", lambda e: e.tensor_scalar(out=wuqr[:, :, :, 0:32], in0=wuqr[:, :, :, 0:32], scalar1=-1.0, scalar2=None, op0=ALU.mult),
             reads=[t_wuqr], writes=[t_wuqr])
        A2 = Arena(nc, xT_rec[0], xT_rec[1])
        xT_extra = []

        def al2(name, shape, dt=F32):
            h, _ = A2.alloc(name, shape, dt)
            t = T(name, after=list(xT_T))
            xT_extra.append(t)
            return h, t
        mq, t_mq = al("maskq", [128, 128], BF16)
        P.dma("pool", lambda e: e.dma_start(out=mq[:], in_=maskq_d[:, :]), writes=[t_mq])
        cosT, t_cosT = al2("a2cosT", [128, NT])
        sinT, t_sinT = al2("a2sinT", [128, NT])
        P.dma("sp", lambda e: e.dma_start(out=cosT[0:64, :], in_=cosT_d[:, :]), writes=[t_cosT])
        P.dma("sp", lambda e: e.dma_start(out=sinT[0:64, :], in_=sinT_d[:, :]), writes=[t_sinT])
        Pex, t_Pex = al2("a2Pex", [128, 2048])
        qt1, t_qt1 = al2("a2qt1", [128, 256])
        qt2, t_qt2 = al2("a2qt2", [128, 256])
        KTb, Vb, qnb, qrb = [], [], [], []
        for i in range(2):
            h_, r = A.alloc(f"KT{i}", [128, NPROMPT], BF16)
            KTb.append((h_, [A.newT(r, f"KT{i}_{k}") for k in range(4)]))
            h_, r = A.alloc(f"V{i}", [128, 16, 128], BF16)
            Vb.append((h_, [A.newT(r, f"V{i}_{k}") for k in range(4)]))
            h_, r = A.alloc(f"qn{i}", [128, NPROMPT], BF16)
            qnb.append((h_, [A.newT(r, f"qn{i}_{k}") for k in range(4)]))
            h_, r = A.alloc(f"qr{i}", [128, NPROMPT], BF16)
            qrb.append((h_, [A.newT(r, f"qr{i}_{k}") for k in range(4)]))
        qns, r = A.alloc("qns", [128, 8, 64], BF16)
        t_qns = [A.newT(r, f"qns{h}") for h in range(8)]
        qrs, r = A.alloc("qrs", [128, 8, 64], BF16)
        t_qrs = [A.newT(r, f"qrs{h}") for h in range(8)]
        NPN = 3
        Pn = [al(f"Pn{i}", [128, 2048], BF16) for i in range(NPN)]
        PT = [al2("a2PT0", [128, 16, 128], BF16), al("PT1", [128, 16, 128], BF16)]
        st8 = [al(f"st8_{i}", [128, 16]) for i in range(NPN)]
        ctr = dict(blk=0, rnd=0, att=0, pb=0)
        OB = 6

        def next_pb():
            ctr["pb"] += 1
            return 5 if ctr["pb"] % 2 else 7

        def proj_items(h):
            items = []
            KT, tKT = KTb[h % 2]
            V, tV = Vb[h % 2]
            qn, tqn = qnb[h % 2]
            qr, tqr = qrb[h % 2]
            for t, (t0, n) in enumerate(TILES):
                cr = [cqnT_T[s] for s in SUB_OF_TILE[t]]

                def it_qn(t=t, t0=t0, n=n, cr=cr):
                    PB = next_pb()

                    def mm(e):
                        for c in range(4):
                            i = e.matmul(bank(PB)[:, 0:n], lhsT=wuq[:, c, h, 0:128], rhs=cqnT[:, c, t0:t0 + n], start=(c == 0), stop=(c == 3))
                        return i
                    P.op("pe", mm, reads=cr + t_wuq, writes=[bankT[PB]])
                    yield
                    if t < 4:
                        P.op("act", lambda e: e.activation(out=qn[:, t0:t0 + n], in_=bank(PB)[:, 0:n], func=AF.Copy), reads=[bankT[PB]], writes=[tqn[t]])
                    else:
                        P.op("act", lambda e: e.activation(out=qns[:, h, :], in_=bank(PB)[:, 0:n], func=AF.Copy), reads=[bankT[PB]], writes=[t_qns[h]])
                items.append(it_qn)
                for hf in range(2 if t < 4 else 1):
                    n2 = 256 if t < 4 else 64
                    c0 = t0 + hf * 256

                    def it_qr(t=t, c0=c0, n2=n2, cr=cr):
                        PB = next_pb()

                        def mm(e):
                            for c in range(4):
                                e.matmul(bank(PB)[0:64, 0:n2], lhsT=wuq[:, c, h, 128:192], rhs=cqnT[:, c, c0:c0 + n2], start=(c == 0), stop=(c == 3),
                                         skip_group_check=True)
                            for c in range(4):
                                i = e.matmul(bank(PB)[0:64, 256:256 + n2], lhsT=wuqr[:, c, h, :], rhs=cqnT[:, c, c0:c0 + n2], start=False, stop=(c == 3),
                                             skip_group_check=True)
                            return i
                        P.op("pe", mm, reads=cr + t_wuq + [t_wuqr], writes=[bankT[PB]])
                        yield
                        P.op("dve", lambda e: e.tensor_tensor(out=qt1[0:64, 0:n2], in0=bank(PB)[0:64, 0:n2], in1=cosT[0:64, c0:c0 + n2], op=ALU.mult),
                             reads=[bankT[PB], t_cosT], writes=[t_qt1])
                        P.op("dve", lambda e: e.tensor_tensor(out=qt2[0:64, 0:n2], in0=bank(PB)[0:64, 256:256 + n2], in1=sinT[0:64, c0:c0 + n2], op=ALU.mult),
                             reads=[bankT[PB], t_sinT], writes=[t_qt2])
                        if t < 4:
                            P.op("pool", lambda e: e.tensor_tensor(out=qr[0:64, c0:c0 + n2], in0=qt1[0:64, 0:n2], in1=qt2[0:64, 0:n2], op=ALU.add),
                                 reads=[t_qt1, t_qt2], writes=[tqr[t]])
                        else:
                            P.op("pool", lambda e: e.tensor_tensor(out=qrs[0:64, h, :], in0=qt1[0:64, 0:n2], in1=qt2[0:64, 0:n2], op=ALU.add),
                                 reads=[t_qt1, t_qt2], writes=[t_qrs[h]])
                    items.append(it_qr)
            for kb in range(4):
                def it_k(kb=kb):
                    PB = next_pb()

                    def mm(e):
                        for c in range(4):
                            i = e.matmul(bank(PB)[:, :], lhsT=wukv[:, c, h * 256:h * 256 + 128], rhs=latTp[:, c, kb * 512:(kb + 1) * 512],
                                         start=(c == 0), stop=(c == 3))
                        return i
                    P.op("pe", mm, reads=[latTp_T[4 * kb + i] for i in range(4)] + t_wukv, writes=[bankT[PB]])
                    yield
                    P.op("act", lambda e: e.activation(out=KT[:, kb * 512:(kb + 1) * 512], in_=bank(PB)[:, :], func=AF.Copy),
                         reads=[bankT[PB]], writes=[tKT[kb]])

                def it_v(kq=kb):
                    PB = next_pb()

                    def mm(e):
                        for j in range(4):
                            kt = 4 * kq + j
                            for c in range(4):
                                i = e.matmul(bank(PB)[:, j * 128:(j + 1) * 128], lhsT=latTp[:, c, kt * 128:(kt + 1) * 128],
                                             rhs=wukv[:, c, h * 256 + 128:h * 256 + 256], start=(c == 0), stop=(c == 3), skip_group_check=True)
                        return i
                    P.op("pe", mm, reads=[latTp_T[4 * kq + i] for i in range(4)] + t_wukv, writes=[bankT[PB]])
                    yield
                    P.op("dve", lambda e: e.tensor_copy(out=V[:, 4 * kq:4 * kq + 4, :], in_=bank(PB)[:, :].rearrange("p (j d) -> p j d", j=4)),
                         reads=[bankT[PB]], writes=[tV[kq]])
                items.append(it_k)
                items.append(it_v)
            return items

        def proj_items_s(h, j, slot):
            items = []
            KT, tKT = KTb[slot]
            V, tV = Vb[slot]
            for kb in range(3):
                cols = min(512, 1056 - kb * 512)

                def it_k(kb=kb, cols=cols):
                    PB = next_pb()

                    def mm(e):
                        for c in range(4):
                            i = e.matmul(bank(PB)[:, 0:cols], lhsT=wukv[:, c, h * 256:h * 256 + 128], rhs=latTs[:, c, j, kb * 512:kb * 512 + cols],
                                         start=(c == 0), stop=(c == 3))
                        return i
                    P.op("pe", mm, reads=latTs_T[j] + t_wukv, writes=[bankT[PB]])
                    yield
                    P.op("act", lambda e: e.activation(out=KT[:, kb * 512:kb * 512 + cols], in_=bank(PB)[:, 0:cols], func=AF.Copy),
                         reads=[bankT[PB]], writes=[tKT[kb]])
                items.append(it_k)
                kts = [kt for kt in range(4 * kb, min(9, 4 * kb + 4))]

                def it_v(kq=kb, kts=kts):
                    PB = next_pb()

                    def mm(e):
                        for kt in kts:
                            kk = min(128, 1056 - kt * 128)
                            for c in range(4):
                                i = e.matmul(bank(PB)[0:kk, (kt - 4 * kq) * 128:(kt - 4 * kq + 1) * 128], lhsT=latTs[:, c, j, kt * 128:kt * 128 + kk],
                                             rhs=wukv[:, c, h * 256 + 128:h * 256 + 256], start=(c == 0), stop=(c == 3), skip_group_check=True)
                        return i
                    P.op("pe", mm, reads=latTs_T[j] + t_wukv, writes=[bankT[PB]])
                    yield
                    nkt_ = len(kts)
                    P.op("dve", lambda e: e.tensor_copy(out=V[:, 4 * kq:4 * kq + nkt_, :],
                                                        in_=bank(PB)[:, 0:nkt_ * 128].rearrange("p (j d) -> p j d", j=nkt_)),
                         reads=[bankT[PB]], writes=[tV[kq]])
                items.append(it_v)
            return items

        stT = [{k: T(f"st{i}_{k}") for k in ["mx0", "mx1", "ng0", "ng1", "sm0", "sm1", "c"]} for i in range(NPN)]

        def stage1(n, h, qT, t_q, qrT_, t_qr_, qc0, mq_, KT, tKT, krT_ap, t_krT, nk, diag, bsz=1024):
            Pnh, t_Pn = Pn[n % NPN]
            sth, _ = st8[n % NPN]
            tt = stT[n % NPN]
            blocks = [(b0, min(bsz, nk - b0)) for b0 in range(0, nk, bsz)]
            for bi, (b0, bn) in enumerate(blocks):
                pr = ctr["blk"] % 2 if bsz == 1024 else 0
                ctr["blk"] += 1
                nbk = (bn + 511) // 512
                bts = [bankT[2 * pr + x] for x in range(nbk)]

                def mms(e, b0=b0, bn=bn, pr=pr, nbk=nbk):
                    for x in range(nbk):
                        k0 = b0 + x * 512
                        cols = min(512, b0 + bn - k0)
                        has_diag = diag is not None and (diag * 128) // 512 == k0 // 512
                        e.matmul(bank(2 * pr + x)[0:mq_, 0:cols], lhsT=qT[:, qc0:qc0 + mq_], rhs=KT[:, k0:k0 + cols], start=True, stop=False,
                                 skip_group_check=True)
                        i = e.matmul(bank(2 * pr + x)[0:mq_, 0:cols], lhsT=qrT_[0:64, qc0:qc0 + mq_], rhs=krT_ap[0:64, k0:k0 + cols],
                                     start=False, stop=not has_diag, skip_group_check=True)
                        if has_diag:
                            dc = (diag * 128) % 512
                            i = e.matmul(bank(2 * pr + x)[0:mq_, dc:dc + 128], lhsT=ident_b[0:mq_, 0:mq_], rhs=mq[0:mq_, :], start=False, stop=True,
                                         skip_group_check=True)
                    return i
                kbs = sorted(set((b0 + x * 512) // 512 for x in range(nbk)))
                P.op("pe", mms, reads=t_q + t_qr_ + [tKT[k] for k in kbs if k < len(tKT)] + [t_identb, t_mq] + t_krT, writes=bts)
                yield
                sc = (psA[0:mq_, 1024 * pr:1024 * pr + bn])
                P.op("dve", lambda e, sc=sc, bi=bi: e.tensor_reduce(out=sth[0:mq_, bi:bi + 1], in_=sc, op=ALU.max, axis=AX.X), reads=bts, writes=[tt[f"mx{bi}"]])
                P.op("dve", lambda e, bi=bi: e.tensor_scalar(out=sth[0:mq_, 2 + bi:3 + bi], in0=sth[0:mq_, bi:bi + 1], scalar1=-SCALE, scalar2=None, op0=ALU.mult),
                     reads=[tt[f"mx{bi}"]], writes=[tt[f"ng{bi}"]])
                yield
                P.op("act", lambda e, sc=sc, bi=bi, b0=b0, bn=bn: e.activation(out=Pex[0:mq_, b0:b0 + bn], in_=sc, func=AF.Exp, scale=SCALE,
                                                                               bias=sth[0:mq_, 2 + bi:3 + bi], accum_out=sth[0:mq_, 4 + bi:5 + bi]),
                     reads=bts + [tt[f"ng{bi}"]], writes=[t_Pex, tt[f"sm{bi}"]])
                yield
            tc = tt["c"]
            if len(blocks) == 1:
                P.op("dve", lambda e: e.reciprocal(out=sth[0:mq_, 6:7], in_=sth[0:mq_, 4:5]), reads=[tt["sm0"]], writes=[tc])
                P.op("dve", lambda e: e.tensor_scalar(out=Pnh[0:mq_, 0:nk], in0=Pex[0:mq_, 0:nk], scalar1=sth[0:mq_, 6:7], scalar2=None, op0=ALU.mult),
                     reads=[t_Pex, tc], writes=[t_Pn])
            else:
                P.op("dve", lambda e: e.tensor_tensor(out=sth[0:mq_, 8:9], in0=sth[0:mq_, 0:1], in1=sth[0:mq_, 1:2], op=ALU.max),
                     reads=[tt["mx0"], tt["mx1"]], writes=[tc])
                P.op("dve", lambda e: e.tensor_scalar(out=sth[0:mq_, 6:8], in0=sth[0:mq_, 0:2], scalar1=sth[0:mq_, 8:9], scalar2=None, op0=ALU.subtract),
                     reads=[tc, tt["mx0"], tt["mx1"]], writes=[tc])
                yield
                P.op("act", lambda e: e.activation(out=sth[0:mq_, 6:8], in_=sth[0:mq_, 6:8], func=AF.Exp, scale=SCALE), reads=[tc], writes=[tc])
                yield
                P.op("dve", lambda e: e.tensor_tensor(out=sth[0:mq_, 9:11], in0=sth[0:mq_, 6:8], in1=sth[0:mq_, 4:6], op=ALU.mult),
                     reads=[tc, tt["sm0"], tt["sm1"]], writes=[tc])
                P.op("dve", lambda e: e.tensor_tensor(out=sth[0:mq_, 11:12], in0=sth[0:mq_, 9:10], in1=sth[0:mq_, 10:11], op=ALU.add), reads=[tc], writes=[tc])
                P.op("dve", lambda e: e.reciprocal(out=sth[0:mq_, 12:13], in_=sth[0:mq_, 11:12]), reads=[tc], writes=[tc])
                P.op("dve", lambda e: e.tensor_scalar(out=sth[0:mq_, 13:15], in0=sth[0:mq_, 6:8], scalar1=sth[0:mq_, 12:13], scalar2=None, op0=ALU.mult),
                     reads=[tc], writes=[tc])
                yield
                for bi, (b0, bn) in enumerate(blocks):
                    eng_ = "dve" if bi == 0 else "pool"
                    P.op(eng_, lambda e, bi=bi, b0=b0, bn=bn: e.tensor_scalar(out=Pnh[0:mq_, b0:b0 + bn], in0=Pex[0:mq_, b0:b0 + bn],
                                                                              scalar1=sth[0:mq_, 13 + bi:14 + bi], scalar2=1.0, op0=ALU.mult, op1=ALU.mult),
                         reads=[t_Pex, tc], writes=[t_Pn])
            yield

        def stage2(n, h, qc0, mq_, nk, V, tV, s_out):
            Pnh, t_Pn = Pn[n % NPN]
            PTh, t_PT = PT[n % 2]
            nkt = (nk + 127) // 128
            use_xbar = USE_XBAR and mq_ == 128 and nk % 128 == 0
            if use_xbar:
                P.dmat("sp", lambda e: e.dma_start_transpose(out=PTh[:, 0:nkt, :], in_=Pnh[0:128, 0:nk]), reads=[t_Pn], writes=[t_PT])
                yield
            for r0 in (range(0, nkt, 8) if not use_xbar else []):
                r1 = min(nkt, r0 + 8)
                bk = 4
                pv = bank(bk).bitcast(BF16)

                def trp(e, r0=r0, r1=r1, pv=pv):
                    for kt in range(r0, r1):
                        kk = min(128, nk - kt * 128)
                        i = e.transpose(out=pv[0:kk, (kt - r0) * 128:(kt - r0) * 128 + mq_], in_=Pnh[0:mq_, kt * 128:kt * 128 + kk],
                                        identity=ident_b[0:mq_, 0:mq_])
                    return i
                P.op("pe", trp, reads=[t_Pn, t_identb], writes=[bankT[bk]])
                yield
                P.op("act", lambda e, r0=r0, r1=r1, pv=pv: e.activation(
                    out=PTh[:, r0:r1, 0:mq_], in_=pv.rearrange("p (k t) -> p k t", k=8)[:, 0:r1 - r0, 0:mq_], func=AF.Copy),
                    reads=[bankT[bk]], writes=[t_PT])
                yield

            def mmo(e):
                for kt in range(nkt):
                    kk = min(128, nk - kt * 128)
                    i = e.matmul(bank(OB)[:, 0:mq_], lhsT=V[0:kk, kt, :], rhs=PTh[0:kk, kt, 0:mq_], start=(kt == 0), stop=(kt == nkt - 1))
                return i
            P.op("pe", mmo, reads=[tV[k] for k in range(min(4, (nkt + 3) // 4))] + [t_PT], writes=[bankT[OB]])
            yield
            P.op("dve", lambda e: e.tensor_copy(out=oT[:, h, qc0:qc0 + mq_], in_=bank(OB)[:, 0:mq_]), reads=[bankT[OB]], writes=[oT_T[h][s_out]])
            yield

        def interleave_aw(pairs):
            pairs = [[g, w] for g, w in pairs if g is not None]
            while pairs:
                for pr in list(pairs):
                    for _ in range(pr[1]):
                        try:
                            next(pr[0])
                        except StopIteration:
                            pairs.remove(pr)
                            break

        def items_gen(items):
            for it in items:
                yield from it()
                yield

        def run_item(it):
            for _ in it():
                pass

        def interleave(gens):
            gens = [g for g in gens if g is not None]
            while gens:
                for g in list(gens):
                    try:
                        next(g)
                    except StopIteration:
                        gens.remove(g)

        NH = NHEADS_DBG
        DEPTH = 2
        for it in proj_items(0):
            run_item(it)
        pend_s2 = []
        for h in range(NH):
            nxt = proj_items(h + 1) if h + 1 < NH else []
            KT, tKT = KTb[h % 2]
            V, tV = Vb[h % 2]
            qn, tqn = qnb[h % 2]
            qr, tqr = qrb[h % 2]
            for i in range(16):
                n = ctr["att"]
                ctr["att"] += 1
                g1 = stage1(n, h, qn, [tqn[i // 4]], qr, [tqr[i // 4]], 128 * i, 128, KT, tKT, krTp, krTp_T[0:i + 1], 128 * (i + 1), i)
                pend_s2.append((n, h, 128 * i, 128, 128 * (i + 1), V, tV, i))
                g2 = stage2(*pend_s2.pop(0)) if len(pend_s2) > DEPTH else None
                take = []
                if i >= DEPTH:
                    for _ in range(2 if i % 2 == 0 else 1):
                        if nxt:
                            take.append(nxt.pop(0))
                interleave_aw([(g2, ATT_W[1]), (g1, ATT_W[0]), (items_gen(take), ATT_W[2])])
            while nxt:
                run_item(nxt.pop(0))
        while pend_s2:
            interleave([stage2(*pend_s2.pop(0))])
        ctxs = [(h, j) for h in range(NH) for j in range(2)]
        nctx = len(ctxs)

        def kv_items(c):
            if c >= nctx:
                return [], []
            its = proj_items_s(ctxs[c][0], ctxs[c][1], c % 2)
            return its[0::2], its[1::2]

        def s1_ctx(c, n):
            h, j = ctxs[c]
            KT, tKT = KTb[c % 2]
            return stage1(n, h, qns[:, h, :], [t_qns[h]], qrs[:, h, :], [t_qrs[h]], 32 * j, 32, KT, tKT, krTs[:, j, :], krTs_T[j], 1056, None, bsz=1536)

        def s2_ctx(c, n):
            h, j = ctxs[c]
            V, tV = Vb[c % 2]
            return stage2(n, h, NPROMPT + 32 * j, 32, 1056, V, tV, 16)

        if nctx:
            n0 = ctr["att"]
            ctr["att"] += nctx
            k0, v0 = kv_items(0)
            k1, _ = kv_items(1)
            interleave([items_gen(k0 + v0)])
            interleave([s1_ctx(0, n0), items_gen(k1)])
            for ci in range(nctx):
                k2, _ = kv_items(ci + 2)
                _, v1 = kv_items(ci + 1)
                interleave([s1_ctx(ci + 1, n0 + ci + 1) if ci + 1 < nctx else None, s2_ctx(ci, n0 + ci), items_gen(k2 + v1)])
        A.reset(mk_a)
        return mk, oT, oT_T, xT_extra

    def mix_ln_phase(mk, oT, oT_T, xT_extra):
        w_out_v = w["w_out"].rearrange("(ko ki) f -> ki ko f", ki=128)
        wob, r = A.alloc("wob", [128, 8, D], BF16)
        t_wob = [A.newT(r, f"wob{j}") for j in range(2)]
        for j in range(2):
            P.dma("pool", lambda e, j=j: e.dma_start(out=wob[:, :, j * 512:(j + 1) * 512], in_=w_out_v[:, 8:16, j * 512:(j + 1) * 512]), writes=[t_wob[j]])
        lng, t_lng = al("lng2", [128, D])
        lnb, t_lnb = al("lnb2", [128, D])
        P.dma("sp", lambda e: e.dma_start(out=lng[:], in_=lnp["ln2_g"][:, :]), writes=[t_lng])
        P.dma("sp", lambda e: e.dma_start(out=lnb[:], in_=lnp["ln2_b"][:, :]), writes=[t_lnb])
        bufs = {}
        for nm, shape, dt, nb in [("xres", [128, D], F32, 2), ("yv", [128, D], F32, 2), ("xo", [128, D], F32, 2), ("xob", [128, D], BF16, 2),
                                  ("st", [128, 12], F32, 2), ("mv", [128, 2], F32, 2), ("rs", [128, 2], F32, 2)]:
            bufs[nm] = [al(f"b3{nm}{i}", shape, dt) for i in range(nb)]
        def down3(s):
            t0, m = SUBT[s]
            xres, t_xres = bufs["xres"][s % 2]
            yv, t_yv = bufs["yv"][s % 2]
            P.dma("sp", lambda e: e.dma_start(out=xres[0:m, :], in_=r1d[t0:t0 + m, :]), reads=[r1d_T[s]], writes=[t_xres])
            for half in range(2):
                bk = 4 * (s % 2) + half

                def mmo(e, bk=bk, half=half):
                    for k in range(8):
                        i = e.matmul(bank(bk)[0:m, :], lhsT=oT[:, k, t0:t0 + m], rhs=wob[:, k, half * 512:(half + 1) * 512], start=(k == 0), stop=(k == 7))
                    return i
                P.op("pe", mmo, reads=[oT_T[h][s] for h in range(8)] + [t_wob[half]], writes=[bankT[bk]])
                yield
                P.op("dve", lambda e, bk=bk, half=half: e.tensor_tensor(
                    out=yv[0:m, half * 512:(half + 1) * 512], in0=bank(bk)[0:m, :], in1=xres[0:m, half * 512:(half + 1) * 512], op=ALU.add),
                    reads=[bankT[bk], t_xres], writes=[t_yv])
                yield

        def tail3(s):
            t0, m = SUBT[s]
            yv, t_yv = bufs["yv"][s % 2]
            return ln_tail(s, t0, m, yv, t_yv, bufs, lng, t_lng, lnb, t_lnb, x2d, x2d_T, True, xT_extra)

        interleave_g([down3(0)])
        for s in range(17):
            interleave_g([down3(s + 1) if s + 1 < 17 else None, tail3(s)])
        A.reset(mk)

    eps_t, r = A.alloc("eps_t", [128, 1], F32)
    t_eps = A.newT(r, "eps")
    P.op("dve", lambda e: e.memset(eps_t[:], EPS), writes=[t_eps])

    x1d_T = [T(f"x1d{s}") for s in range(17)]
    r1d_T = [T(f"r1d{s}") for s in range(17)]
    ffn_phase("f1", w["ffn1_w_gate"], w["ffn1_w_up"], w["ffn1_w_down"], lnp["ln1_g"], lnp["ln1_b"], xin, None, x1d, x1d_T, True, pre_x=True)
    x2d_T = [T(f"x2d{s}") for s in range(17)]
    ssd_phase()
    if not STOP_B1:
        mix_ln_phase(*mla_phase())
        ffn_phase("f2", w["ffn2_w_gate"], w["ffn2_w_up"], w["ffn2_w_down"], lnp["ln3_g"], lnp["ln3_b"], x2d, x2d_T, y_out, None, False)

    P.build()
    nc._prog_stats = P.stats
    return nc


_NC_CACHE = {}


def _get_nc(debug=False):
    if debug not in _NC_CACHE:
        _NC_CACHE[debug] = build_program(debug)
    return _NC_CACHE[debug]


def make_in_maps(inputs):
    f32 = np.float32
    g = {k: np.asarray(v) for k, v in inputs.items()}
    shared = {}
    for nm in ["ffn1_w_gate", "ffn1_w_up", "ffn1_w_down", "ffn2_w_gate", "ffn2_w_up", "ffn2_w_down",
               "w_in", "w_uq", "w_ukv", "w_out"]:
        shared[nm] = np.ascontiguousarray(g[nm][0], dtype=f32)
    for nm in ["ln1_g", "ln1_b", "ln2_g", "ln2_b", "ln3_g", "ln3_b"]:
        shared[nm] = np.ascontiguousarray(np.broadcast_to(g[nm][0][None, :], (128, D)), dtype=f32)
    shared["ident"] = np.eye(128, dtype=f32)
    cwv = g["conv_w"][0]
    shared["cwp"] = np.ascontiguousarray(cwv.reshape(4, 12, 128).transpose(2, 1, 0), dtype=f32)
    shared["cbp"] = np.ascontiguousarray(g["conv_b"][0].reshape(12, 128).T, dtype=f32)
    for nm, key in [("dtb", "dt_bias"), ("alog", "a_log"), ("dsk", "d_skip")]:
        shared[nm] = np.ascontiguousarray(np.broadcast_to(g[key][0][None, :], (128, 16)), dtype=f32)
    shared["gss"] = np.ascontiguousarray(np.broadcast_to(g["ssd_norm_g"][0][None, :], (128, D)), dtype=f32)
    j = np.arange(128)[:, None]
    i = np.arange(128)[None, :]
    csts = np.zeros((128, 8, 128), f32)
    for v, ch in enumerate((64, 32)):
        same = (j // ch) == (i // ch)
        lim = 128 if ch == 64 else 64
        ok = (j < lim) & (i < lim)
        csts[:, 4 * v + 0, :] = (same & (j <= i) & ok)
        csts[:, 4 * v + 1, :] = (same & ok)
        csts[:, 4 * v + 2, :] = ((j // ch) == 0) & (j < lim)
        csts[:, 4 * v + 3, :] = ((j // ch) == 1) & (j < lim)
    shared["csts"] = csts
    nmp = np.where(((j // 64) == (i // 64)) & (i >= j), 0.0, -30000.0).astype(f32)
    shared["nmp"] = np.ascontiguousarray(np.tile(nmp, (1, 4)))
    nms = np.where(((j // 32) == (i // 32)) & (i >= j), 0.0, -30000.0).astype(f32)[:, 0:64]
    shared["nms"] = np.ascontiguousarray(np.tile(nms, (1, 4)))
    idl = np.zeros((128, 2, 64), f32)
    kk = np.arange(128)
    idl[kk, 0, kk % 64] = 1.0
    idl[kk[:64], 1, kk[:64] % 32] = 1.0
    shared["idl"] = idl
    il = np.arange(64)[None, :]
    nmlp = np.where(il >= (kk[:, None] % 64), 0.0, -30000.0).astype(f32)
    shared["nmlp"] = np.ascontiguousarray(np.tile(nmlp, (1, 4)))
    nmls = np.where(il[:, :32] >= (kk[:, None] % 32), 0.0, -30000.0).astype(f32)
    shared["nmls"] = np.ascontiguousarray(np.tile(nmls, (1, 4)))
    shared["gq"] = np.ascontiguousarray(np.broadcast_to(g["q_norm_g"][0][None, :], (128, 512)), dtype=f32)
    shared["gkv"] = np.ascontiguousarray(np.broadcast_to(g["kv_norm_g"][0][None, :], (128, 512)), dtype=f32)
    pos = np.concatenate([np.arange(NPROMPT), PAST + np.arange(32), PAST + np.arange(32)]).astype(np.float64)
    inv = 10000.0 ** (-np.arange(0, 64, 2, dtype=np.float64) / 64.0)
    ang = pos[:, None] * inv[None, :]
    cs_, sn_ = np.cos(ang).astype(f32), np.sin(ang).astype(f32)
    shared["cs4"] = np.ascontiguousarray(np.concatenate([cs_, sn_, sn_, cs_], axis=1))
    shared["cosT"] = np.ascontiguousarray(np.concatenate([cs_, cs_], axis=1).T)
    shared["sinT"] = np.ascontiguousarray(np.concatenate([sn_, sn_], axis=1).T)
    shared["maskq"] = np.where((j < 64) & (i >= 64), -30000.0, 0.0).astype(f32)
    maps = []
    for c in range(8):
        m = dict(shared)
        m["xin"] = np.ascontiguousarray(np.concatenate(
            [g["x_prompt"][c], g["x_sample"][2 * c], g["x_sample"][2 * c + 1]], axis=0), dtype=f32)
        m["sconv"] = np.ascontiguousarray(g["state_conv"][0, 2 * c:2 * c + 2], dtype=f32)
        m["sssm"] = np.ascontiguousarray(g["state_ssm"][0, 2 * c:2 * c + 2], dtype=f32)
        m["clat"] = np.ascontiguousarray(g["cache_latent"][0, 2 * c:2 * c + 2], dtype=f32)
        m["ckr"] = np.ascontiguousarray(g["cache_k_rope"][0, 2 * c:2 * c + 2], dtype=f32)
        maps.append(m)
    return maps


def kernel(**inputs):
    nc = _get_nc(False)
    maps = make_in_maps(inputs)
    res = run_bass_kernel_spmd(nc, maps, core_ids=list(range(8)))
    rr = res.results
    f32 = np.float32

    def prm(key, n0=NPROMPT):
        return np.stack([np.asarray(rr[c][key][0:n0], f32) for c in range(8)], axis=0)

    def smp(key):
        return np.stack([np.asarray(rr[c][key][NPROMPT + 32 * j:NPROMPT + 32 * (j + 1)], f32) for c in range(8) for j in range(2)], axis=0)

    y_p, y_s = prm("y"), smp("y")
    lat_p, lat_s = prm("lat_o")[None], smp("lat_o")[None]
    kr_p, kr_s = prm("kr_o")[None], smp("kr_o")[None]
    conv_p = np.stack([np.asarray(rr[c]["conv_o"][0], f32) for c in range(8)], axis=0)[None]
    conv_s = np.stack([np.asarray(rr[c]["conv_o"][1 + j], f32) for c in range(8) for j in range(2)], axis=0)[None]
    ssm_p = np.stack([np.asarray(rr[c]["ssm_o"][0], f32) for c in range(8)], axis=0)[None]
    ssm_s = np.stack([np.asarray(rr[c]["ssm_o"][1 + j], f32) for c in range(8) for j in range(2)], axis=0)[None]
    return (y_p, y_s, lat_p, kr_p, conv_p, ssm_p, lat_s, kr_s, conv_s, ssm_s)
```

```python
import math
from contextlib import ExitStack
import numpy as np
import concourse.bass as bass
import concourse.mybir as mybir
from concourse.bass_utils import run_bass_kernel_spmd

F32 = mybir.dt.float32
BF16 = mybir.dt.bfloat16
AF = mybir.ActivationFunctionType
ALU = mybir.AluOpType
AX = mybir.AxisListType

D = 1024
DFF = 2816
NFF = DFF // 128
NPROMPT = 2048
NSAMP = 64
NT = NPROMPT + NSAMP
PAST = 1024
ALPHA = 2.0 ** 0.25
EPS = 1e-5
DINP = 3664
SUBT = [(128 * s, 128) for s in range(16)] + [(2048, 64)]
TILES = [(512 * t, 512) for t in range(4)] + [(2048, 64)]
SUB_OF_TILE = [[4 * t + i for i in range(4)] for t in range(4)] + [[16]]
ENGS = ("pe", "act", "dve", "pool", "sp")
STOP_B1 = False
MLA_LEVEL = 9
USE_XBAR = False
SSD_W = (1, 2)
ATT_W = (1, 1, 1)
SSD_K = 8
NHEADS_DBG = 8


class T:
    __slots__ = ("name", "w", "r", "after", "excl")

    def __init__(self, name, after=None, excl=False):
        self.name = name
        self.w = None
        self.r = []
        self.after = after
        self.excl = excl


class Op:
    __slots__ = ("eng", "fn", "reads", "writes", "dma", "sig", "deps", "sem", "val", "idx")

    def __init__(self, eng, fn, reads, writes, dma):
        self.eng = eng
        self.fn = fn
        self.reads = reads
        self.writes = writes
        self.dma = dma
        self.sig = dma
        self.deps = []
        self.sem = None
        self.val = 0


class Prog:
    def __init__(self, nc, n_dma_sems=16):
        self.nc = nc
        self.ops = []
        self.n_dma_sems = n_dma_sems
        self.xbar = T("xbar")

    def op(self, eng, fn, reads=(), writes=()):
        o = Op(eng, fn, list(reads), list(writes), False)
        self.ops.append(o)
        return o

    def dma(self, eng, fn, reads=(), writes=()):
        o = Op(eng, fn, list(reads) + [self.xbar], list(writes), True)
        self.ops.append(o)
        return o

    def dmat(self, eng, fn, reads=(), writes=()):
        o = Op(eng, fn, list(reads), list(writes) + [self.xbar], True)
        self.ops.append(o)
        return o

    def build(self):
        nc = self.nc
        for i, x in enumerate(self.ops):
            x.idx = i
            deps = {}
            for t in x.reads + x.writes:
                if t.after is not None:
                    for a in t.after:
                        if a.w is not None:
                            deps[a.w.idx] = a.w
                        for r in a.r:
                            deps[r.idx] = r
                    t.after = None
            for t in x.reads:
                if t.w is not None:
                    deps[t.w.idx] = t.w
                if t.excl:
                    for r in t.r:
                        if r.eng != x.eng:
                            deps[r.idx] = r
            for t in x.writes:
                if t.w is not None:
                    deps[t.w.idx] = t.w
                for r in t.r:
                    if r.eng == x.eng == "pe" and not r.dma and not x.dma:
                        continue
                    deps[r.idx] = r
            deps.pop(i, None)
            for y in deps.values():
                if y.eng == "pe" and x.eng == "pe" and not y.dma and not x.dma:
                    continue
                y.sig = True
                x.deps.append(y)
            for t in x.reads:
                t.r.append(x)
            for t in x.writes:
                t.w = x
                t.r = []
        with ExitStack() as es:
            esem = {e: es.enter_context(nc.semaphore(f"s_{e}")) for e in ENGS}
            dq = ("sp", "pool", "act")
            dsem = {e: [es.enter_context(nc.semaphore(f"d_{e}{k}")) for k in range(self.n_dma_sems)] for e in dq}
            cnt = {e: 0 for e in ENGS}
            dcnt = {e: [0] * self.n_dma_sems for e in dq}
            drr = {e: 0 for e in dq}
            prev_on_sem = {}
            for x in self.ops:
                if x.dma:
                    k = drr[x.eng]
                    drr[x.eng] = (k + 1) % self.n_dma_sems
                    dcnt[x.eng][k] += 16
                    x.sem = dsem[x.eng][k]
                    x.val = dcnt[x.eng][k]
                    key = (x.eng, k)
                    if key in prev_on_sem:
                        x.deps.append(prev_on_sem[key])
                    prev_on_sem[key] = x
                elif x.sig:
                    cnt[x.eng] += 1
                    x.sem = esem[x.eng]
                    x.val = cnt[x.eng]
            self.stats = dict(cnt=dict(cnt), dmax={e: max(dcnt[e]) for e in dq}, nops=len(self.ops))
            per_eng = {e: [o for o in self.ops if o.eng == e] for e in ENGS}
            finals = [(esem[e], cnt[e]) for e in ENGS if cnt[e]]
            for e in dq:
                finals += [(dsem[e][k], dcnt[e][k]) for k in range(self.n_dma_sems) if dcnt[e][k]]

            def emit(e, eng):
                seen = {}
                for x in per_eng[e]:
                    need = {}
                    for y in x.deps:
                        sid = id(y.sem)
                        if seen.get(sid, 0) < y.val and need.get(sid, (None, 0))[1] < y.val:
                            need[sid] = (y.sem, y.val)
                    for sid, (s, v) in need.items():
                        eng.wait_ge(s, v)
                        seen[sid] = v
                    inst = x.fn(eng)
                    if x.sig:
                        inst.then_inc(x.sem, 16 if x.dma else 1)
                if e == "sp":
                    for s, v in finals:
                        if seen.get(id(s), 0) < v:
                            eng.wait_ge(s, v)

            with nc.Block() as block:
                @block.tensor
                def _(eng):
                    emit("pe", eng)

                @block.scalar
                def _(eng):
                    emit("act", eng)

                @block.vector
                def _(eng):
                    emit("dve", eng)

                @block.gpsimd
                def _(eng):
                    emit("pool", eng)

                @block.sync
                def _(eng):
                    emit("sp", eng)


class Arena:
    def __init__(self, nc, lo, hi):
        self.nc = nc
        self.lo = lo
        self.hi = hi
        self.cur = lo
        self.live = []
        self.dead = []
        self.n = 0

    def alloc(self, name, shape, dtype):
        esz = 2 if dtype == BF16 else 4
        nbytes = int(np.prod(shape[1:])) * esz
        off = (self.cur + 63) // 64 * 64
        assert off + nbytes <= self.hi, f"SBUF overflow at {name}: {off + nbytes} > {self.hi}"
        self.cur = off + nbytes
        self.n += 1
        h = self.nc.alloc_sbuf_tensor_at(f"{name}_{self.n}", list(shape), dtype, offset=off)
        rec = (off, off + nbytes, [])
        self.live.append(rec)
        return h, rec

    def newT(self, rec, name):
        lo, hi, lst = rec
        after = []
        for (dlo, dhi, dl) in self.dead:
            if dlo < hi and lo < dhi:
                after += dl
        t = T(name, after if after else None)
        lst.append(t)
        return t

    def mark(self):
        return (self.cur, len(self.live))

    def reset(self, m):
        cur, n = m
        self.dead += self.live[n:]
        del self.live[n:]
        self.cur = cur


def build_program(debug=False):
    nc = bass.Bass("TRN2", target_bir_lowering=False)

    def din(name, shape, dt=F32):
        return nc.dram_tensor(name, list(shape), dt, kind="ExternalInput").ap()

    def dout(name, shape, dt=F32):
        return nc.dram_tensor(name, list(shape), dt, kind="ExternalOutput").ap()

    def dscr(name, shape, dt=F32):
        return nc.dram_tensor(name, list(shape), dt, kind="ExternalOutput" if debug else "Internal").ap()

    xin = din("xin", [NT, D])
    w = {}
    for nm, shp in [("ffn1_w_gate", [D, DFF]), ("ffn1_w_up", [D, DFF]), ("ffn1_w_down", [DFF, D]),
                    ("ffn2_w_gate", [D, DFF]), ("ffn2_w_up", [D, DFF]), ("ffn2_w_down", [DFF, D]),
                    ("w_in", [D, DINP]), ("w_uq", [512, 1536]), ("w_ukv", [512, 2048]), ("w_out", [2048, D])]:
        w[nm] = din(nm, shp)
    lnp = {nm: din(nm, [128, D]) for nm in ["ln1_g", "ln1_b", "ln2_g", "ln2_b", "ln3_g", "ln3_b"]}
    ident_d = din("ident", [128, 128])

    cwp_d = din("cwp", [128, 12, 4])
    cbp_d = din("cbp", [128, 12])
    dtb_d = din("dtb", [128, 16])
    alog_d = din("alog", [128, 16])
    dsk_d = din("dsk", [128, 16])
    gss_d = din("gss", [128, D])
    csts_d = din("csts", [128, 8, 128])
    nmp_d = din("nmp", [128, 512])
    nms_d = din("nms", [128, 256])
    idl_d = din("idl", [128, 2, 64])
    nmlp_d = din("nmlp", [128, 256])
    nmls_d = din("nmls", [128, 128])
    sconv_d = din("sconv", [2, 3, 1536])
    sssm_d = din("sssm", [2, 16, 64, 128])

    gq_d = din("gq", [128, 512])
    gkv_d = din("gkv", [128, 512])
    clat_d = din("clat", [2, PAST, 512])
    ckr_d = din("ckr", [2, PAST, 64])
    cs4_d = din("cs4", [NT, 128])
    cosT_d = din("cosT", [64, NT])
    sinT_d = din("sinT", [64, NT])
    maskq_d = din("maskq", [128, 128])

    y_out = dout("y", [NT, D])
    lat_o = dout("lat_o", [NT, 512])
    kr_o = dout("kr_o", [NT, 64])
    x2d = dscr("x2d", [NT, D])
    conv_o = dout("conv_o", [3, 3, 1536])
    ssm_o = dout("ssm_o", [3, 16, 64, 128])
    x1d = dscr("x1d", [NT, D])
    r1d = dscr("r1d", [NT, D])
    dbg_y = dout("dbg_y", [NT, D]) if debug else None
    dbg_yn = dout("dbg_yn", [NT, D], BF16) if debug else None

    P = Prog(nc)
    A = Arena(nc, 16512, 229344)

    psA = nc.alloc_psum_tensor("psA", [128, 2048], F32)
    psB = nc.alloc_psum_tensor("psB", [128, 2048], F32)

    def bank(i):
        t = psA if i < 4 else psB
        j = i % 4
        return t[:, j * 512:(j + 1) * 512]

    bankT = [T(f"bank{i}", excl=True) for i in range(8)]

    ident_f, r = A.alloc("ident_f", [128, 128], F32)
    t_identf = A.newT(r, "ident_f")
    ident_b, r = A.alloc("ident_b", [128, 128], BF16)
    t_identb = A.newT(r, "ident_b")
    P.dma("sp", lambda e: e.dma_start(out=ident_f[:], in_=ident_d[:, :]), writes=[t_identf])
    P.dma("pool", lambda e: e.dma_start(out=ident_b[:], in_=ident_d[:, :]), writes=[t_identb])

    xT, r = A.alloc("xT", [128, 8, NT], BF16)
    xT_rec = r
    xT_T = [A.newT(r, f"xT{s}") for s in range(17)]

    m0 = A.mark()
    xb = []
    for i in range(2):
        h, r = A.alloc(f"xb{i}", [128, D], BF16)
        xb.append((h, A.newT(r, f"xb{i}")))
    for s, (t0, m) in enumerate(SUBT):
        h, th = xb[s % 2]
        P.dma("pool", lambda e, h=h, t0=t0, m=m: e.dma_start(out=h[0:m, :], in_=xin[t0:t0 + m, :]), writes=[th])
        bk = 6 + (s % 2)
        pv = bank(bk).bitcast(BF16)

        def tr(e, h=h, m=m, pv=pv):
            for k in range(8):
                i = e.transpose(out=pv[:, k * 128:k * 128 + m], in_=h[0:m, k * 128:(k + 1) * 128], identity=ident_b[0:m, 0:m])
            return i
        P.op("pe", tr, reads=[th, t_identb], writes=[bankT[bk]])
        src = pv.rearrange("p (k t) -> p k t", k=8)[:, :, 0:m]
        if s % 2 == 0:
            P.op("act", lambda e, src=src, t0=t0, m=m: e.activation(out=xT[:, :, t0:t0 + m], in_=src, func=AF.Copy),
                 reads=[bankT[bk]], writes=[xT_T[s]])
        else:
            P.op("dve", lambda e, src=src, t0=t0, m=m: e.tensor_copy(out=xT[:, :, t0:t0 + m], in_=src),
                 reads=[bankT[bk]], writes=[xT_T[s]])
    A.reset(m0)


    def interleave_g(gens):
        gens = [g for g in gens if g is not None]
        while gens:
            for g in list(gens):
                try:
                    next(g)
                except StopIteration:
                    gens.remove(g)

    def ln_tail(s, t0, m, yv, t_yv, bufs, lng, t_lng, lnb, t_lnb, out_d, out_T, make_xT, extra_w=()):
        xo, t_xo = bufs["xo"][s % 2]
        xob, t_xob = bufs["xob"][s % 2]
        st, t_st = bufs["st"][s % 2]
        mv, t_mv = bufs["mv"][s % 2]
        rs, t_rs = bufs["rs"][s % 2]
        for half in range(2):
            P.op("dve", lambda e, half=half: e.bn_stats(out=st[0:m, half * 6:(half + 1) * 6], in_=yv[0:m, half * 512:(half + 1) * 512]),
                 reads=[t_yv], writes=[t_st])
        P.op("dve", lambda e: e.bn_aggr(out=mv[0:m, :], in_=st[0:m, :]), reads=[t_st], writes=[t_mv])
        yield
        P.op("act", lambda e: e.activation(out=rs[0:m, 0:1], in_=mv[0:m, 1:2], func=AF.Sqrt, bias=eps_t[0:m, :]),
             reads=[t_mv, t_eps], writes=[t_rs])
        yield
        P.op("dve", lambda e: e.reciprocal(out=rs[0:m, 1:2], in_=rs[0:m, 0:1]), reads=[t_rs], writes=[t_rs])
        P.op("dve", lambda e: e.tensor_scalar(out=yv[0:m, :], in0=yv[0:m, :], scalar1=mv[0:m, 0:1], scalar2=rs[0:m, 1:2],
                                              op0=ALU.subtract, op1=ALU.mult), reads=[t_yv, t_mv, t_rs], writes=[t_yv])
        P.op("dve", lambda e: e.tensor_tensor(out=yv[0:m, :], in0=yv[0:m, :], in1=lng[0:m, :], op=ALU.mult), reads=[t_yv, t_lng], writes=[t_yv])
        P.op("dve", lambda e: e.tensor_tensor(out=xo[0:m, :], in0=yv[0:m, :], in1=lnb[0:m, :], op=ALU.add), reads=[t_yv, t_lnb], writes=[t_xo])
        yield
        P.dma("sp", lambda e: e.dma_start(out=out_d[t0:t0 + m, :], in_=xo[0:m, :]), reads=[t_xo], writes=([out_T[s]] if out_T else []))
        if make_xT:
            P.op("act", lambda e: e.activation(out=xob[0:m, :], in_=xo[0:m, :], func=AF.Copy), reads=[t_xo], writes=[t_xob])
            bk = 6 + (s % 2)
            pv = bank(bk).bitcast(BF16)

            def tr(e):
                for k in range(8):
                    i = e.transpose(out=pv[:, k * 128:k * 128 + m], in_=xob[0:m, k * 128:(k + 1) * 128], identity=ident_b[0:m, 0:m])
                return i
            yield
            P.op("pe", tr, reads=[t_xob, t_identb], writes=[bankT[bk]])
            yield
            src = pv.rearrange("p (k t) -> p k t", k=8)[:, :, 0:m]
            P.op("act", lambda e: e.activation(out=xT[:, :, t0:t0 + m], in_=src, func=AF.Copy),
                 reads=[bankT[bk]], writes=[xT_T[s]] + list(extra_w))
        yield

    def ffn_phase(tag, wg_d, wu_d, wd_d, g_d, b_d, res_d, res_T, out_d, out_T, make_xT):
        mk = A.mark()
        lng, r = A.alloc("lng", [128, D], F32)
        t_lng = A.newT(r, "lng")
        lnb, r = A.alloc("lnb", [128, D], F32)
        t_lnb = A.newT(r, "lnb")
        P.dma("sp", lambda e: e.dma_start(out=lng[:], in_=g_d[:, :]), writes=[t_lng])
        P.dma("sp", lambda e: e.dma_start(out=lnb[:], in_=b_d[:, :]), writes=[t_lnb])
        hT, r = A.alloc("hT", [128, NFF, NT], BF16)
        hT_T = [[A.newT(r, f"hT{c}_{t}") for t in range(5)] for c in range(NFF)]
        wd, r = A.alloc("wd", [128, NFF, D], BF16)
        wd_T = [A.newT(r, f"wd{c}") for c in range(NFF)]
        mk2 = A.mark()
        wgb, wub, sgb = [], [], []
        for i in range(2):
            h, r = A.alloc(f"wg{i}", [128, 8, 256], BF16)
            wgb.append((h, A.newT(r, f"wg{i}")))
            h, r = A.alloc(f"wu{i}", [128, 8, 256], BF16)
            wub.append((h, A.newT(r, f"wu{i}")))
        for i in range(2):
            h, r = A.alloc(f"sg{i}", [128, 512], F32)
            sgb.append((h, A.newT(r, f"sg{i}")))
        wg_v = wg_d.rearrange("(ko ki) f -> ki ko f", ki=128)
        wu_v = wu_d.rearrange("(ko ki) f -> ki ko f", ki=128)
        wd_v = wd_d.rearrange("(c p) d -> p c d", p=128)

        def load_block(b):
            hg, tg = wgb[b % 2]
            hu, tu = wub[b % 2]
            P.dma("pool", lambda e: e.dma_start(out=hg[:], in_=wg_v[:, :, b * 256:(b + 1) * 256]), writes=[tg])
            P.dma("pool", lambda e: e.dma_start(out=hu[:], in_=wu_v[:, :, b * 256:(b + 1) * 256]), writes=[tu])

        load_block(0)
        it = 0
        for b in range(NFF // 2):
            if b + 1 < NFF // 2:
                load_block(b + 1)
            for c in (2 * b, 2 * b + 1):
                P.dma("pool", lambda e, c=c: e.dma_start(out=wd[:, c, :], in_=wd_v[:, c, :]), writes=[wd_T[c]])
            hg, tg = wgb[b % 2]
            hu, tu = wub[b % 2]
            for hc in range(2):
                c = 2 * b + hc
                for t, (t0, n) in enumerate(TILES):
                    bg, bu = (it % 2), 2 + (it % 2)
                    sg, tsg = sgb[it % 2]
                    it += 1
                    xr = [xT_T[s] for s in SUB_OF_TILE[t]]

                    def mm(e, wt, bk, t0=t0, n=n, hc=hc):
                        for k in range(8):
                            i = e.matmul(bank(bk)[:, 0:n], lhsT=wt[:, k, hc * 128:(hc + 1) * 128], rhs=xT[:, k, t0:t0 + n],
                                         start=(k == 0), stop=(k == 7))
                        return i
                    P.op("pe", lambda e, mm=mm, hg=hg, bg=bg: mm(e, hg, bg), reads=xr + [tg], writes=[bankT[bg]])
                    P.op("pe", lambda e, mm=mm, hu=hu, bu=bu: mm(e, hu, bu), reads=xr + [tu], writes=[bankT[bu]])
                    P.op("act", lambda e, sg=sg, bg=bg, n=n: e.activation(out=sg[:, 0:n], in_=bank(bg)[:, 0:n], func=AF.Silu),
                         reads=[bankT[bg]], writes=[tsg])
                    P.op("dve", lambda e, sg=sg, bu=bu, n=n, c=c, t0=t0: e.tensor_tensor(
                        out=hT[:, c, t0:t0 + n], in0=sg[:, 0:n], in1=bank(bu)[:, 0:n], op=ALU.mult),
                        reads=[tsg, bankT[bu]], writes=[hT_T[c][t]])
        A.reset(mk2)
        bufs = {}
        for nm, shape, dt, nb in [("xres", [128, D], F32, 2), ("yv", [128, D], F32, 2),
                                  ("xo", [128, D], F32, 2), ("xob", [128, D], BF16, 2),
                                  ("st", [128, 12], F32, 2), ("mv", [128, 2], F32, 2), ("rs", [128, 2], F32, 2)]:
            bufs[nm] = []
            for i in range(nb):
                h, r = A.alloc(f"{nm}{i}", shape, dt)
                bufs[nm].append((h, A.newT(r, f"{nm}{i}")))
        def down(s):
            t0, m = SUBT[s]
            t = min(s // 4, 4)
            xres, t_xres = bufs["xres"][s % 2]
            xa, t_xa = xres, t_xres
            yv, t_yv = bufs["yv"][s % 2]
            P.dma("sp", lambda e: e.dma_start(out=xres[0:m, :], in_=res_d[t0:t0 + m, :]),
                  reads=([res_T[s]] if res_T else []), writes=[t_xres])
            P.op("act", lambda e: e.activation(out=xa[0:m, :], in_=xres[0:m, :], func=AF.Copy, scale=ALPHA), reads=[t_xres], writes=[t_xa])
            for half in range(2):
                bk = 4 * (s % 2) + half

                def mmd(e, bk=bk, half=half):
                    for c in range(NFF):
                        i = e.matmul(bank(bk)[0:m, :], lhsT=hT[:, c, t0:t0 + m], rhs=wd[:, c, half * 512:(half + 1) * 512],
                                     start=(c == 0), stop=(c == NFF - 1))
                    return i
                P.op("pe", mmd, reads=[hT_T[c][t] for c in range(NFF)] + wd_T, writes=[bankT[bk]])
                yield
                P.op("dve", lambda e, bk=bk, half=half: e.scalar_tensor_tensor(
                    out=yv[0:m, half * 512:(half + 1) * 512], in0=bank(bk)[0:m, :], scalar=0.5,
                    in1=xa[0:m, half * 512:(half + 1) * 512], op0=ALU.mult, op1=ALU.add),
                    reads=[bankT[bk], t_xa], writes=[t_yv])
                yield

        def tail(s):
            t0, m = SUBT[s]
            yv, t_yv = bufs["yv"][s % 2]
            return ln_tail(s, t0, m, yv, t_yv, bufs, lng, t_lng, lnb, t_lnb, out_d, out_T, make_xT)

        interleave_g([down(0)])
        for s in range(17):
            interleave_g([down(s + 1) if s + 1 < 17 else None, tail(s)])
        A.reset(mk)

    def ncdma(e, out, in_, k):
        with nc.allow_non_contiguous_dma(reason="small strided state transfer"):
            return e.dma_start(out=out[:, :, k:k + 1], in_=in_[:, :, k:k + 1])

    def al(name, shape, dt=F32):
        h, r = A.alloc(name, shape, dt)
        return h, A.newT(r, name)

    def ssd_phase():
        mk = A.mark()
        w_in_v = w["w_in"].rearrange("(ko ki) f -> ki ko f", ki=128)
        w_out_v = w["w_out"].rearrange("(ko ki) f -> ki ko f", ki=128)
        wz, r = A.alloc("wz", [128, 8, 1024], BF16)
        t_wz = [A.newT(r, f"wz{j}") for j in range(2)]
        wx, r = A.alloc("wx", [128, 8, 1536], BF16)
        t_wx = [A.newT(r, f"wx{j}") for j in range(3)]
        wdt, t_wdt = al("wdt", [128, 8, 16], BF16)
        wot, r = A.alloc("wot", [128, 8, 1024], BF16)
        t_wot = [A.newT(r, f"wot{j}") for j in range(2)]
        for j in range(3):
            P.dma("pool", lambda e, j=j: e.dma_start(out=wx[:, :, j * 512:(j + 1) * 512], in_=w_in_v[:, :, 1024 + j * 512:1024 + (j + 1) * 512]), writes=[t_wx[j]])
        for j in range(2):
            P.dma("pool", lambda e, j=j: e.dma_start(out=wz[:, :, j * 512:(j + 1) * 512], in_=w_in_v[:, :, j * 512:(j + 1) * 512]), writes=[t_wz[j]])
        P.dma("pool", lambda e: e.dma_start(out=wdt[:], in_=w_in_v[:, :, 2560:2576]), writes=[t_wdt])
        for j in range(2):
            P.dma("pool", lambda e, j=j: e.dma_start(out=wot[:, :, j * 512:(j + 1) * 512], in_=w_out_v[:, 0:8, j * 512:(j + 1) * 512]), writes=[t_wot[j]])
        cw, t_cw = al("cw", [128, 12, 4])
        cb, t_cb = al("cb", [128, 12])
        dtb, t_dtb = al("dtb", [128, 16])
        Abc, t_Abc = al("Abc", [128, 16])
        dsk, t_dsk = al("dsk", [128, 16])
        gss, t_gss = al("gss", [128, D])
        cst, t_cst = al("cst", [128, 8, 128])
        ones_f, t_ones = al("ones_f", [128, 128])
        one_t, t_one = al("one_t", [128, 1])
        nmp, t_nmp = al("nmp", [128, 512], BF16)
        nms, t_nms = al("nms", [128, 256], BF16)
        idl, t_idl = al("idl", [128, 2, 64])
        nmlp, t_nmlp = al("nmlp", [128, 256], BF16)
        nmls, t_nmls = al("nmls", [128, 128], BF16)
        P.dma("sp", lambda e: e.dma_start(out=idl[:], in_=idl_d), writes=[t_idl])
        P.dma("pool", lambda e: e.dma_start(out=nmlp[:], in_=nmlp_d[:, :]), writes=[t_nmlp])
        P.dma("pool", lambda e: e.dma_start(out=nmls[:], in_=nmls_d[:, :]), writes=[t_nmls])
        for (h, th, d_) in [(cw, t_cw, cwp_d), (cb, t_cb, cbp_d), (dtb, t_dtb, dtb_d), (Abc, t_Abc, alog_d), (dsk, t_dsk, dsk_d),
                            (gss, t_gss, gss_d), (cst, t_cst, csts_d)]:
            P.dma("sp", lambda e, h=h, d_=d_: e.dma_start(out=h[:], in_=d_), writes=[th])
        P.dma("pool", lambda e: e.dma_start(out=nmp[:], in_=nmp_d[:, :]), writes=[t_nmp])
        P.dma("pool", lambda e: e.dma_start(out=nms[:], in_=nms_d[:, :]), writes=[t_nms])
        P.op("pool", lambda e: e.memset(ones_f[:], 1.0), writes=[t_ones])
        P.op("pool", lambda e: e.memset(one_t[:], 1.0), writes=[t_one])
        P.op("act", lambda e: e.activation(out=Abc[:], in_=Abc[:], func=AF.Exp), reads=[t_Abc], writes=[t_Abc])
        P.op("dve", lambda e: e.tensor_scalar(out=Abc[:], in0=Abc[:], scalar1=-1.0, scalar2=None, op0=ALU.mult), reads=[t_Abc], writes=[t_Abc])
        S, t_S = al("S", [128, D])
        Sb, t_Sb = al("Sb", [128, D], BF16)
        P.op("pool", lambda e: e.memset(S[:], 0.0), writes=[t_S])
        P.op("pool", lambda e: e.memset(Sb[:], 0.0), writes=[t_Sb])
        Ss = [al(f"Ss{j}", [128, D]) for j in range(2)]
        Sbs = [al(f"Sbs{j}", [128, D], BF16) for j in range(2)]
        sio, t_sio = al("sio", [128, 8, 128])
        ssm_v = lambda ap: ap.rearrange("(b h2) p n -> (h2 p) b n", h2=2)

        def state_in(j):
            P.dma("sp", lambda e: e.dma_start(out=sio[:], in_=ssm_v(sssm_d[j])), writes=[t_sio])

            def tr(e):
                for b in range(8):
                    i = e.transpose(out=bank(4 + b // 4)[:, (b % 4) * 128:(b % 4 + 1) * 128], in_=sio[:, b, :], identity=ident_f[:])
                return i
            P.op("pe", tr, reads=[t_sio, t_identf], writes=[bankT[4], bankT[5]])
            P.op("act", lambda e: e.activation(out=Ss[j][0][:], in_=psB[:, 0:1024], func=AF.Copy), reads=[bankT[4], bankT[5]], writes=[Ss[j][1]])
            P.op("dve", lambda e: e.tensor_copy(out=Sbs[j][0][:], in_=psB[:, 0:1024]), reads=[bankT[4], bankT[5]], writes=[Sbs[j][1]])

        def state_out(Sx, t_Sx, idx):
            def tr(e):
                for b in range(8):
                    i = e.transpose(out=bank(4 + b // 4)[:, (b % 4) * 128:(b % 4 + 1) * 128], in_=Sx[:, b * 128:(b + 1) * 128], identity=ident_f[:])
                return i
            P.op("pe", tr, reads=[t_Sx, t_identf], writes=[bankT[4], bankT[5]])
            P.op("act", lambda e: e.activation(out=sio[:], in_=psB[:, 0:1024].rearrange("p (b n) -> p b n", b=8), func=AF.Copy),
                 reads=[bankT[4], bankT[5]], writes=[t_sio])
            P.dma("sp", lambda e: e.dma_start(out=ssm_v(ssm_o[idx]), in_=sio[:]), reads=[t_sio])

        state_in(0)
        state_in(1)
        xbcb = []
        for i in range(2):
            h, r = A.alloc(f"xbc{i}", [128, 12, 131], F32)
            xbcb.append((h, A.newT(r, f"xh{i}"), [A.newT(r, f"xq{i}_{q}") for q in range(3)]))
        acc, t_acc = al("acc", [128, 12, 128])
        t_accc = [T(f"acc{ch}") for ch in range(12)]
        bcb2 = [al(f"bcb{i}", [128, 4, 128], BF16) for i in range(2)]
        Btok2 = [al(f"Btok{i}", [128, 2, 128], BF16) for i in range(2)]
        sm = {nm: al("sm_" + nm, [128, 16]) for nm in ["dtv", "ab", "ll", "dt", "dA", "acs", "dd", "decst", "dtd", "lndt", "acsl"]}
        indec2 = [al(f"indec{i}", [128, 16]) for i in range(2)]
        cd2 = [al(f"cd{i}", [128, 2, 16]) for i in range(2)]
        x_dt, t_xdt = al("x_dt", [128, D], BF16)
        xd2 = [al(f"xd{i}", [128, D], BF16) for i in range(2)]
        xs_sb, t_xs = al("xs_sb", [128, D])
        xsD, t_xsD = al("xsD", [128, D], BF16)
        dg, t_dg = al("dg", [128, 16, 128])
        t_dgq = [T(f"dgq{q}", after=[t_dg]) for q in range(4)]
        Mm, t_M = al("Mm", [128, 16, 128], BF16)
        ysb, t_y = al("ysb", [128, D])
        sz2 = [al(f"sz{i}", [128, D]) for i in range(2)]
        ssq, t_ssq = al("ssq", [128, 4])
        yn, t_yn = al("yn", [128, D], BF16)
        ynT, t_ynT = al("ynT", [128, 8, 128], BF16)
        r1s, t_r1s = ysb, t_y
        junk = yn[:].bitcast(F32)
        t_junk = t_yn
        conv_v = lambda ap: ap.rearrange("k (c p) -> p c k", p=128)
        g_ = lambda nm: sm[nm][0]
        t_ = lambda nm: sm[nm][1]
        pv2 = bank(2).bitcast(BF16)
        pv6 = bank(6).bitcast(BF16)

        def prepA(s):
            t0, m = SUBT[s]
            p = s % 2
            samp = (s == 16)
            co = 4 if samp else 0
            CH = 32 if samp else 64
            vi = 1 if samp else 0
            xbc, t_xh, t_xq = xbcb[p]
            pxbc, t_pxh, t_pxq = xbcb[1 - p]
            bcb, t_bcb = bcb2[p]
            Btok, t_Btok = Btok2[p]
            indec, t_indec = indec2[p]
            cd, t_cd = cd2[p]
            xd, t_xd = xd2[p]
            sz, t_sz = sz2[p]
            xr = [xT_T[s]]
            segs = [(0, 32, 0), (35, 32, 32)] if samp else [(0, 128, 0)]
            if samp:
                for j in range(2):
                    for k in range(3):
                        P.dma("sp", lambda e, j=j, k=k: ncdma(e, xbc[:, :, j * 35:j * 35 + 3], conv_v(sconv_d[j]), k), writes=[t_xh])
            elif s == 0:
                P.op("pool", lambda e: e.memset(xbc[:, :, 0:3], 0.0), writes=[t_xh])
            else:
                P.op("pool", lambda e: e.tensor_copy(out=xbc[:, :, 0:3], in_=pxbc[:, :, 128:131]), reads=[t_pxq[2], t_pxq[1], t_pxq[0]], writes=[t_xh])
            yield
            for q in range(3):
                bq = 2 + (q % 2)

                def mmx(e, q=q, bq=bq):
                    for j in range(4):
                        ch = 4 * q + j
                        for k in range(8):
                            i = e.matmul(bank(bq)[:, j * 128:j * 128 + m], lhsT=wx[:, k, ch * 128:(ch + 1) * 128], rhs=xT[:, k, t0:t0 + m],
                                         start=(k == 0), stop=(k == 7), skip_group_check=True)
                    return i
                P.op("pe", mmx, reads=xr + t_wx, writes=[bankT[bq]])
                yield
                bv = bank(bq).rearrange("p (j t) -> p j t", j=4)
                if samp:
                    for j in range(2):
                        P.op("act", lambda e, q=q, j=j, bv=bv: e.activation(
                            out=xbc[:, 4 * q:4 * q + 4, j * 35 + 3:j * 35 + 35], in_=bv[:, :, j * 32:(j + 1) * 32], func=AF.Copy),
                            reads=[bankT[bq]], writes=[t_xq[q]])
                else:
                    P.op("act", lambda e, q=q, bv=bv: e.activation(out=xbc[:, 4 * q:4 * q + 4, 3:131], in_=bv[:, :, :], func=AF.Copy),
                         reads=[bankT[bq]], writes=[t_xq[q]])
                yield
            if s == 15:
                for k in range(3):
                    P.dma("sp", lambda e, k=k: ncdma(e, conv_v(conv_o[0]), xbc[:, :, 128:131], k), reads=t_xq)
            if samp:
                for j in range(2):
                    for k in range(3):
                        P.dma("sp", lambda e, j=j, k=k: ncdma(e, conv_v(conv_o[1 + j]), xbc[:, :, j * 35 + 32:j * 35 + 35], k), reads=t_xq)
            def mmdt(e):
                for k in range(8):
                    i = e.matmul(bank(3)[0:m, 0:16], lhsT=xT[:, k, t0:t0 + m], rhs=wdt[:, k, :], start=(k == 0), stop=(k == 7))
                return i
            P.op("pe", mmdt, reads=xr + [t_wdt], writes=[bankT[3]])
            yield
            P.op("dve", lambda e: e.tensor_tensor(out=g_("dtv")[0:m, :], in0=bank(3)[0:m, 0:16], in1=dtb[0:m, :], op=ALU.add),
                 reads=[bankT[3], t_dtb], writes=[t_("dtv")])
            yield
            P.op("act", lambda e: e.activation(out=g_("ab")[0:m, :], in_=g_("dtv")[0:m, :], func=AF.Abs), reads=[t_("dtv")], writes=[t_("ab")])
            P.op("act", lambda e: e.activation(out=g_("ab")[0:m, :], in_=g_("ab")[0:m, :], func=AF.Exp, scale=-1.0), reads=[t_("ab")], writes=[t_("ab")])
            P.op("act", lambda e: e.activation(out=g_("ll")[0:m, :], in_=g_("ab")[0:m, :], func=AF.Ln, bias=one_t[0:m, :]),
                 reads=[t_("ab"), t_one], writes=[t_("ll")])
            yield
            def conv_op(k, ch):
                if True:
                    for (c0, n, o0) in segs:
                        if k == 0:
                            P.op("dve", lambda e, ch=ch, c0=c0, n=n, o0=o0: e.tensor_scalar(
                                out=acc[:, ch, o0:o0 + n], in0=xbc[:, ch, c0:c0 + n], scalar1=cw[:, ch, 0:1], scalar2=cb[:, ch:ch + 1],
                                op0=ALU.mult, op1=ALU.add), reads=[t_xh, t_xq[ch // 4], t_cw, t_cb], writes=[t_accc[ch]])
                        else:
                            P.op("dve", lambda e, ch=ch, k=k, c0=c0, n=n, o0=o0: e.scalar_tensor_tensor(
                                out=acc[:, ch, o0:o0 + n], in0=xbc[:, ch, c0 + k:c0 + k + n], scalar=cw[:, ch, k:k + 1], in1=acc[:, ch, o0:o0 + n],
                                op0=ALU.mult, op1=ALU.add), reads=[t_xh, t_xq[ch // 4], t_cw, t_accc[ch]], writes=[t_accc[ch]])
            for k in range(2):
                for ch in range(12):
                    conv_op(k, ch)
                    if ch % 4 == 3:
                        yield
            if True:
                if True:
                    P.op("dve", lambda e: e.scalar_tensor_tensor(out=g_("dt")[0:m, :], in0=g_("dtv")[0:m, :], scalar=0.0, in1=g_("ll")[0:m, :],
                                                                 op0=ALU.max, op1=ALU.add), reads=[t_("dtv"), t_("ll")], writes=[t_("dt")])
                    P.op("dve", lambda e: e.tensor_tensor(out=g_("dA")[0:m, :], in0=g_("dt")[0:m, :], in1=Abc[0:m, :], op=ALU.mult),
                         reads=[t_("dt"), t_Abc], writes=[t_("dA")])
                    yield
                if True:
                    def mmacs(e):
                        e.matmul(bank(3)[0:m, 0:16], lhsT=cst[0:m, co + 0, 0:m], rhs=g_("dA")[0:m, :], start=True, stop=True, skip_group_check=True)
                        e.matmul(bank(3)[0:m, 16:32], lhsT=cst[0:m, co + 1, 0:m], rhs=g_("dA")[0:m, :], start=True, stop=True, skip_group_check=True)
                        e.matmul(bank(3)[:, 32:48], lhsT=cst[0:m, co + 2, :], rhs=g_("dA")[0:m, :], start=True, stop=True, skip_group_check=True)
                        return e.matmul(bank(3)[:, 48:64], lhsT=cst[0:m, co + 3, :], rhs=g_("dA")[0:m, :], start=True, stop=True, skip_group_check=True)
                    P.op("pe", mmacs, reads=[t_("dA"), t_cst], writes=[bankT[3]])
                    yield
                    P.op("act", lambda e: e.activation(out=g_("acs")[0:m, :], in_=bank(3)[0:m, 0:16], func=AF.Copy), reads=[bankT[3]], writes=[t_("acs")])
                    P.op("act", lambda e: e.activation(out=cd[:, :, :], in_=bank(3)[:, 32:64].rearrange("p (c h) -> p c h", c=2), func=AF.Exp),
                         reads=[bankT[3]], writes=[t_cd])
                    yield
                    P.op("dve", lambda e: e.tensor_tensor(out=g_("dd")[0:m, :], in0=bank(3)[0:m, 16:32], in1=g_("acs")[0:m, :], op=ALU.subtract),
                         reads=[bankT[3], t_("acs")], writes=[t_("dd")])
                    yield
                    P.op("act", lambda e: e.activation(out=g_("lndt")[0:m, :], in_=g_("dt")[0:m, :], func=AF.Ln), reads=[t_("dt")], writes=[t_("lndt")])
                    P.op("act", lambda e: e.activation(out=indec[0:m, :], in_=g_("acs")[0:m, :], func=AF.Exp), reads=[t_("acs")], writes=[t_indec])
                    P.op("act", lambda e: e.activation(out=g_("decst")[0:m, :], in_=g_("dd")[0:m, :], func=AF.Exp), reads=[t_("dd")], writes=[t_("decst")])
                    yield
                    P.op("dve", lambda e: e.tensor_tensor(out=g_("dtd")[0:m, :], in0=g_("dt")[0:m, :], in1=g_("decst")[0:m, :], op=ALU.mult),
                         reads=[t_("dt"), t_("decst")], writes=[t_("dtd")])
                    P.op("dve", lambda e: e.tensor_tensor(out=g_("acsl")[0:m, :], in0=g_("acs")[0:m, :], in1=g_("lndt")[0:m, :], op=ALU.subtract),
                         reads=[t_("acs"), t_("lndt")], writes=[t_("acsl")])
                    P.op("dve", lambda e: e.tensor_tensor(
                        out=dg[0:m, :, 0:CH], in0=idl[0:m, vi, 0:CH].unsqueeze(1).to_broadcast([m, 16, CH]),
                        in1=g_("acs")[0:m, :].unsqueeze(2).to_broadcast([m, 16, CH]), op=ALU.mult), reads=[t_idl, t_("acs")], writes=t_dgq)
                    yield
            nm_ = nmls if samp else nmlp
            t_nm = t_nmls if samp else t_nmlp
            rest_conv = [(k, ch) for k in (2, 3) for ch in range(12)]
            for q in range(4):
                bq = q % 2

                def mmR(e, q=q, bq=bq):
                    e.matmul(bank(bq)[0:m, 0:4 * CH], lhsT=cst[0:m, co + 1, 0:m], rhs=dg[0:m, 4 * q:4 * q + 4, 0:CH], start=True, stop=False)
                    return e.matmul(bank(bq)[0:m, 0:4 * CH], lhsT=ident_b[0:m, 0:m], rhs=nm_[0:m, 0:4 * CH], start=False, stop=True)
                P.op("pe", mmR, reads=[t_dgq[q], t_cst, t_identb, t_nm], writes=[bankT[bq]])
                yield
                for (k, ch) in rest_conv[6 * q:6 * q + 6]:
                    conv_op(k, ch)
                yield
                Rv = bank(bq)[0:m, 0:4 * CH].rearrange("p (h i) -> p h i", h=4)
                P.op("dve", lambda e, q=q, Rv=Rv: e.tensor_tensor(
                    out=dg[0:m, 4 * q:4 * q + 4, 0:CH], in0=Rv, in1=g_("acsl")[0:m, 4 * q:4 * q + 4].unsqueeze(2).to_broadcast([m, 4, CH]), op=ALU.subtract),
                    reads=[bankT[bq], t_("acsl")], writes=[t_dgq[q]])
                yield
            P.op("act", lambda e: e.activation(out=dg[0:m, :, 0:CH], in_=dg[0:m, :, 0:CH], func=AF.Exp), reads=t_dgq, writes=t_dgq)
            yield
            for half in range(2):
                def mmz(e, half=half):
                    for k in range(8):
                        i = e.matmul(bank(half)[0:m, :], lhsT=xT[:, k, t0:t0 + m], rhs=wz[:, k, half * 512:(half + 1) * 512], start=(k == 0), stop=(k == 7))
                    return i
                P.op("pe", mmz, reads=xr + [t_wz[half]], writes=[bankT[half]])
            yield
            P.op("act", lambda e: e.activation(out=sz[0:m, :], in_=psA[0:m, 0:1024], func=AF.Silu), reads=[bankT[0], bankT[1]], writes=[t_sz])
            P.op("act", lambda e: e.activation(out=bcb[:, :, 0:m], in_=acc[:, 8:12, 0:m], func=AF.Silu), reads=t_accc[8:12], writes=[t_bcb])
            P.op("act", lambda e: e.activation(out=acc[:, 0:8, 0:m], in_=acc[:, 0:8, 0:m], func=AF.Silu), reads=t_accc[0:8], writes=[t_acc] + t_accc[0:8])
            yield
            def trb(e):
                for g in range(2):
                    i = e.transpose(out=pv2[0:m, g * 128:(g + 1) * 128], in_=bcb[:, g, 0:m], identity=ident_b[:])
                return i
            P.op("pe", trb, reads=[t_bcb, t_identb], writes=[bankT[2]])

            def mmcb(e):
                for c in range(2):
                    for g in range(2):
                        i = e.matmul(bank(3)[c * CH:(c + 1) * CH, 128 + g * 64:128 + g * 64 + CH], lhsT=bcb[:, g, c * CH:(c + 1) * CH],
                                     rhs=bcb[:, 2 + g, c * CH:(c + 1) * CH], start=True, stop=True, skip_group_check=True)
                return i
            P.op("pe", mmcb, reads=[t_bcb], writes=[bankT[3]])

            def trx(e):
                for cc in range(8):
                    i = e.transpose(out=bank(cc // 4)[0:m, (cc % 4) * 128:(cc % 4 + 1) * 128], in_=acc[:, cc, 0:m], identity=ident_f[:])
                return i
            P.op("pe", trx, reads=[t_acc, t_identf], writes=[bankT[0], bankT[1]])
            yield
            P.op("act", lambda e: e.activation(out=Btok[0:m, :, :], in_=pv2[0:m, 0:256].rearrange("p (g n) -> p g n", g=2), func=AF.Copy),
                 reads=[bankT[2]], writes=[t_Btok])
            P.op("act", lambda e: e.activation(out=x_dt[0:m, :], in_=psA[0:m, 0:1024], func=AF.Copy), reads=[bankT[0], bankT[1]], writes=[t_xdt])
            yield
            for g in range(2):
                bvx = bank(g)[0:m, :].rearrange("p (h d) -> p h d", h=8)
                P.op("dve", lambda e, g=g, bvx=bvx: e.tensor_tensor(
                    out=xd[0:m, g * 512:(g + 1) * 512].rearrange("p (h d) -> p h d", h=8), in0=bvx,
                    in1=g_("dtd")[0:m, g * 8:(g + 1) * 8].unsqueeze(2).to_broadcast([m, 8, 64]), op=ALU.mult),
                    reads=[bankT[g], t_("dtd")], writes=[t_xd])
            yield
            P.op("act", lambda e: e.activation(out=xs_sb[0:m, :], in_=psA[0:m, 0:1024], func=AF.Copy), reads=[bankT[0], bankT[1]], writes=[t_xs])
            yield
            P.op("pool", lambda e: e.tensor_tensor(
                out=xsD[0:m, :].rearrange("p (h d) -> p h d", h=16), in0=xs_sb[0:m, :].rearrange("p (h d) -> p h d", h=16),
                in1=dsk[0:m, :].unsqueeze(2).to_broadcast([m, 16, 64]), op=ALU.mult), reads=[t_xs, t_dsk], writes=[t_xsD])
            cbv = bank(3)[0:m, 128:256].rearrange("p (g i) -> p g i", g=2)[:, :, 0:CH]
            P.op("dve", lambda e: e.tensor_tensor(
                out=Mm[0:m, :, 0:CH].rearrange("p (g h) i -> p g h i", g=2), in0=dg[0:m, :, 0:CH].rearrange("p (g h) i -> p g h i", g=2),
                in1=cbv.unsqueeze(2).to_broadcast([m, 2, 8, CH]), op=ALU.mult), reads=t_dgq + [bankT[3]], writes=[t_M])
            yield

        def runBC(s):
            t0, m = SUBT[s]
            p = s % 2
            samp = (s == 16)
            CH = 32 if samp else 64
            bcb, t_bcb = bcb2[p]
            Btok, t_Btok = Btok2[p]
            indec, t_indec = indec2[p]
            cd, t_cd = cd2[p]
            xd, t_xd = xd2[p]
            sz, t_sz = sz2[p]
            xres, t_xres = sz, t_sz
            for g in range(2):
                def mmy(e, g=g):
                    e.matmul(bank(4 + g)[0:m, :], lhsT=ident_b[0:m, 0:m], rhs=xsD[0:m, g * 512:(g + 1) * 512], start=True, stop=False, skip_group_check=True)
                    for hh in range(8):
                        h = g * 8 + hh
                        for c in range(2):
                            i = e.matmul(bank(4 + g)[c * CH:(c + 1) * CH, hh * 64:(hh + 1) * 64], lhsT=Mm[c * CH:(c + 1) * CH, h, 0:CH],
                                         rhs=x_dt[c * CH:(c + 1) * CH, h * 64:(h + 1) * 64], start=False, stop=(hh == 7 and c == 1), skip_group_check=True)
                    return i
                P.op("pe", mmy, reads=[t_identb, t_xsD, t_M, t_xdt], writes=[bankT[4 + g]])
            yield
            P.op("act", lambda e: e.activation(out=ysb[0:m, :], in_=psB[0:m, 0:1024], func=AF.Copy), reads=[bankT[4], bankT[5]], writes=[t_y])
            yield
            pending_out = []
            for c in range(2):
                r0, r1 = c * CH, (c + 1) * CH
                if samp:
                    (Sx, t_Sx), (Sbx, t_Sbx) = Ss[c], Sbs[c]
                else:
                    (Sx, t_Sx), (Sbx, t_Sbx) = (S, t_S), (Sb, t_Sb)
                for g in range(2):
                    P.op("pe", lambda e, g=g, r0=r0, r1=r1, Sbx=Sbx: e.matmul(
                        bank(6 + g)[r0:r1, :], lhsT=bcb[:, 2 + g, r0:r1], rhs=Sbx[:, g * 512:(g + 1) * 512], start=True, stop=True, skip_group_check=True),
                        reads=[t_bcb, t_Sbx], writes=[bankT[6 + g]])
                    P.op("pe", lambda e, g=g, r0=r0, r1=r1: e.matmul(
                        bank(4 + g)[:, :], lhsT=Btok[r0:r1, g, :], rhs=xd[r0:r1, g * 512:(g + 1) * 512], start=True, stop=True),
                        reads=[t_Btok, t_xd], writes=[bankT[4 + g]])
                yield
                P.op("pool", lambda e, c=c, Sx=Sx: e.tensor_tensor(
                    out=Sx[:, :].rearrange("p (h d) -> p h d", h=16), in0=Sx[:, :].rearrange("p (h d) -> p h d", h=16),
                    in1=cd[:, c, :].unsqueeze(2).to_broadcast([128, 16, 64]), op=ALU.mult), reads=[t_Sx, t_cd], writes=[t_Sx])
                P.op("dve", lambda e, Sx=Sx: e.tensor_tensor(out=Sx[:, :], in0=Sx[:, :], in1=psB[:, 0:1024], op=ALU.add),
                     reads=[t_Sx, bankT[4], bankT[5]], writes=[t_Sx])
                yield
                P.op("act", lambda e, Sx=Sx, Sbx=Sbx: e.activation(out=Sbx[:, :], in_=Sx[:, :], func=AF.Copy), reads=[t_Sx], writes=[t_Sbx])
                yield
                if samp:
                    pending_out.append((Sx, t_Sx, 1 + c))
                elif s == 15 and c == 1:
                    pending_out.append((Sx, t_Sx, 0))
            for g in range(2):
                P.op("dve", lambda e, g=g: e.tensor_tensor(
                    out=junk[0:m, 0:512].rearrange("p (h d) -> p h d", h=8), in0=bank(6 + g)[0:m, :].rearrange("p (h d) -> p h d", h=8),
                    in1=indec[0:m, g * 8:(g + 1) * 8].unsqueeze(2).to_broadcast([m, 8, 64]), op=ALU.mult),
                    reads=[bankT[6 + g], t_indec], writes=[t_junk])
                P.op("pool", lambda e, g=g: e.tensor_tensor(out=ysb[0:m, g * 512:(g + 1) * 512], in0=ysb[0:m, g * 512:(g + 1) * 512], in1=junk[0:m, 0:512], op=ALU.add),
                     reads=[t_junk, t_y], writes=[t_y])
                yield
            for po in pending_out:
                state_out(*po)
                yield
            if debug:
                P.dma("sp", lambda e: e.dma_start(out=dbg_y[t0:t0 + m, :], in_=ysb[0:m, :]), reads=[t_y])
            P.op("pool", lambda e: e.tensor_tensor(out=ysb[0:m, :], in0=ysb[0:m, :], in1=sz[0:m, :], op=ALU.mult), reads=[t_y, t_sz], writes=[t_y])
            yield
            for g in range(2):
                P.op("act", lambda e, g=g: e.activation(out=junk[0:m, 0:512], in_=ysb[0:m, g * 512:(g + 1) * 512], func=AF.Square,
                                                        accum_out=ssq[0:m, g:g + 1]), reads=[t_y], writes=[t_junk, t_ssq])
            P.op("act", lambda e: e.activation(out=ssq[0:m, 2:4], in_=ssq[0:m, 0:2], func=AF.Ln, scale=1.0 / 512.0, bias=eps_t[0:m, :]),
                 reads=[t_ssq, t_eps], writes=[t_ssq])
            P.op("act", lambda e: e.activation(out=ssq[0:m, 0:2], in_=ssq[0:m, 2:4], func=AF.Exp, scale=-0.5), reads=[t_ssq], writes=[t_ssq])
            yield
            for g in range(2):
                P.op("dve", lambda e, g=g: e.scalar_tensor_tensor(
                    out=yn[0:m, g * 512:(g + 1) * 512], in0=ysb[0:m, g * 512:(g + 1) * 512], scalar=ssq[0:m, g:g + 1],
                    in1=gss[0:m, g * 512:(g + 1) * 512], op0=ALU.mult, op1=ALU.mult), reads=[t_y, t_ssq, t_gss], writes=[t_yn])
            yield
            if debug:
                P.dma("sp", lambda e: e.dma_start(out=dbg_yn[t0:t0 + m, :], in_=yn[0:m, :]), reads=[t_yn])

            def tryn(e):
                for k in range(8):
                    i = e.transpose(out=pv6[:, k * 128:k * 128 + m], in_=yn[0:m, k * 128:(k + 1) * 128], identity=ident_b[0:m, 0:m])
                return i
            P.op("pe", tryn, reads=[t_yn, t_identb], writes=[bankT[6]])
            P.dma("sp", lambda e: e.dma_start(out=xres[0:m, :], in_=x1d[t0:t0 + m, :]), reads=[x1d_T[s], t_sz], writes=[t_xres])
            yield
            P.op("act", lambda e: e.activation(out=ynT[:, :, 0:m], in_=pv6.rearrange("p (k t) -> p k t", k=8)[:, :, 0:m], func=AF.Copy),
                 reads=[bankT[6]], writes=[t_ynT])
            yield
            for half in range(2):
                def mmo(e, half=half):
                    for k in range(8):
                        i = e.matmul(bank(4 + half)[0:m, :], lhsT=ynT[:, k, 0:m], rhs=wot[:, k, half * 512:(half + 1) * 512], start=(k == 0), stop=(k == 7))
                    return i
                P.op("pe", mmo, reads=[t_ynT, t_wot[half]], writes=[bankT[4 + half]])
            yield
            P.op("dve", lambda e: e.scalar_tensor_tensor(out=r1s[0:m, :], in0=xres[0:m, :], scalar=ALPHA, in1=psB[0:m, 0:1024],
                                                         op0=ALU.mult, op1=ALU.add), reads=[t_xres, bankT[4], bankT[5]], writes=[t_r1s])
            yield
            P.dma("sp", lambda e: e.dma_start(out=r1d[t0:t0 + m, :], in_=r1s[0:m, :]), reads=[t_r1s], writes=[r1d_T[s]])
            yield

        def interleave_w(pairs):
            pairs = [[g, w] for g, w in pairs if g is not None]
            while pairs:
                for pr in list(pairs):
                    for _ in range(pr[1]):
                        try:
                            next(pr[0])
                        except StopIteration:
                            pairs.remove(pr)
                            break

        def interleave2(gens):
            gens = [g for g in gens if g is not None]
            while gens:
                for g in list(gens):
                    try:
                        next(g)
                    except StopIteration:
                        gens.remove(g)

        interleave2([prepA(0)])
        for s in range(17):
            gA = prepA(s + 1) if s + 1 < 17 else None
            if gA is not None:
                for _ in range(SSD_K):
                    next(gA)
            interleave_w([(gA, SSD_W[1]), (runBC(s), SSD_W[0])])
        A.reset(mk)

    SCALE = 192.0 ** -0.5

    def mla_phase():
        mk = A.mark()
        w_in_v = w["w_in"].rearrange("(ko ki) f -> ki ko f", ki=128)
        oT, r = A.alloc("oT", [128, 8, NT], BF16)
        oT_T = [[A.newT(r, f"oT{h}_{s}") for s in range(17)] for h in range(8)]
        cqnT, r = A.alloc("cqnT", [128, 4, NT], BF16)
        cqnT_T = [A.newT(r, f"cqnT{s}") for s in range(17)]
        latTp, r = A.alloc("latTp", [128, 4, NPROMPT], BF16)
        latTp_T = [A.newT(r, f"latTp{s}") for s in range(16)]
        latTs, r = A.alloc("latTs", [128, 4, 2, 1056], BF16)
        latTs_T = [[A.newT(r, f"latTs{j}_{a}") for a in range(9)] for j in range(2)]
        krTp, r = A.alloc("krTp", [128, NPROMPT], BF16)
        krTp_T = [A.newT(r, f"krTp{s}") for s in range(16)]
        krTs, r = A.alloc("krTs", [128, 2, 1056], BF16)
        krTs_T = [[A.newT(r, f"krTs{j}_{a}") for a in range(9)] for j in range(2)]
        mk_a = A.mark()
        wq, t_wq = al("wq_in", [128, 8, 512], BF16)
        wkv, t_wkv = al("wkv_in", [128, 8, 512], BF16)
        wkr, t_wkr = al("wkr_in", [128, 8, 64], BF16)
        P.dma("pool", lambda e: e.dma_start(out=wq[:], in_=w_in_v[:, :, 2576:3088]), writes=[t_wq])
        P.dma("pool", lambda e: e.dma_start(out=wkv[:], in_=w_in_v[:, :, 3088:3600]), writes=[t_wkv])
        P.dma("pool", lambda e: e.dma_start(out=wkr[:], in_=w_in_v[:, :, 3600:3664]), writes=[t_wkr])
        gq, t_gq = al("gq", [128, 512])
        gkv, t_gkv = al("gkv", [128, 512])
        P.dma("sp", lambda e: e.dma_start(out=gq[:], in_=gq_d[:, :]), writes=[t_gq])
        P.dma("sp", lambda e: e.dma_start(out=gkv[:], in_=gkv_d[:, :]), writes=[t_gkv])
        ctok, t_ctok = al("ctok", [128, 8, 512], BF16)
        ckr, t_ckr = al("ckr", [128, 8, 64], BF16)
        for j in range(2):
            P.dma("pool", lambda e, j=j: e.dma_start(out=ctok[:], in_=clat_d[j].rearrange("(a p) f -> p a f", p=128)), writes=[t_ctok])
            P.dma("pool", lambda e, j=j: e.dma_start(out=ckr[:], in_=ckr_d[j].rearrange("(a p) f -> p a f", p=128)), writes=[t_ckr])
            for a in range(8):
                bk = 4 + (a % 2)
                pv = bank(bk).bitcast(BF16)

                def trc(e, a=a, pv=pv):
                    for c in range(4):
                        i = e.transpose(out=pv[:, c * 128:(c + 1) * 128], in_=ctok[:, a, c * 128:(c + 1) * 128], identity=ident_b[:])
                    return e.transpose(out=pv[0:64, 512:640], in_=ckr[:, a, :], identity=ident_b[:])
                P.op("pe", trc, reads=[t_ctok, t_ckr, t_identb], writes=[bankT[bk]])
                P.op("act", lambda e, a=a, j=j, pv=pv: e.activation(out=latTs[:, :, j, a * 128:(a + 1) * 128],
                                                                  in_=pv[:, 0:512].rearrange("p (c t) -> p c t", c=4), func=AF.Copy),
                     reads=[bankT[bk]], writes=[latTs_T[j][a]])
                P.op("dve", lambda e, a=a, j=j, pv=pv: e.tensor_copy(out=krTs[0:64, j, a * 128:(a + 1) * 128], in_=pv[0:64, 512:640]),
                     reads=[bankT[bk]], writes=[krTs_T[j][a]])
        sq, t_sq = al("sq", [128, 4])
        cqn2 = [al(f"cqn{i}", [128, 512], BF16) for i in range(2)]
        latf = [al(f"latf{i}", [128, 512]) for i in range(2)]
        latb2 = [al(f"latb{i}", [128, 512], BF16) for i in range(2)]
        cs4 = [al(f"cs4_{i}", [128, 128]) for i in range(2)]
        rt, t_rt = al("rt", [128, 128])
        krf = [al(f"krf{i}", [128, 64]) for i in range(2)]
        krb2 = [al(f"krb{i}", [128, 64], BF16) for i in range(2)]
        jk, t_jk = al("jk", [128, 512])

        sq2 = [al(f"sq2_{i}", [128, 4]) for i in range(2)]

        def b2a_bk(s, i):
            return (0, 1, 2)[i] if s % 2 == 0 else (5, 6, 7)[i]

        def b2a_part1a(s):
            t0, m = SUBT[s]
            xr = [xT_T[s]]
            csh, t_cs = cs4[s % 2]
            sq, t_sq = sq2[s % 2]
            P.dma("sp", lambda e: e.dma_start(out=csh[0:m, :], in_=cs4_d[t0:t0 + m, :]), writes=[t_cs])
            for (ib, wt, t_wt, nn) in [(0, wq, t_wq, 512), (1, wkv, t_wkv, 512), (2, wkr, t_wkr, 64)]:
                bk = b2a_bk(s, ib)

                def mmp(e, bk=bk, wt=wt, nn=nn):
                    for k in range(8):
                        i = e.matmul(bank(bk)[0:m, 0:nn], lhsT=xT[:, k, t0:t0 + m], rhs=wt[:, k, :], start=(k == 0), stop=(k == 7))
                    return i
                P.op("pe", mmp, reads=xr + [t_wt], writes=[bankT[bk]])
                yield
            for i_ in range(2):
                bk = b2a_bk(s, i_)
                P.op("act", lambda e, i_=i_, bk=bk: e.activation(out=jk[0:m, :], in_=bank(bk)[0:m, :], func=AF.Square, accum_out=sq[0:m, i_:i_ + 1]),
                     reads=[bankT[bk]], writes=[t_jk, t_sq])
                yield
            P.op("act", lambda e: e.activation(out=sq[0:m, 2:4], in_=sq[0:m, 0:2], func=AF.Sqrt, scale=1.0 / 512.0, bias=eps_t[0:m, :]),
                 reads=[t_sq, t_eps], writes=[t_sq])
            yield

        def b2a_part1b(s):
            t0, m = SUBT[s]
            cqn, t_cqn = cqn2[s % 2]
            latfh, t_latf = latf[s % 2]
            csh, t_cs = cs4[s % 2]
            krfh, t_krf = krf[s % 2]
            sq, t_sq = sq2[s % 2]
            b0, b1, b2_ = b2a_bk(s, 0), b2a_bk(s, 1), b2a_bk(s, 2)
            P.op("dve", lambda e: e.tensor_tensor(out=rt[0:m, 0:64], in0=bank(b2_)[0:m, 0:64], in1=csh[0:m, 0:64], op=ALU.mult),
                 reads=[bankT[b2_], t_cs], writes=[t_rt])
            P.op("dve", lambda e: e.tensor_tensor(out=rt[0:m, 64:128], in0=bank(b2_)[0:m, 0:64], in1=csh[0:m, 64:128], op=ALU.mult),
                 reads=[bankT[b2_], t_cs], writes=[t_rt])
            yield
            P.op("dve", lambda e: e.tensor_tensor(out=krfh[0:m, 0:32], in0=rt[0:m, 0:32], in1=rt[0:m, 32:64], op=ALU.subtract), reads=[t_rt], writes=[t_krf])
            P.op("dve", lambda e: e.tensor_tensor(out=krfh[0:m, 32:64], in0=rt[0:m, 64:96], in1=rt[0:m, 96:128], op=ALU.add), reads=[t_rt], writes=[t_krf])
            yield
            P.op("dve", lambda e: e.reciprocal(out=sq[0:m, 0:2], in_=sq[0:m, 2:4]), reads=[t_sq], writes=[t_sq])
            P.op("dve", lambda e: e.scalar_tensor_tensor(out=cqn[0:m, :], in0=bank(b0)[0:m, :], scalar=sq[0:m, 0:1], in1=gq[0:m, :],
                                                         op0=ALU.mult, op1=ALU.mult), reads=[bankT[b0], t_sq, t_gq], writes=[t_cqn])
            yield
            P.op("dve", lambda e: e.scalar_tensor_tensor(out=latfh[0:m, :], in0=bank(b1)[0:m, :], scalar=sq[0:m, 1:2], in1=gkv[0:m, :],
                                                         op0=ALU.mult, op1=ALU.mult), reads=[bankT[b1], t_sq, t_gkv], writes=[t_latf])
            yield

        def b2a_part2(s):
            t0, m = SUBT[s]
            cqn, t_cqn = cqn2[s % 2]
            latfh, t_latf = latf[s % 2]
            latb, t_latb = latb2[s % 2]
            krfh, t_krf = krf[s % 2]
            krb, t_krb = krb2[s % 2]
            P.dma("sp", lambda e: e.dma_start(out=lat_o[t0:t0 + m, :], in_=latfh[0:m, :]), reads=[t_latf])
            P.dma("sp", lambda e: e.dma_start(out=kr_o[t0:t0 + m, :], in_=krfh[0:m, :]), reads=[t_krf])
            P.op("act", lambda e: e.activation(out=latb[0:m, :], in_=latfh[0:m, :], func=AF.Copy), reads=[t_latf], writes=[t_latb])
            P.op("act", lambda e: e.activation(out=krb[0:m, :], in_=krfh[0:m, :], func=AF.Copy), reads=[t_krf], writes=[t_krb])
            yield
            bk = 4
            pv = bank(bk).bitcast(BF16)

            def trq(e):
                for c in range(4):
                    i = e.transpose(out=pv[:, c * 128:c * 128 + m], in_=cqn[0:m, c * 128:(c + 1) * 128], identity=ident_b[0:m, 0:m])
                for c in range(4):
                    i = e.transpose(out=pv[:, 512 + c * 128:512 + c * 128 + m], in_=latb[0:m, c * 128:(c + 1) * 128], identity=ident_b[0:m, 0:m])
                return i
            P.op("pe", trq, reads=[t_cqn, t_latb, t_identb], writes=[bankT[bk]])
            pv3 = bank(3).bitcast(BF16)
            P.op("pe", lambda e: e.transpose(out=pv3[0:64, 0:m], in_=krb[0:m, :], identity=ident_b[0:m, 0:m]),
                 reads=[t_krb, t_identb], writes=[bankT[3]])
            yield
            pvv = pv.rearrange("p (c t) -> p c t", c=8)
            P.op("act", lambda e: e.activation(out=cqnT[:, :, t0:t0 + m], in_=pvv[:, 0:4, 0:m], func=AF.Copy), reads=[bankT[bk]], writes=[cqnT_T[s]])
            yield
            if s < 16:
                P.op("dve", lambda e: e.tensor_copy(out=latTp[:, :, t0:t0 + m], in_=pvv[:, 4:8, 0:m]), reads=[bankT[bk]], writes=[latTp_T[s]])
                P.op("dve", lambda e: e.tensor_copy(out=krTp[0:64, t0:t0 + m], in_=pv3[0:64, 0:m]), reads=[bankT[3]], writes=[krTp_T[s]])
            else:
                for j in range(2):
                    P.op("dve", lambda e, j=j: e.tensor_copy(out=latTs[:, :, j, 1024:1056], in_=pvv[:, 4:8, j * 32:(j + 1) * 32]),
                         reads=[bankT[bk]], writes=[latTs_T[j][8]])
                    P.op("dve", lambda e, j=j: e.tensor_copy(out=krTs[0:64, j, 1024:1056], in_=pv3[0:64, j * 32:(j + 1) * 32]),
                         reads=[bankT[3]], writes=[krTs_T[j][8]])
            yield

        interleave_g([b2a_part1a(0)])
        interleave_g([b2a_part1a(1), b2a_part1b(0)])
        for s in range(17):
            interleave_g([b2a_part1a(s + 2) if s + 2 < 17 else None, b2a_part1b(s + 1) if s + 1 < 17 else None, b2a_part2(s)])
        A.reset(mk_a)
        if MLA_LEVEL <= 1:
            return mk, oT, oT_T, []
        w_uq_v = w["w_uq"].rearrange("(c p) (h t) -> p c h t", p=128, t=192)
        wuq, r = A.alloc("wuq", [128, 4, 8, 192], BF16)
        t_wuq = [A.newT(r, f"wuq{c}") for c in range(4)]
        wuqr, r = A.alloc("wuqr", [128, 4, 8, 64], BF16)
        t_wuqr = A.newT(r, "wuqr")
        wukv, r = A.alloc("wukv", [128, 4, 2048], BF16)
        t_wukv = [A.newT(r, f"wukv{c}") for c in range(4)]
        w_ukv_v = w["w_ukv"].rearrange("(c p) f -> p c f", p=128)
        for c in range(4):
            P.dma("pool", lambda e, c=c: e.dma_start(out=wuq[:, c, :, :], in_=w_uq_v[:, c, :, :]), writes=[t_wuq[c]])
            P.dma("pool", lambda e, c=c: e.dma_start(out=wukv[:, c, :], in_=w_ukv_v[:, c, :]), writes=[t_wukv[c]])
            P.dma("pool", lambda e, c=c: e.dma_start(out=wuqr[:, c, :, 0:32], in_=w_uq_v[:, c, :, 160:192]), writes=[t_wuqr])
            P.dma("pool", lambda e, c=c: e.dma_start(out=wuqr[:, c, :, 32:64], in_=w_uq_v[:, c, :, 128:160]), writes=[t_wuqr])
        P.op("dve", lambda e: e.tensor_scalar(out=wuqr[:, :, :, 0:32], in0=wuqr[:, :, :, 0:32], scalar1=-1.0, scalar2=None, op0=ALU.mult),
             reads=[t_wuqr], writes=[t_wuqr])
        A2 = Arena(nc, xT_rec[0], xT_rec[1])
        xT_extra = []

        def al2(name, shape, dt=F32):
            h, _ = A2.alloc(name, shape, dt)
            t = T(name, after=list(xT_T))
            xT_extra.append(t)
            return h, t
        mq, t_mq = al("maskq", [128, 128], BF16)
        P.dma("pool", lambda e: e.dma_start(out=mq[:], in_=maskq_d[:, :]), writes=[t_mq])
        cosT, t_cosT = al2("a2cosT", [128, NT])
        sinT, t_sinT = al2("a2sinT", [128, NT])
        P.dma("sp", lambda e: e.dma_start(out=cosT[0:64, :], in_=cosT_d[:, :]), writes=[t_cosT])
        P.dma("sp", lambda e: e.dma_start(out=sinT[0:64, :], in_=sinT_d[:, :]), writes=[t_sinT])
        Pex, t_Pex = al2("a2Pex", [128, 2048])
        qt1, t_qt1 = al2("a2qt1", [128, 256])
        qt2, t_qt2 = al2("a2qt2", [128, 256])
        KTb, Vb, qnb, qrb = [], [], [], []
        for i in range(2):
            h_, r = A.alloc(f"KT{i}", [128, NPROMPT], BF16)
            KTb.append((h_, [A.newT(r, f"KT{i}_{k}") for k in range(4)]))
            h_, r = A.alloc(f"V{i}", [128, 16, 128], BF16)
            Vb.append((h_, [A.newT(r, f"V{i}_{k}") for k in range(4)]))
            h_, r = A.alloc(f"qn{i}", [128, NPROMPT], BF16)
            qnb.append((h_, [A.newT(r, f"qn{i}_{k}") for k in range(4)]))
            h_, r = A.alloc(f"qr{i}", [128, NPROMPT], BF16)
            qrb.append((h_, [A.newT(r, f"qr{i}_{k}") for k in range(4)]))
        qns, r = A.alloc("qns", [128, 8, 64], BF16)
        t_qns = [A.newT(r, f"qns{h}") for h in range(8)]
        qrs, r = A.alloc("qrs", [128, 8, 64], BF16)
        t_qrs = [A.newT(r, f"qrs{h}") for h in range(8)]
        NPN = 3
        Pn = [al(f"Pn{i}", [128, 2048], BF16) for i in range(NPN)]
        PT = [al2("a2PT0", [128, 16, 128], BF16), al("PT1", [128, 16, 128], BF16)]
        st8 = [al(f"st8_{i}", [128, 16]) for i in range(NPN)]
        ctr = dict(blk=0, rnd=0, att=0, pb=0)
        OB = 6

        def next_pb():
            ctr["pb"] += 1
            return 5 if ctr["pb"] % 2 else 7

        def proj_items(h):
            items = []
            KT, tKT = KTb[h % 2]
            V, tV = Vb[h % 2]
            qn, tqn = qnb[h % 2]
            qr, tqr = qrb[h % 2]
            for t, (t0, n) in enumerate(TILES):
                cr = [cqnT_T[s] for s in SUB_OF_TILE[t]]

                def it_qn(t=t, t0=t0, n=n, cr=cr):
                    PB = next_pb()

                    def mm(e):
                        for c in range(4):
                            i = e.matmul(bank(PB)[:, 0:n], lhsT=wuq[:, c, h, 0:128], rhs=cqnT[:, c, t0:t0 + n], start=(c == 0), stop=(c == 3))
                        return i
                    P.op("pe", mm, reads=cr + t_wuq, writes=[bankT[PB]])
                    yield
                    if t < 4:
                        P.op("act", lambda e: e.activation(out=qn[:, t0:t0 + n], in_=bank(PB)[:, 0:n], func=AF.Copy), reads=[bankT[PB]], writes=[tqn[t]])
                    else:
                        P.op("act", lambda e: e.activation(out=qns[:, h, :], in_=bank(PB)[:, 0:n], func=AF.Copy), reads=[bankT[PB]], writes=[t_qns[h]])
                items.append(it_qn)
                for hf in range(2 if t < 4 else 1):
                    n2 = 256 if t < 4 else 64
                    c0 = t0 + hf * 256

                    def it_qr(t=t, c0=c0, n2=n2, cr=cr):
                        PB = next_pb()

                        def mm(e):
                            for c in range(4):
                                e.matmul(bank(PB)[0:64, 0:n2], lhsT=wuq[:, c, h, 128:192], rhs=cqnT[:, c, c0:c0 + n2], start=(c == 0), stop=(c == 3),
                                         skip_group_check=True)
                            for c in range(4):
                                i = e.matmul(bank(PB)[0:64, 256:256 + n2], lhsT=wuqr[:, c, h, :], rhs=cqnT[:, c, c0:c0 + n2], start=False, stop=(c == 3),
                                             skip_group_check=True)
                            return i
                        P.op("pe", mm, reads=cr + t_wuq + [t_wuqr], writes=[bankT[PB]])
                        yield
                        P.op("dve", lambda e: e.tensor_tensor(out=qt1[0:64, 0:n2], in0=bank(PB)[0:64, 0:n2], in1=cosT[0:64, c0:c0 + n2], op=ALU.mult),
                             reads=[bankT[PB], t_cosT], writes=[t_qt1])
                        P.op("dve", lambda e: e.tensor_tensor(out=qt2[0:64, 0:n2], in0=bank(PB)[0:64, 256:256 + n2], in1=sinT[0:64, c0:c0 + n2], op=ALU.mult),
                             reads=[bankT[PB], t_sinT], writes=[t_qt2])
                        if t < 4:
                            P.op("pool", lambda e: e.tensor_tensor(out=qr[0:64, c0:c0 + n2], in0=qt1[0:64, 0:n2], in1=qt2[0:64, 0:n2], op=ALU.add),
                                 reads=[t_qt1, t_qt2], writes=[tqr[t]])
                        else:
                            P.op("pool", lambda e: e.tensor_tensor(out=qrs[0:64, h, :], in0=qt1[0:64, 0:n2], in1=qt2[0:64, 0:n2], op=ALU.add),
                                 reads=[t_qt1, t_qt2], writes=[t_qrs[h]])
                    items.append(it_qr)
            for kb in range(4):
                def it_k(kb=kb):
                    PB = next_pb()

                    def mm(e):
                        for c in range(4):
                            i = e.matmul(bank(PB)[:, :], lhsT=wukv[:, c, h * 256:h * 256 + 128], rhs=latTp[:, c, kb * 512:(kb + 1) * 512],
                                         start=(c == 0), stop=(c == 3))
                        return i
                    P.op("pe", mm, reads=[latTp_T[4 * kb + i] for i in range(4)] + t_wukv, writes=[bankT[PB]])
                    yield
                    P.op("act", lambda e: e.activation(out=KT[:, kb * 512:(kb + 1) * 512], in_=bank(PB)[:, :], func=AF.Copy),
                         reads=[bankT[PB]], writes=[tKT[kb]])

                def it_v(kq=kb):
                    PB = next_pb()

                    def mm(e):
                        for j in range(4):
                            kt = 4 * kq + j
                            for c in range(4):
                                i = e.matmul(bank(PB)[:, j * 128:(j + 1) * 128], lhsT=latTp[:, c, kt * 128:(kt + 1) * 128],
                                             rhs=wukv[:, c, h * 256 + 128:h * 256 + 256], start=(c == 0), stop=(c == 3), skip_group_check=True)
                        return i
                    P.op("pe", mm, reads=[latTp_T[4 * kq + i] for i in range(4)] + t_wukv, writes=[bankT[PB]])
                    yield
                    P.op("act", lambda e: e.activation(out=V[:, 4 * kq:4 * kq + 4, :], in_=bank(PB)[:, :].rearrange("p (j d) -> p j d", j=4), func=AF.Copy),
                         reads=[bankT[PB]], writes=[tV[kq]])
                items.append(it_k)
                items.append(it_v)
            return items

        def proj_items_s(h, j, slot):
            items = []
            KT, tKT = KTb[slot]
            V, tV = Vb[slot]
            for kb in range(3):
                cols = min(512, 1056 - kb * 512)

                def it_k(kb=kb, cols=cols):
                    PB = next_pb()

                    def mm(e):
                        for c in range(4):
                            i = e.matmul(bank(PB)[:, 0:cols], lhsT=wukv[:, c, h * 256:h * 256 + 128], rhs=latTs[:, c, j, kb * 512:kb * 512 + cols],
                                         start=(c == 0), stop=(c == 3))
                        return i
                    P.op("pe", mm, reads=latTs_T[j] + t_wukv, writes=[bankT[PB]])
                    yield
                    P.op("act", lambda e: e.activation(out=KT[:, kb * 512:kb * 512 + cols], in_=bank(PB)[:, 0:cols], func=AF.Copy),
                         reads=[bankT[PB]], writes=[tKT[kb]])
                items.append(it_k)
                kts = [kt for kt in range(4 * kb, min(9, 4 * kb + 4))]

                def it_v(kq=kb, kts=kts):
                    PB = next_pb()

                    def mm(e):
                        for kt in kts:
                            kk = min(128, 1056 - kt * 128)
                            for c in range(4):
                                i = e.matmul(bank(PB)[0:kk, (kt - 4 * kq) * 128:(kt - 4 * kq + 1) * 128], lhsT=latTs[:, c, j, kt * 128:kt * 128 + kk],
                                             rhs=wukv[:, c, h * 256 + 128:h * 256 + 256], start=(c == 0), stop=(c == 3), skip_group_check=True)
                        return i
                    P.op("pe", mm, reads=latTs_T[j] + t_wukv, writes=[bankT[PB]])
                    yield
                    nkt_ = len(kts)
                    P.op("dve", lambda e: e.tensor_copy(out=V[:, 4 * kq:4 * kq + nkt_, :],
                                                        in_=bank(PB)[:, 0:nkt_ * 128].rearrange("p (j d) -> p j d", j=nkt_)),
                         reads=[bankT[PB]], writes=[tV[kq]])
                items.append(it_v)
            return items

        stT = [{k: T(f"st{i}_{k}") for k in ["mx0", "mx1", "ng0", "ng1", "sm0", "sm1", "c"]} for i in range(NPN)]

        def stage1(n, h, qT, t_q, qrT_, t_qr_, qc0, mq_, KT, tKT, krT_ap, t_krT, nk, diag, bsz=1024):
            Pnh, t_Pn = Pn[n % NPN]
            sth, _ = st8[n % NPN]
            tt = stT[n % NPN]
            blocks = [(b0, min(bsz, nk - b0)) for b0 in range(0, nk, bsz)]
            for bi, (b0, bn) in enumerate(blocks):
                pr = ctr["blk"] % 2 if bsz == 1024 else 0
                ctr["blk"] += 1
                nbk = (bn + 511) // 512
                bts = [bankT[2 * pr + x] for x in range(nbk)]

                def mms(e, b0=b0, bn=bn, pr=pr, nbk=nbk):
                    for x in range(nbk):
                        k0 = b0 + x * 512
                        cols = min(512, b0 + bn - k0)
                        has_diag = diag is not None and (diag * 128) // 512 == k0 // 512
                        e.matmul(bank(2 * pr + x)[0:mq_, 0:cols], lhsT=qT[:, qc0:qc0 + mq_], rhs=KT[:, k0:k0 + cols], start=True, stop=False,
                                 skip_group_check=True)
                        i = e.matmul(bank(2 * pr + x)[0:mq_, 0:cols], lhsT=qrT_[0:64, qc0:qc0 + mq_], rhs=krT_ap[0:64, k0:k0 + cols],
                                     start=False, stop=not has_diag, skip_group_check=True)
                        if has_diag:
                            dc = (diag * 128) % 512
                            i = e.matmul(bank(2 * pr + x)[0:mq_, dc:dc + 128], lhsT=ident_b[0:mq_, 0:mq_], rhs=mq[0:mq_, :], start=False, stop=True,
                                         skip_group_check=True)
                    return i
                kbs = sorted(set((b0 + x * 512) // 512 for x in range(nbk)))
                P.op("pe", mms, reads=t_q + t_qr_ + [tKT[k] for k in kbs if k < len(tKT)] + [t_identb, t_mq] + t_krT, writes=bts)
                yield
                sc = (psA[0:mq_, 1024 * pr:1024 * pr + bn])
                P.op("dve", lambda e, sc=sc, bi=bi: e.tensor_reduce(out=sth[0:mq_, bi:bi + 1], in_=sc, op=ALU.max, axis=AX.X), reads=bts, writes=[tt[f"mx{bi}"]])
                P.op("dve", lambda e, bi=bi: e.tensor_scalar(out=sth[0:mq_, 2 + bi:3 + bi], in0=sth[0:mq_, bi:bi + 1], scalar1=-SCALE, scalar2=None, op0=ALU.mult),
                     reads=[tt[f"mx{bi}"]], writes=[tt[f"ng{bi}"]])
                yield
                P.op("act", lambda e, sc=sc, bi=bi, b0=b0, bn=bn: e.activation(out=Pex[0:mq_, b0:b0 + bn], in_=sc, func=AF.Exp, scale=SCALE,
                                                                               bias=sth[0:mq_, 2 + bi:3 + bi], accum_out=sth[0:mq_, 4 + bi:5 + bi]),
                     reads=bts + [tt[f"ng{bi}"]], writes=[t_Pex, tt[f"sm{bi}"]])
                yield
            tc = tt["c"]
            if len(blocks) == 1:
                P.op("dve", lambda e: e.reciprocal(out=sth[0:mq_, 6:7], in_=sth[0:mq_, 4:5]), reads=[tt["sm0"]], writes=[tc])
                P.op("dve", lambda e: e.tensor_scalar(out=Pnh[0:mq_, 0:nk], in0=Pex[0:mq_, 0:nk], scalar1=sth[0:mq_, 6:7], scalar2=None, op0=ALU.mult),
                     reads=[t_Pex, tc], writes=[t_Pn])
            else:
                P.op("dve", lambda e: e.tensor_tensor(out=sth[0:mq_, 8:9], in0=sth[0:mq_, 0:1], in1=sth[0:mq_, 1:2], op=ALU.max),
                     reads=[tt["mx0"], tt["mx1"]], writes=[tc])
                P.op("dve", lambda e: e.tensor_scalar(out=sth[0:mq_, 6:8], in0=sth[0:mq_, 0:2], scalar1=sth[0:mq_, 8:9], scalar2=None, op0=ALU.subtract),
                     reads=[tc, tt["mx0"], tt["mx1"]], writes=[tc])
                yield
                P.op("act", lambda e: e.activation(out=sth[0:mq_, 6:8], in_=sth[0:mq_, 6:8], func=AF.Exp, scale=SCALE), reads=[tc], writes=[tc])
                yield
                P.op("dve", lambda e: e.tensor_tensor(out=sth[0:mq_, 9:11], in0=sth[0:mq_, 6:8], in1=sth[0:mq_, 4:6], op=ALU.mult),
                     reads=[tc, tt["sm0"], tt["sm1"]], writes=[tc])
                P.op("dve", lambda e: e.tensor_tensor(out=sth[0:mq_, 11:12], in0=sth[0:mq_, 9:10], in1=sth[0:mq_, 10:11], op=ALU.add), reads=[tc], writes=[tc])
                P.op("dve", lambda e: e.reciprocal(out=sth[0:mq_, 12:13], in_=sth[0:mq_, 11:12]), reads=[tc], writes=[tc])
                P.op("dve", lambda e: e.tensor_scalar(out=sth[0:mq_, 13:15], in0=sth[0:mq_, 6:8], scalar1=sth[0:mq_, 12:13], scalar2=None, op0=ALU.mult),
                     reads=[tc], writes=[tc])
                yield
                for bi, (b0, bn) in enumerate(blocks):
                    eng_ = "dve" if bi == 0 else "pool"
                    P.op(eng_, lambda e, bi=bi, b0=b0, bn=bn: e.tensor_scalar(out=Pnh[0:mq_, b0:b0 + bn], in0=Pex[0:mq_, b0:b0 + bn],
                                                                              scalar1=sth[0:mq_, 13 + bi:14 + bi], scalar2=1.0, op0=ALU.mult, op1=ALU.mult),
                         reads=[t_Pex, tc], writes=[t_Pn])
            yield

        def stage2(n, h, qc0, mq_, nk, V, tV, s_out):
            Pnh, t_Pn = Pn[n % NPN]
            PTh, t_PT = PT[n % 2]
            nkt = (nk + 127) // 128
            use_xbar = USE_XBAR and mq_ == 128 and nk % 128 == 0
            if use_xbar:
                P.dmat("sp", lambda e: e.dma_start_transpose(out=PTh[:, 0:nkt, :], in_=Pnh[0:128, 0:nk]), reads=[t_Pn], writes=[t_PT])
                yield
            for r0 in (range(0, nkt, 8) if not use_xbar else []):
                r1 = min(nkt, r0 + 8)
                bk = 4
                pv = bank(bk).bitcast(BF16)

                def trp(e, r0=r0, r1=r1, pv=pv):
                    for kt in range(r0, r1):
                        kk = min(128, nk - kt * 128)
                        i = e.transpose(out=pv[0:kk, (kt - r0) * 128:(kt - r0) * 128 + mq_], in_=Pnh[0:mq_, kt * 128:kt * 128 + kk],
                                        identity=ident_b[0:mq_, 0:mq_])
                    return i
                P.op("pe", trp, reads=[t_Pn, t_identb], writes=[bankT[bk]])
                yield
                P.op("act", lambda e, r0=r0, r1=r1, pv=pv: e.activation(
                    out=PTh[:, r0:r1, 0:mq_], in_=pv.rearrange("p (k t) -> p k t", k=8)[:, 0:r1 - r0, 0:mq_], func=AF.Copy),
                    reads=[bankT[bk]], writes=[t_PT])
                yield

            def mmo(e):
                for kt in range(nkt):
                    kk = min(128, nk - kt * 128)
                    i = e.matmul(bank(OB)[:, 0:mq_], lhsT=V[0:kk, kt, :], rhs=PTh[0:kk, kt, 0:mq_], start=(kt == 0), stop=(kt == nkt - 1))
                return i
            P.op("pe", mmo, reads=[tV[k] for k in range(min(4, (nkt + 3) // 4))] + [t_PT], writes=[bankT[OB]])
            yield
            P.op("dve", lambda e: e.tensor_copy(out=oT[:, h, qc0:qc0 + mq_], in_=bank(OB)[:, 0:mq_]), reads=[bankT[OB]], writes=[oT_T[h][s_out]])
            yield

        def interleave_aw(pairs):
            pairs = [[g, w] for g, w in pairs if g is not None]
            while pairs:
                for pr in list(pairs):
                    for _ in range(pr[1]):
                        try:
                            next(pr[0])
                        except StopIteration:
                            pairs.remove(pr)
                            break

        def items_gen(items):
            for it in items:
                yield from it()
                yield

        def run_item(it):
            for _ in it():
                pass

        def interleave(gens):
            gens = [g for g in gens if g is not None]
            while gens:
                for g in list(gens):
                    try:
                        next(g)
                    except StopIteration:
                        gens.remove(g)

        NH = NHEADS_DBG
        DEPTH = 2
        for it in proj_items(0):
            run_item(it)
        pend_s2 = []
        for h in range(NH):
            nxt = proj_items(h + 1) if h + 1 < NH else []
            KT, tKT = KTb[h % 2]
            V, tV = Vb[h % 2]
            qn, tqn = qnb[h % 2]
            qr, tqr = qrb[h % 2]
            for i in range(16):
                n = ctr["att"]
                ctr["att"] += 1
                g1 = stage1(n, h, qn, [tqn[i // 4]], qr, [tqr[i // 4]], 128 * i, 128, KT, tKT, krTp, krTp_T[0:i + 1], 128 * (i + 1), i)
                pend_s2.append((n, h, 128 * i, 128, 128 * (i + 1), V, tV, i))
                g2 = stage2(*pend_s2.pop(0)) if len(pend_s2) > DEPTH else None
                take = []
                if i >= DEPTH:
                    for _ in range(2 if i % 2 == 0 else 1):
                        if nxt:
                            take.append(nxt.pop(0))
                interleave_aw([(g2, ATT_W[1]), (g1, ATT_W[0]), (items_gen(take), ATT_W[2])])
            while nxt:
                run_item(nxt.pop(0))
        while pend_s2:
            interleave([stage2(*pend_s2.pop(0))])
        ctxs = [(h, j) for h in range(NH) for j in range(2)]
        nctx = len(ctxs)

        def kv_items(c):
            if c >= nctx:
                return [], []
            its = proj_items_s(ctxs[c][0], ctxs[c][1], c % 2)
            return its[0::2], its[1::2]

        def s1_ctx(c, n):
            h, j = ctxs[c]
            KT, tKT = KTb[c % 2]
            return stage1(n, h, qns[:, h, :], [t_qns[h]], qrs[:, h, :], [t_qrs[h]], 32 * j, 32, KT, tKT, krTs[:, j, :], krTs_T[j], 1056, None, bsz=1536)

        def s2_ctx(c, n):
            h, j = ctxs[c]
            V, tV = Vb[c % 2]
            return stage2(n, h, NPROMPT + 32 * j, 32, 1056, V, tV, 16)

        if nctx:
            n0 = ctr["att"]
            ctr["att"] += nctx
            k0, v0 = kv_items(0)
            k1, _ = kv_items(1)
            interleave([items_gen(k0 + v0)])
            interleave([s1_ctx(0, n0), items_gen(k1)])
            for ci in range(nctx):
                k2, _ = kv_items(ci + 2)
                _, v1 = kv_items(ci + 1)
                interleave([s1_ctx(ci + 1, n0 + ci + 1) if ci + 1 < nctx else None, s2_ctx(ci, n0 + ci), items_gen(k2 + v1)])
        A.reset(mk_a)
        return mk, oT, oT_T, xT_extra

    def mix_ln_phase(mk, oT, oT_T, xT_extra):
        w_out_v = w["w_out"].rearrange("(ko ki) f -> ki ko f", ki=128)
        wob, r = A.alloc("wob", [128, 8, D], BF16)
        t_wob = [A.newT(r, f"wob{j}") for j in range(2)]
        for j in range(2):
            P.dma("pool", lambda e, j=j: e.dma_start(out=wob[:, :, j * 512:(j + 1) * 512], in_=w_out_v[:, 8:16, j * 512:(j + 1) * 512]), writes=[t_wob[j]])
        lng, t_lng = al("lng2", [128, D])
        lnb, t_lnb = al("lnb2", [128, D])
        P.dma("sp", lambda e: e.dma_start(out=lng[:], in_=lnp["ln2_g"][:, :]), writes=[t_lng])
        P.dma("sp", lambda e: e.dma_start(out=lnb[:], in_=lnp["ln2_b"][:, :]), writes=[t_lnb])
        bufs = {}
        for nm, shape, dt, nb in [("xres", [128, D], F32, 2), ("yv", [128, D], F32, 2), ("xo", [128, D], F32, 2), ("xob", [128, D], BF16, 2),
                                  ("st", [128, 12], F32, 2), ("mv", [128, 2], F32, 2), ("rs", [128, 2], F32, 2)]:
            bufs[nm] = [al(f"b3{nm}{i}", shape, dt) for i in range(nb)]
        def down3(s):
            t0, m = SUBT[s]
            xres, t_xres = bufs["xres"][s % 2]
            yv, t_yv = bufs["yv"][s % 2]
            P.dma("sp", lambda e: e.dma_start(out=xres[0:m, :], in_=r1d[t0:t0 + m, :]), reads=[r1d_T[s]], writes=[t_xres])
            for half in range(2):
                bk = 4 * (s % 2) + half

                def mmo(e, bk=bk, half=half):
                    for k in range(8):
                        i = e.matmul(bank(bk)[0:m, :], lhsT=oT[:, k, t0:t0 + m], rhs=wob[:, k, half * 512:(half + 1) * 512], start=(k == 0), stop=(k == 7))
                    return i
                P.op("pe", mmo, reads=[oT_T[h][s] for h in range(8)] + [t_wob[half]], writes=[bankT[bk]])
                yield
                P.op("dve", lambda e, bk=bk, half=half: e.tensor_tensor(
                    out=yv[0:m, half * 512:(half + 1) * 512], in0=bank(bk)[0:m, :], in1=xres[0:m, half * 512:(half + 1) * 512], op=ALU.add),
                    reads=[bankT[bk], t_xres], writes=[t_yv])
                yield

        def tail3(s):
            t0, m = SUBT[s]
            yv, t_yv = bufs["yv"][s % 2]
            return ln_tail(s, t0, m, yv, t_yv, bufs, lng, t_lng, lnb, t_lnb, x2d, x2d_T, True, xT_extra)

        interleave_g([down3(0)])
        for s in range(17):
            interleave_g([down3(s + 1) if s + 1 < 17 else None, tail3(s)])
        A.reset(mk)

    eps_t, r = A.alloc("eps_t", [128, 1], F32)
    t_eps = A.newT(r, "eps")
    P.op("dve", lambda e: e.memset(eps_t[:], EPS), writes=[t_eps])

    x1d_T = [T(f"x1d{s}") for s in range(17)]
    r1d_T = [T(f"r1d{s}") for s in range(17)]
    ffn_phase("f1", w["ffn1_w_gate"], w["ffn1_w_up"], w["ffn1_w_down"], lnp["ln1_g"], lnp["ln1_b"], xin, None, x1d, x1d_T, True)
    x2d_T = [T(f"x2d{s}") for s in range(17)]
    ssd_phase()
    if not STOP_B1:
        mix_ln_phase(*mla_phase())
        ffn_phase("f2", w["ffn2_w_gate"], w["ffn2_w_up"], w["ffn2_w_down"], lnp["ln3_g"], lnp["ln3_b"], x2d, x2d_T, y_out, None, False)

    P.build()
    nc._prog_stats = P.stats
    return nc


_NC_CACHE = {}


def _get_nc(debug=False):
    if debug not in _NC_CACHE:
        _NC_CACHE[debug] = build_program(debug)
    return _NC_CACHE[debug]


def make_in_maps(inputs):
    f32 = np.float32
    g = {k: np.asarray(v) for k, v in inputs.items()}
    shared = {}
    for nm in ["ffn1_w_gate", "ffn1_w_up", "ffn1_w_down", "ffn2_w_gate", "ffn2_w_up", "ffn2_w_down",
               "w_in", "w_uq", "w_ukv", "w_out"]:
        shared[nm] = np.ascontiguousarray(g[nm][0], dtype=f32)
    for nm in ["ln1_g", "ln1_b", "ln2_g", "ln2_b", "ln3_g", "ln3_b"]:
        shared[nm] = np.ascontiguousarray(np.broadcast_to(g[nm][0][None, :], (128, D)), dtype=f32)
    shared["ident"] = np.eye(128, dtype=f32)
    cwv = g["conv_w"][0]
    shared["cwp"] = np.ascontiguousarray(cwv.reshape(4, 12, 128).transpose(2, 1, 0), dtype=f32)
    shared["cbp"] = np.ascontiguousarray(g["conv_b"][0].reshape(12, 128).T, dtype=f32)
    for nm, key in [("dtb", "dt_bias"), ("alog", "a_log"), ("dsk", "d_skip")]:
        shared[nm] = np.ascontiguousarray(np.broadcast_to(g[key][0][None, :], (128, 16)), dtype=f32)
    shared["gss"] = np.ascontiguousarray(np.broadcast_to(g["ssd_norm_g"][0][None, :], (128, D)), dtype=f32)
    j = np.arange(128)[:, None]
    i = np.arange(128)[None, :]
    csts = np.zeros((128, 8, 128), f32)
    for v, ch in enumerate((64, 32)):
        same = (j // ch) == (i // ch)
        lim = 128 if ch == 64 else 64
        ok = (j < lim) & (i < lim)
        csts[:, 4 * v + 0, :] = (same & (j <= i) & ok)
        csts[:, 4 * v + 1, :] = (same & ok)
        csts[:, 4 * v + 2, :] = ((j // ch) == 0) & (j < lim)
        csts[:, 4 * v + 3, :] = ((j // ch) == 1) & (j < lim)
    shared["csts"] = csts
    nmp = np.where(((j // 64) == (i // 64)) & (i >= j), 0.0, -30000.0).astype(f32)
    shared["nmp"] = np.ascontiguousarray(np.tile(nmp, (1, 4)))
    nms = np.where(((j // 32) == (i // 32)) & (i >= j), 0.0, -30000.0).astype(f32)[:, 0:64]
    shared["nms"] = np.ascontiguousarray(np.tile(nms, (1, 4)))
    idl = np.zeros((128, 2, 64), f32)
    kk = np.arange(128)
    idl[kk, 0, kk % 64] = 1.0
    idl[kk[:64], 1, kk[:64] % 32] = 1.0
    shared["idl"] = idl
    il = np.arange(64)[None, :]
    nmlp = np.where(il >= (kk[:, None] % 64), 0.0, -30000.0).astype(f32)
    shared["nmlp"] = np.ascontiguousarray(np.tile(nmlp, (1, 4)))
    nmls = np.where(il[:, :32] >= (kk[:, None] % 32), 0.0, -30000.0).astype(f32)
    shared["nmls"] = np.ascontiguousarray(np.tile(nmls, (1, 4)))
    shared["gq"] = np.ascontiguousarray(np.broadcast_to(g["q_norm_g"][0][None, :], (128, 512)), dtype=f32)
    shared["gkv"] = np.ascontiguousarray(np.broadcast_to(g["kv_norm_g"][0][None, :], (128, 512)), dtype=f32)
    pos = np.concatenate([np.arange(NPROMPT), PAST + np.arange(32), PAST + np.arange(32)]).astype(np.float64)
    inv = 10000.0 ** (-np.arange(0, 64, 2, dtype=np.float64) / 64.0)
    ang = pos[:, None] * inv[None, :]
    cs_, sn_ = np.cos(ang).astype(f32), np.sin(ang).astype(f32)
    shared["cs4"] = np.ascontiguousarray(np.concatenate([cs_, sn_, sn_, cs_], axis=1))
    shared["cosT"] = np.ascontiguousarray(np.concatenate([cs_, cs_], axis=1).T)
    shared["sinT"] = np.ascontiguousarray(np.concatenate([sn_, sn_], axis=1).T)
    shared["maskq"] = np.where((j < 64) & (i >= 64), -30000.0, 0.0).astype(f32)
    maps = []
    for c in range(8):
        m = dict(shared)
        m["xin"] = np.ascontiguousarray(np.concatenate(
            [g["x_prompt"][c], g["x_sample"][2 * c], g["x_sample"][2 * c + 1]], axis=0), dtype=f32)
        m["sconv"] = np.ascontiguousarray(g["state_conv"][0, 2 * c:2 * c + 2], dtype=f32)
        m["sssm"] = np.ascontiguousarray(g["state_ssm"][0, 2 * c:2 * c + 2], dtype=f32)
        m["clat"] = np.ascontiguousarray(g["cache_latent"][0, 2 * c:2 * c + 2], dtype=f32)
        m["ckr"] = np.ascontiguousarray(g["cache_k_rope"][0, 2 * c:2 * c + 2], dtype=f32)
        maps.append(m)
    return maps


def kernel(**inputs):
    nc = _get_nc(False)
    maps = make_in_maps(inputs)
    res = run_bass_kernel_spmd(nc, maps, core_ids=list(range(8)))
    rr = res.results
    f32 = np.float32

    def prm(key, n0=NPROMPT):
        return np.stack([np.asarray(rr[c][key][0:n0], f32) for c in range(8)], axis=0)

    def smp(key):
        return np.stack([np.asarray(rr[c][key][NPROMPT + 32 * j:NPROMPT + 32 * (j + 1)], f32) for c in range(8) for j in range(2)], axis=0)

    y_p, y_s = prm("y"), smp("y")
    lat_p, lat_s = prm("lat_o")[None], smp("lat_o")[None]
    kr_p, kr_s = prm("kr_o")[None], smp("kr_o")[None]
    conv_p = np.stack([np.asarray(rr[c]["conv_o"][0], f32) for c in range(8)], axis=0)[None]
    conv_s = np.stack([np.asarray(rr[c]["conv_o"][1 + j], f32) for c in range(8) for j in range(2)], axis=0)[None]
    ssm_p = np.stack([np.asarray(rr[c]["ssm_o"][0], f32) for c in range(8)], axis=0)[None]
    ssm_s = np.stack([np.asarray(rr[c]["ssm_o"][1 + j], f32) for c in range(8) for j in range(2)], axis=0)[None]
    return (y_p, y_s, lat_p, kr_p, conv_p, ssm_p, lat_s, kr_s, conv_s, ssm_s)
```

```python
import math
from contextlib import ExitStack
import numpy as np
import concourse.bass as bass
import concourse.mybir as mybir
from concourse.bass_utils import run_bass_kernel_spmd

F32 = mybir.dt.float32
BF16 = mybir.dt.bfloat16
AF = mybir.ActivationFunctionType
ALU = mybir.AluOpType
AX = mybir.AxisListType

D = 1024
DFF = 2816
NFF = DFF // 128
NPROMPT = 2048
NSAMP = 64
NT = NPROMPT + NSAMP
PAST = 1024
ALPHA = 2.0 ** 0.25
EPS = 1e-5
DINP = 3664
SUBT = [(128 * s, 128) for s in range(16)] + [(2048, 64)]
TILES = [(512 * t, 512) for t in range(4)] + [(2048, 64)]
SUB_OF_TILE = [[4 * t + i for i in range(4)] for t in range(4)] + [[16]]
ENGS = ("pe", "act", "dve", "pool", "sp")
STOP_B1 = False
MLA_LEVEL = 9
USE_XBAR = False
SSD_W = (1, 2)
ATT_W = (1, 1, 1)
SSD_K = 8
NHEADS_DBG = 8


class T:
    __slots__ = ("name", "w", "r", "after", "excl")

    def __init__(self, name, after=None, excl=False):
        self.name = name
        self.w = None
        self.r = []
        self.after = after
        self.excl = excl


class Op:
    __slots__ = ("eng", "fn", "reads", "writes", "dma", "sig", "deps", "sem", "val", "idx")

    def __init__(self, eng, fn, reads, writes, dma):
        self.eng = eng
        self.fn = fn
        self.reads = reads
        self.writes = writes
        self.dma = dma
        self.sig = dma
        self.deps = []
        self.sem = None
        self.val = 0


class Prog:
    def __init__(self, nc, n_dma_sems=16):
        self.nc = nc
        self.ops = []
        self.n_dma_sems = n_dma_sems
        self.xbar = T("xbar")

    def op(self, eng, fn, reads=(), writes=()):
        o = Op(eng, fn, list(reads), list(writes), False)
        self.ops.append(o)
        return o

    def dma(self, eng, fn, reads=(), writes=()):
        o = Op(eng, fn, list(reads) + [self.xbar], list(writes), True)
        self.ops.append(o)
        return o

    def dmat(self, eng, fn, reads=(), writes=()):
        o = Op(eng, fn, list(reads), list(writes) + [self.xbar], True)
        self.ops.append(o)
        return o

    def build(self):
        nc = self.nc
        for i, x in enumerate(self.ops):
            x.idx = i
            deps = {}
            for t in x.reads + x.writes:
                if t.after is not None:
                    for a in t.after:
                        if a.w is not None:
                            deps[a.w.idx] = a.w
                        for r in a.r:
                            deps[r.idx] = r
                    t.after = None
            for t in x.reads:
                if t.w is not None:
                    deps[t.w.idx] = t.w
                if t.excl:
                    for r in t.r:
                        if r.eng != x.eng:
                            deps[r.idx] = r
            for t in x.writes:
                if t.w is not None:
                    deps[t.w.idx] = t.w
                for r in t.r:
                    if r.eng == x.eng == "pe" and not r.dma and not x.dma:
                        continue
                    deps[r.idx] = r
            deps.pop(i, None)
            for y in deps.values():
                if y.eng == "pe" and x.eng == "pe" and not y.dma and not x.dma:
                    continue
                y.sig = True
                x.deps.append(y)
            for t in x.reads:
                t.r.append(x)
            for t in x.writes:
                t.w = x
                t.r = []
        with ExitStack() as es:
            esem = {e: es.enter_context(nc.semaphore(f"s_{e}")) for e in ENGS}
            dq = ("sp", "pool", "act")
            dsem = {e: [es.enter_context(nc.semaphore(f"d_{e}{k}")) for k in range(self.n_dma_sems)] for e in dq}
            cnt = {e: 0 for e in ENGS}
            dcnt = {e: [0] * self.n_dma_sems for e in dq}
            drr = {e: 0 for e in dq}
            prev_on_sem = {}
            for x in self.ops:
                if x.dma:
                    k = drr[x.eng]
                    drr[x.eng] = (k + 1) % (8 if x.eng == "pool" else self.n_dma_sems)
                    dcnt[x.eng][k] += 16
                    x.sem = dsem[x.eng][k]
                    x.val = dcnt[x.eng][k]
                    key = (x.eng, k)
                    if key in prev_on_sem:
                        x.deps.append(prev_on_sem[key])
                    prev_on_sem[key] = x
                elif x.sig:
                    cnt[x.eng] += 1
                    x.sem = esem[x.eng]
                    x.val = cnt[x.eng]
            self.stats = dict(cnt=dict(cnt), dmax={e: max(dcnt[e]) for e in dq}, nops=len(self.ops))
            per_eng = {e: [o for o in self.ops if o.eng == e] for e in ENGS}
            finals = [(esem[e], cnt[e]) for e in ENGS if cnt[e]]
            for e in dq:
                finals += [(dsem[e][k], dcnt[e][k]) for k in range(self.n_dma_sems) if dcnt[e][k]]

            def emit(e, eng):
                seen = {}
                for x in per_eng[e]:
                    need = {}
                    for y in x.deps:
                        sid = id(y.sem)
                        if seen.get(sid, 0) < y.val and need.get(sid, (None, 0))[1] < y.val:
                            need[sid] = (y.sem, y.val)
                    for sid, (s, v) in need.items():
                        eng.wait_ge(s, v)
                        seen[sid] = v
                    inst = x.fn(eng)
                    if x.sig:
                        inst.then_inc(x.sem, 16 if x.dma else 1)
                if e == "sp":
                    for s, v in finals:
                        if seen.get(id(s), 0) < v:
                            eng.wait_ge(s, v)

            with nc.Block() as block:
                @block.tensor
                def _(eng):
                    emit("pe", eng)

                @block.scalar
                def _(eng):
                    emit("act", eng)

                @block.vector
                def _(eng):
                    emit("dve", eng)

                @block.gpsimd
                def _(eng):
                    emit("pool", eng)

                @block.sync
                def _(eng):
                    emit("sp", eng)


class Arena:
    def __init__(self, nc, lo, hi):
        self.nc = nc
        self.lo = lo
        self.hi = hi
        self.cur = lo
        self.live = []
        self.dead = []
        self.n = 0

    def alloc(self, name, shape, dtype):
        esz = 2 if dtype == BF16 else 4
        nbytes = int(np.prod(shape[1:])) * esz
        off = (self.cur + 63) // 64 * 64
        assert off + nbytes <= self.hi, f"SBUF overflow at {name}: {off + nbytes} > {self.hi}"
        self.cur = off + nbytes
        self.n += 1
        h = self.nc.alloc_sbuf_tensor_at(f"{name}_{self.n}", list(shape), dtype, offset=off)
        rec = (off, off + nbytes, [])
        self.live.append(rec)
        return h, rec

    def newT(self, rec, name):
        lo, hi, lst = rec
        after = []
        for (dlo, dhi, dl) in self.dead:
            if dlo < hi and lo < dhi:
                after += dl
        t = T(name, after if after else None)
        lst.append(t)
        return t

    def mark(self):
        return (self.cur, len(self.live))

    def reset(self, m):
        cur, n = m
        self.dead += self.live[n:]
        del self.live[n:]
        self.cur = cur


def build_program(debug=False):
    nc = bass.Bass("TRN2", target_bir_lowering=False)

    def din(name, shape, dt=F32):
        return nc.dram_tensor(name, list(shape), dt, kind="ExternalInput").ap()

    def dout(name, shape, dt=F32):
        return nc.dram_tensor(name, list(shape), dt, kind="ExternalOutput").ap()

    def dscr(name, shape, dt=F32):
        return nc.dram_tensor(name, list(shape), dt, kind="ExternalOutput" if debug else "Internal").ap()

    xin = din("xin", [NT, D])
    w = {}
    for nm, shp in [("ffn1_w_gate", [D, DFF]), ("ffn1_w_up", [D, DFF]), ("ffn1_w_down", [DFF, D]),
                    ("ffn2_w_gate", [D, DFF]), ("ffn2_w_up", [D, DFF]), ("ffn2_w_down", [DFF, D]),
                    ("w_in", [D, DINP]), ("w_uq", [512, 1536]), ("w_ukv", [512, 2048]), ("w_out", [2048, D])]:
        w[nm] = din(nm, shp)
    lnp = {nm: din(nm, [128, D]) for nm in ["ln1_g", "ln1_b", "ln2_g", "ln2_b", "ln3_g", "ln3_b"]}
    ident_d = din("ident", [128, 128])

    cwp_d = din("cwp", [128, 12, 4])
    cbp_d = din("cbp", [128, 12])
    dtb_d = din("dtb", [128, 16])
    alog_d = din("alog", [128, 16])
    dsk_d = din("dsk", [128, 16])
    gss_d = din("gss", [128, D])
    csts_d = din("csts", [128, 8, 128])
    nmp_d = din("nmp", [128, 512])
    nms_d = din("nms", [128, 256])
    idl_d = din("idl", [128, 2, 64])
    nmlp_d = din("nmlp", [128, 256])
    nmls_d = din("nmls", [128, 128])
    sconv_d = din("sconv", [2, 3, 1536])
    sssm_d = din("sssm", [2, 16, 64, 128])

    gq_d = din("gq", [128, 512])
    gkv_d = din("gkv", [128, 512])
    clat_d = din("clat", [2, PAST, 512])
    ckr_d = din("ckr", [2, PAST, 64])
    cs4_d = din("cs4", [NT, 128])
    cosT_d = din("cosT", [64, NT])
    sinT_d = din("sinT", [64, NT])
    maskq_d = din("maskq", [128, 128])

    y_out = dout("y", [NT, D])
    lat_o = dout("lat_o", [NT, 512])
    kr_o = dout("kr_o", [NT, 64])
    x2d = dscr("x2d", [NT, D])
    conv_o = dout("conv_o", [3, 3, 1536])
    ssm_o = dout("ssm_o", [3, 16, 64, 128])
    x1d = dscr("x1d", [NT, D])
    r1d = dscr("r1d", [NT, D])
    dbg_y = dout("dbg_y", [NT, D]) if debug else None
    dbg_yn = dout("dbg_yn", [NT, D], BF16) if debug else None

    P = Prog(nc)
    A = Arena(nc, 16512, 229344)

    psA = nc.alloc_psum_tensor("psA", [128, 2048], F32)
    psB = nc.alloc_psum_tensor("psB", [128, 2048], F32)

    def bank(i):
        t = psA if i < 4 else psB
        j = i % 4
        return t[:, j * 512:(j + 1) * 512]

    bankT = [T(f"bank{i}", excl=True) for i in range(8)]

    ident_f, r = A.alloc("ident_f", [128, 128], F32)
    t_identf = A.newT(r, "ident_f")
    ident_b, r = A.alloc("ident_b", [128, 128], BF16)
    t_identb = A.newT(r, "ident_b")
    P.dma("sp", lambda e: e.dma_start(out=ident_f[:], in_=ident_d[:, :]), writes=[t_identf])
    P.dma("pool", lambda e: e.dma_start(out=ident_b[:], in_=ident_d[:, :]), writes=[t_identb])

    xT, r = A.alloc("xT", [128, 8, NT], BF16)
    xT_rec = r
    xT_T = [A.newT(r, f"xT{s}") for s in range(17)]

    m0 = A.mark()
    xb = []
    for i in range(2):
        h, r = A.alloc(f"xb{i}", [128, D], BF16)
        xb.append((h, A.newT(r, f"xb{i}")))
    for s, (t0, m) in enumerate(SUBT):
        h, th = xb[s % 2]
        P.dma("pool", lambda e, h=h, t0=t0, m=m: e.dma_start(out=h[0:m, :], in_=xin[t0:t0 + m, :]), writes=[th])
        bk = 6 + (s % 2)
        pv = bank(bk).bitcast(BF16)

        def tr(e, h=h, m=m, pv=pv):
            for k in range(8):
                i = e.transpose(out=pv[:, k * 128:k * 128 + m], in_=h[0:m, k * 128:(k + 1) * 128], identity=ident_b[0:m, 0:m])
            return i
        P.op("pe", tr, reads=[th, t_identb], writes=[bankT[bk]])
        src = pv.rearrange("p (k t) -> p k t", k=8)[:, :, 0:m]
        if s % 2 == 0:
            P.op("act", lambda e, src=src, t0=t0, m=m: e.activation(out=xT[:, :, t0:t0 + m], in_=src, func=AF.Copy),
                 reads=[bankT[bk]], writes=[xT_T[s]])
        else:
            P.op("dve", lambda e, src=src, t0=t0, m=m: e.tensor_copy(out=xT[:, :, t0:t0 + m], in_=src),
                 reads=[bankT[bk]], writes=[xT_T[s]])
    A.reset(m0)


    def interleave_g(gens):
        gens = [g for g in gens if g is not None]
        while gens:
            for g in list(gens):
                try:
                    next(g)
                except StopIteration:
                    gens.remove(g)

    def ln_tail(s, t0, m, yv, t_yv, bufs, lng, t_lng, lnb, t_lnb, out_d, out_T, make_xT, extra_w=()):
        xo, t_xo = bufs["xo"][s % 2]
        xob, t_xob = bufs["xob"][s % 2]
        st, t_st = bufs["st"][s % 2]
        mv, t_mv = bufs["mv"][s % 2]
        rs, t_rs = bufs["rs"][s % 2]
        for half in range(2):
            P.op("dve", lambda e, half=half: e.bn_stats(out=st[0:m, half * 6:(half + 1) * 6], in_=yv[0:m, half * 512:(half + 1) * 512]),
                 reads=[t_yv], writes=[t_st])
        P.op("dve", lambda e: e.bn_aggr(out=mv[0:m, :], in_=st[0:m, :]), reads=[t_st], writes=[t_mv])
        yield
        P.op("act", lambda e: e.activation(out=rs[0:m, 0:1], in_=mv[0:m, 1:2], func=AF.Sqrt, bias=eps_t[0:m, :]),
             reads=[t_mv, t_eps], writes=[t_rs])
        yield
        P.op("dve", lambda e: e.reciprocal(out=rs[0:m, 1:2], in_=rs[0:m, 0:1]), reads=[t_rs], writes=[t_rs])
        P.op("dve", lambda e: e.tensor_scalar(out=yv[0:m, :], in0=yv[0:m, :], scalar1=mv[0:m, 0:1], scalar2=rs[0:m, 1:2],
                                              op0=ALU.subtract, op1=ALU.mult), reads=[t_yv, t_mv, t_rs], writes=[t_yv])
        P.op("dve", lambda e: e.tensor_tensor(out=yv[0:m, :], in0=yv[0:m, :], in1=lng[0:m, :], op=ALU.mult), reads=[t_yv, t_lng], writes=[t_yv])
        P.op("dve", lambda e: e.tensor_tensor(out=xo[0:m, :], in0=yv[0:m, :], in1=lnb[0:m, :], op=ALU.add), reads=[t_yv, t_lnb], writes=[t_xo])
        yield
        P.dma("sp", lambda e: e.dma_start(out=out_d[t0:t0 + m, :], in_=xo[0:m, :]), reads=[t_xo], writes=([out_T[s]] if out_T else []))
        if make_xT:
            P.op("act", lambda e: e.activation(out=xob[0:m, :], in_=xo[0:m, :], func=AF.Copy), reads=[t_xo], writes=[t_xob])
            bk = 6 + (s % 2)
            pv = bank(bk).bitcast(BF16)

            def tr(e):
                for k in range(8):
                    i = e.transpose(out=pv[:, k * 128:k * 128 + m], in_=xob[0:m, k * 128:(k + 1) * 128], identity=ident_b[0:m, 0:m])
                return i
            yield
            P.op("pe", tr, reads=[t_xob, t_identb], writes=[bankT[bk]])
            yield
            src = pv.rearrange("p (k t) -> p k t", k=8)[:, :, 0:m]
            P.op("act", lambda e: e.activation(out=xT[:, :, t0:t0 + m], in_=src, func=AF.Copy),
                 reads=[bankT[bk]], writes=[xT_T[s]] + list(extra_w))
        yield

    def ffn_phase(tag, wg_d, wu_d, wd_d, g_d, b_d, res_d, res_T, out_d, out_T, make_xT):
        mk = A.mark()
        lng, r = A.alloc("lng", [128, D], F32)
        t_lng = A.newT(r, "lng")
        lnb, r = A.alloc("lnb", [128, D], F32)
        t_lnb = A.newT(r, "lnb")
        P.dma("sp", lambda e: e.dma_start(out=lng[:], in_=g_d[:, :]), writes=[t_lng])
        P.dma("sp", lambda e: e.dma_start(out=lnb[:], in_=b_d[:, :]), writes=[t_lnb])
        hT, r = A.alloc("hT", [128, NFF, NT], BF16)
        hT_T = [[A.newT(r, f"hT{c}_{t}") for t in range(5)] for c in range(NFF)]
        wd, r = A.alloc("wd", [128, NFF, D], BF16)
        wd_T = [A.newT(r, f"wd{c}") for c in range(NFF)]
        mk2 = A.mark()
        wgb, wub, sgb = [], [], []
        for i in range(2):
            h, r = A.alloc(f"wg{i}", [128, 8, 256], BF16)
            wgb.append((h, A.newT(r, f"wg{i}")))
            h, r = A.alloc(f"wu{i}", [128, 8, 256], BF16)
            wub.append((h, A.newT(r, f"wu{i}")))
        for i in range(2):
            h, r = A.alloc(f"sg{i}", [128, 512], F32)
            sgb.append((h, A.newT(r, f"sg{i}")))
        wg_v = wg_d.rearrange("(ko ki) f -> ki ko f", ki=128)
        wu_v = wu_d.rearrange("(ko ki) f -> ki ko f", ki=128)
        wd_v = wd_d.rearrange("(c p) d -> p c d", p=128)

        def load_block(b):
            hg, tg = wgb[b % 2]
            hu, tu = wub[b % 2]
            P.dma("pool", lambda e: e.dma_start(out=hg[:], in_=wg_v[:, :, b * 256:(b + 1) * 256]), writes=[tg])
            P.dma("pool", lambda e: e.dma_start(out=hu[:], in_=wu_v[:, :, b * 256:(b + 1) * 256]), writes=[tu])

        load_block(0)
        it = 0
        for b in range(NFF // 2):
            if b + 1 < NFF // 2:
                load_block(b + 1)
            for c in (2 * b, 2 * b + 1):
                P.dma("pool", lambda e, c=c: e.dma_start(out=wd[:, c, :], in_=wd_v[:, c, :]), writes=[wd_T[c]])
            hg, tg = wgb[b % 2]
            hu, tu = wub[b % 2]
            for hc in range(2):
                c = 2 * b + hc
                for t, (t0, n) in enumerate(TILES):
                    bg, bu = (it % 2), 2 + (it % 2)
                    sg, tsg = sgb[it % 2]
                    it += 1
                    xr = [xT_T[s] for s in SUB_OF_TILE[t]]

                    def mm(e, wt, bk, t0=t0, n=n, hc=hc):
                        for k in range(8):
                            i = e.matmul(bank(bk)[:, 0:n], lhsT=wt[:, k, hc * 128:(hc + 1) * 128], rhs=xT[:, k, t0:t0 + n],
                                         start=(k == 0), stop=(k == 7))
                        return i
                    P.op("pe", lambda e, mm=mm, hg=hg, bg=bg: mm(e, hg, bg), reads=xr + [tg], writes=[bankT[bg]])
                    P.op("pe", lambda e, mm=mm, hu=hu, bu=bu: mm(e, hu, bu), reads=xr + [tu], writes=[bankT[bu]])
                    P.op("act", lambda e, sg=sg, bg=bg, n=n: e.activation(out=sg[:, 0:n], in_=bank(bg)[:, 0:n], func=AF.Silu),
                         reads=[bankT[bg]], writes=[tsg])
                    P.op("dve", lambda e, sg=sg, bu=bu, n=n, c=c, t0=t0: e.tensor_tensor(
                        out=hT[:, c, t0:t0 + n], in0=sg[:, 0:n], in1=bank(bu)[:, 0:n], op=ALU.mult),
                        reads=[tsg, bankT[bu]], writes=[hT_T[c][t]])
        A.reset(mk2)
        bufs = {}
        for nm, shape, dt, nb in [("xres", [128, D], F32, 2), ("yv", [128, D], F32, 2),
                                  ("xo", [128, D], F32, 2), ("xob", [128, D], BF16, 2),
                                  ("st", [128, 12], F32, 2), ("mv", [128, 2], F32, 2), ("rs", [128, 2], F32, 2)]:
            bufs[nm] = []
            for i in range(nb):
                h, r = A.alloc(f"{nm}{i}", shape, dt)
                bufs[nm].append((h, A.newT(r, f"{nm}{i}")))
        def down(s):
            t0, m = SUBT[s]
            t = min(s // 4, 4)
            xres, t_xres = bufs["xres"][s % 2]
            xa, t_xa = xres, t_xres
            yv, t_yv = bufs["yv"][s % 2]
            P.dma("sp", lambda e: e.dma_start(out=xres[0:m, :], in_=res_d[t0:t0 + m, :]),
                  reads=([res_T[s]] if res_T else []), writes=[t_xres])
            P.op("act", lambda e: e.activation(out=xa[0:m, :], in_=xres[0:m, :], func=AF.Copy, scale=ALPHA), reads=[t_xres], writes=[t_xa])
            for half in range(2):
                bk = 4 * (s % 2) + half

                def mmd(e, bk=bk, half=half):
                    for c in range(NFF):
                        i = e.matmul(bank(bk)[0:m, :], lhsT=hT[:, c, t0:t0 + m], rhs=wd[:, c, half * 512:(half + 1) * 512],
                                     start=(c == 0), stop=(c == NFF - 1))
                    return i
                P.op("pe", mmd, reads=[hT_T[c][t] for c in range(NFF)] + wd_T, writes=[bankT[bk]])
                yield
                P.op("dve", lambda e, bk=bk, half=half: e.scalar_tensor_tensor(
                    out=yv[0:m, half * 512:(half + 1) * 512], in0=bank(bk)[0:m, :], scalar=0.5,
                    in1=xa[0:m, half * 512:(half + 1) * 512], op0=ALU.mult, op1=ALU.add),
                    reads=[bankT[bk], t_xa], writes=[t_yv])
                yield

        def tail(s):
            t0, m = SUBT[s]
            yv, t_yv = bufs["yv"][s % 2]
            return ln_tail(s, t0, m, yv, t_yv, bufs, lng, t_lng, lnb, t_lnb, out_d, out_T, make_xT)

        interleave_g([down(0)])
        for s in range(17):
            interleave_g([down(s + 1) if s + 1 < 17 else None, tail(s)])
        A.reset(mk)

    def ncdma(e, out, in_, k):
        with nc.allow_non_contiguous_dma(reason="small strided state transfer"):
            return e.dma_start(out=out[:, :, k:k + 1], in_=in_[:, :, k:k + 1])

    def al(name, shape, dt=F32):
        h, r = A.alloc(name, shape, dt)
        return h, A.newT(r, name)

    def ssd_phase():
        mk = A.mark()
        w_in_v = w["w_in"].rearrange("(ko ki) f -> ki ko f", ki=128)
        w_out_v = w["w_out"].rearrange("(ko ki) f -> ki ko f", ki=128)
        wz, r = A.alloc("wz", [128, 8, 1024], BF16)
        t_wz = [A.newT(r, f"wz{j}") for j in range(2)]
        wx, r = A.alloc("wx", [128, 8, 1536], BF16)
        t_wx = [A.newT(r, f"wx{j}") for j in range(3)]
        wdt, t_wdt = al("wdt", [128, 8, 16], BF16)
        wot, r = A.alloc("wot", [128, 8, 1024], BF16)
        t_wot = [A.newT(r, f"wot{j}") for j in range(2)]
        for j in range(3):
            P.dma("pool", lambda e, j=j: e.dma_start(out=wx[:, :, j * 512:(j + 1) * 512], in_=w_in_v[:, :, 1024 + j * 512:1024 + (j + 1) * 512]), writes=[t_wx[j]])
        for j in range(2):
            P.dma("pool", lambda e, j=j: e.dma_start(out=wz[:, :, j * 512:(j + 1) * 512], in_=w_in_v[:, :, j * 512:(j + 1) * 512]), writes=[t_wz[j]])
        P.dma("pool", lambda e: e.dma_start(out=wdt[:], in_=w_in_v[:, :, 2560:2576]), writes=[t_wdt])
        for j in range(2):
            P.dma("pool", lambda e, j=j: e.dma_start(out=wot[:, :, j * 512:(j + 1) * 512], in_=w_out_v[:, 0:8, j * 512:(j + 1) * 512]), writes=[t_wot[j]])
        cw, t_cw = al("cw", [128, 12, 4])
        cb, t_cb = al("cb", [128, 12])
        dtb, t_dtb = al("dtb", [128, 16])
        Abc, t_Abc = al("Abc", [128, 16])
        dsk, t_dsk = al("dsk", [128, 16])
        gss, t_gss = al("gss", [128, D])
        cst, t_cst = al("cst", [128, 8, 128])
        ones_f, t_ones = al("ones_f", [128, 128])
        one_t, t_one = al("one_t", [128, 1])
        nmp, t_nmp = al("nmp", [128, 512], BF16)
        nms, t_nms = al("nms", [128, 256], BF16)
        idl, t_idl = al("idl", [128, 2, 64])
        nmlp, t_nmlp = al("nmlp", [128, 256], BF16)
        nmls, t_nmls = al("nmls", [128, 128], BF16)
        P.dma("sp", lambda e: e.dma_start(out=idl[:], in_=idl_d), writes=[t_idl])
        P.dma("pool", lambda e: e.dma_start(out=nmlp[:], in_=nmlp_d[:, :]), writes=[t_nmlp])
        P.dma("pool", lambda e: e.dma_start(out=nmls[:], in_=nmls_d[:, :]), writes=[t_nmls])
        for (h, th, d_) in [(cw, t_cw, cwp_d), (cb, t_cb, cbp_d), (dtb, t_dtb, dtb_d), (Abc, t_Abc, alog_d), (dsk, t_dsk, dsk_d),
                            (gss, t_gss, gss_d), (cst, t_cst, csts_d)]:
            P.dma("sp", lambda e, h=h, d_=d_: e.dma_start(out=h[:], in_=d_), writes=[th])
        P.dma("pool", lambda e: e.dma_start(out=nmp[:], in_=nmp_d[:, :]), writes=[t_nmp])
        P.dma("pool", lambda e: e.dma_start(out=nms[:], in_=nms_d[:, :]), writes=[t_nms])
        P.op("pool", lambda e: e.memset(ones_f[:], 1.0), writes=[t_ones])
        P.op("pool", lambda e: e.memset(one_t[:], 1.0), writes=[t_one])
        P.op("act", lambda e: e.activation(out=Abc[:], in_=Abc[:], func=AF.Exp), reads=[t_Abc], writes=[t_Abc])
        P.op("dve", lambda e: e.tensor_scalar(out=Abc[:], in0=Abc[:], scalar1=-1.0, scalar2=None, op0=ALU.mult), reads=[t_Abc], writes=[t_Abc])
        S, t_S = al("S", [128, D])
        Sb, t_Sb = al("Sb", [128, D], BF16)
        P.op("pool", lambda e: e.memset(S[:], 0.0), writes=[t_S])
        P.op("pool", lambda e: e.memset(Sb[:], 0.0), writes=[t_Sb])
        Ss = [al(f"Ss{j}", [128, D]) for j in range(2)]
        Sbs = [al(f"Sbs{j}", [128, D], BF16) for j in range(2)]
        sio, t_sio = al("sio", [128, 8, 128])
        ssm_v = lambda ap: ap.rearrange("(b h2) p n -> (h2 p) b n", h2=2)

        def state_in(j):
            P.dma("sp", lambda e: e.dma_start(out=sio[:], in_=ssm_v(sssm_d[j])), writes=[t_sio])

            def tr(e):
                for b in range(8):
                    i = e.transpose(out=bank(4 + b // 4)[:, (b % 4) * 128:(b % 4 + 1) * 128], in_=sio[:, b, :], identity=ident_f[:])
                return i
            P.op("pe", tr, reads=[t_sio, t_identf], writes=[bankT[4], bankT[5]])
            P.op("act", lambda e: e.activation(out=Ss[j][0][:], in_=psB[:, 0:1024], func=AF.Copy), reads=[bankT[4], bankT[5]], writes=[Ss[j][1]])
            P.op("dve", lambda e: e.tensor_copy(out=Sbs[j][0][:], in_=psB[:, 0:1024]), reads=[bankT[4], bankT[5]], writes=[Sbs[j][1]])

        def state_out(Sx, t_Sx, idx):
            def tr(e):
                for b in range(8):
                    i = e.transpose(out=bank(4 + b // 4)[:, (b % 4) * 128:(b % 4 + 1) * 128], in_=Sx[:, b * 128:(b + 1) * 128], identity=ident_f[:])
                return i
            P.op("pe", tr, reads=[t_Sx, t_identf], writes=[bankT[4], bankT[5]])
            P.op("act", lambda e: e.activation(out=sio[:], in_=psB[:, 0:1024].rearrange("p (b n) -> p b n", b=8), func=AF.Copy),
                 reads=[bankT[4], bankT[5]], writes=[t_sio])
            P.dma("sp", lambda e: e.dma_start(out=ssm_v(ssm_o[idx]), in_=sio[:]), reads=[t_sio])

        state_in(0)
        state_in(1)
        xbcb = []
        for i in range(2):
            h, r = A.alloc(f"xbc{i}", [128, 12, 131], F32)
            xbcb.append((h, A.newT(r, f"xh{i}"), [A.newT(r, f"xq{i}_{q}") for q in range(3)]))
        acc, t_acc = al("acc", [128, 12, 128])
        t_accc = [T(f"acc{ch}") for ch in range(12)]
        bcb2 = [al(f"bcb{i}", [128, 4, 128], BF16) for i in range(2)]
        Btok2 = [al(f"Btok{i}", [128, 2, 128], BF16) for i in range(2)]
        sm = {nm: al("sm_" + nm, [128, 16]) for nm in ["dtv", "ab", "ll", "dt", "dA", "acs", "dd", "decst", "dtd", "lndt", "acsl"]}
        indec2 = [al(f"indec{i}", [128, 16]) for i in range(2)]
        cd2 = [al(f"cd{i}", [128, 2, 16]) for i in range(2)]
        x_dt, t_xdt = al("x_dt", [128, D], BF16)
        xd2 = [al(f"xd{i}", [128, D], BF16) for i in range(2)]
        xs_sb, t_xs = al("xs_sb", [128, D])
        xsD, t_xsD = al("xsD", [128, D], BF16)
        dg, t_dg = al("dg", [128, 16, 128])
        t_dgq = [T(f"dgq{q}", after=[t_dg]) for q in range(4)]
        Mm, t_M = al("Mm", [128, 16, 128], BF16)
        ysb, t_y = al("ysb", [128, D])
        sz2 = [al(f"sz{i}", [128, D]) for i in range(2)]
        ssq, t_ssq = al("ssq", [128, 4])
        yn, t_yn = al("yn", [128, D], BF16)
        ynT, t_ynT = al("ynT", [128, 8, 128], BF16)
        r1s, t_r1s = ysb, t_y
        junk = yn[:].bitcast(F32)
        t_junk = t_yn
        conv_v = lambda ap: ap.rearrange("k (c p) -> p c k", p=128)
        g_ = lambda nm: sm[nm][0]
        t_ = lambda nm: sm[nm][1]
        pv2 = bank(2).bitcast(BF16)
        pv6 = bank(6).bitcast(BF16)

        def prepA(s):
            t0, m = SUBT[s]
            p = s % 2
            samp = (s == 16)
            co = 4 if samp else 0
            CH = 32 if samp else 64
            vi = 1 if samp else 0
            xbc, t_xh, t_xq = xbcb[p]
            pxbc, t_pxh, t_pxq = xbcb[1 - p]
            bcb, t_bcb = bcb2[p]
            Btok, t_Btok = Btok2[p]
            indec, t_indec = indec2[p]
            cd, t_cd = cd2[p]
            xd, t_xd = xd2[p]
            sz, t_sz = sz2[p]
            xr = [xT_T[s]]
            segs = [(0, 32, 0), (35, 32, 32)] if samp else [(0, 128, 0)]
            if samp:
                for j in range(2):
                    for k in range(3):
                        P.dma("sp", lambda e, j=j, k=k: ncdma(e, xbc[:, :, j * 35:j * 35 + 3], conv_v(sconv_d[j]), k), writes=[t_xh])
            elif s == 0:
                P.op("pool", lambda e: e.memset(xbc[:, :, 0:3], 0.0), writes=[t_xh])
            else:
                P.op("pool", lambda e: e.tensor_copy(out=xbc[:, :, 0:3], in_=pxbc[:, :, 128:131]), reads=[t_pxq[2], t_pxq[1], t_pxq[0]], writes=[t_xh])
            yield
            for q in range(3):
                bq = 2 + (q % 2)

                def mmx(e, q=q, bq=bq):
                    for j in range(4):
                        ch = 4 * q + j
                        for k in range(8):
                            i = e.matmul(bank(bq)[:, j * 128:j * 128 + m], lhsT=wx[:, k, ch * 128:(ch + 1) * 128], rhs=xT[:, k, t0:t0 + m],
                                         start=(k == 0), stop=(k == 7), skip_group_check=True)
                    return i
                P.op("pe", mmx, reads=xr + t_wx, writes=[bankT[bq]])
                yield
                bv = bank(bq).rearrange("p (j t) -> p j t", j=4)
                if samp:
                    for j in range(2):
                        P.op("act", lambda e, q=q, j=j, bv=bv: e.activation(
                            out=xbc[:, 4 * q:4 * q + 4, j * 35 + 3:j * 35 + 35], in_=bv[:, :, j * 32:(j + 1) * 32], func=AF.Copy),
                            reads=[bankT[bq]], writes=[t_xq[q]])
                else:
                    P.op("act", lambda e, q=q, bv=bv: e.activation(out=xbc[:, 4 * q:4 * q + 4, 3:131], in_=bv[:, :, :], func=AF.Copy),
                         reads=[bankT[bq]], writes=[t_xq[q]])
                yield
            if s == 15:
                for k in range(3):
                    P.dma("sp", lambda e, k=k: ncdma(e, conv_v(conv_o[0]), xbc[:, :, 128:131], k), reads=t_xq)
            if samp:
                for j in range(2):
                    for k in range(3):
                        P.dma("sp", lambda e, j=j, k=k: ncdma(e, conv_v(conv_o[1 + j]), xbc[:, :, j * 35 + 32:j * 35 + 35], k), reads=t_xq)
            def mmdt(e):
                for k in range(8):
                    i = e.matmul(bank(3)[0:m, 0:16], lhsT=xT[:, k, t0:t0 + m], rhs=wdt[:, k, :], start=(k == 0), stop=(k == 7))
                return i
            P.op("pe", mmdt, reads=xr + [t_wdt], writes=[bankT[3]])
            yield
            P.op("dve", lambda e: e.tensor_tensor(out=g_("dtv")[0:m, :], in0=bank(3)[0:m, 0:16], in1=dtb[0:m, :], op=ALU.add),
                 reads=[bankT[3], t_dtb], writes=[t_("dtv")])
            yield
            P.op("act", lambda e: e.activation(out=g_("ab")[0:m, :], in_=g_("dtv")[0:m, :], func=AF.Abs), reads=[t_("dtv")], writes=[t_("ab")])
            P.op("act", lambda e: e.activation(out=g_("ab")[0:m, :], in_=g_("ab")[0:m, :], func=AF.Exp, scale=-1.0), reads=[t_("ab")], writes=[t_("ab")])
            P.op("act", lambda e: e.activation(out=g_("ll")[0:m, :], in_=g_("ab")[0:m, :], func=AF.Ln, bias=one_t[0:m, :]),
                 reads=[t_("ab"), t_one], writes=[t_("ll")])
            yield
            def conv_op(k, ch):
                if True:
                    for (c0, n, o0) in segs:
                        if k == 0:
                            P.op("dve", lambda e, ch=ch, c0=c0, n=n, o0=o0: e.tensor_scalar(
                                out=acc[:, ch, o0:o0 + n], in0=xbc[:, ch, c0:c0 + n], scalar1=cw[:, ch, 0:1], scalar2=cb[:, ch:ch + 1],
                                op0=ALU.mult, op1=ALU.add), reads=[t_xh, t_xq[ch // 4], t_cw, t_cb], writes=[t_accc[ch]])
                        else:
                            P.op("dve", lambda e, ch=ch, k=k, c0=c0, n=n, o0=o0: e.scalar_tensor_tensor(
                                out=acc[:, ch, o0:o0 + n], in0=xbc[:, ch, c0 + k:c0 + k + n], scalar=cw[:, ch, k:k + 1], in1=acc[:, ch, o0:o0 + n],
                                op0=ALU.mult, op1=ALU.add), reads=[t_xh, t_xq[ch // 4], t_cw, t_accc[ch]], writes=[t_accc[ch]])
            for k in range(2):
                for ch in range(12):
                    conv_op(k, ch)
                    if ch % 4 == 3:
                        yield
            if True:
                if True:
                    P.op("dve", lambda e: e.scalar_tensor_tensor(out=g_("dt")[0:m, :], in0=g_("dtv")[0:m, :], scalar=0.0, in1=g_("ll")[0:m, :],
                                                                 op0=ALU.max, op1=ALU.add), reads=[t_("dtv"), t_("ll")], writes=[t_("dt")])
                    P.op("dve", lambda e: e.tensor_tensor(out=g_("dA")[0:m, :], in0=g_("dt")[0:m, :], in1=Abc[0:m, :], op=ALU.mult),
                         reads=[t_("dt"), t_Abc], writes=[t_("dA")])
                    yield
                if True:
                    def mmacs(e):
                        e.matmul(bank(3)[0:m, 0:16], lhsT=cst[0:m, co + 0, 0:m], rhs=g_("dA")[0:m, :], start=True, stop=True, skip_group_check=True)
                        e.matmul(bank(3)[0:m, 16:32], lhsT=cst[0:m, co + 1, 0:m], rhs=g_("dA")[0:m, :], start=True, stop=True, skip_group_check=True)
                        e.matmul(bank(3)[:, 32:48], lhsT=cst[0:m, co + 2, :], rhs=g_("dA")[0:m, :], start=True, stop=True, skip_group_check=True)
                        return e.matmul(bank(3)[:, 48:64], lhsT=cst[0:m, co + 3, :], rhs=g_("dA")[0:m, :], start=True, stop=True, skip_group_check=True)
                    P.op("pe", mmacs, reads=[t_("dA"), t_cst], writes=[bankT[3]])
                    yield
                    P.op("act", lambda e: e.activation(out=g_("acs")[0:m, :], in_=bank(3)[0:m, 0:16], func=AF.Copy), reads=[bankT[3]], writes=[t_("acs")])
                    P.op("act", lambda e: e.activation(out=cd[:, :, :], in_=bank(3)[:, 32:64].rearrange("p (c h) -> p c h", c=2), func=AF.Exp),
                         reads=[bankT[3]], writes=[t_cd])
                    yield
                    P.op("dve", lambda e: e.tensor_tensor(out=g_("dd")[0:m, :], in0=bank(3)[0:m, 16:32], in1=g_("acs")[0:m, :], op=ALU.subtract),
                         reads=[bankT[3], t_("acs")], writes=[t_("dd")])
                    yield
                    P.op("act", lambda e: e.activation(out=g_("lndt")[0:m, :], in_=g_("dt")[0:m, :], func=AF.Ln), reads=[t_("dt")], writes=[t_("lndt")])
                    P.op("act", lambda e: e.activation(out=indec[0:m, :], in_=g_("acs")[0:m, :], func=AF.Exp), reads=[t_("acs")], writes=[t_indec])
                    P.op("act", lambda e: e.activation(out=g_("decst")[0:m, :], in_=g_("dd")[0:m, :], func=AF.Exp), reads=[t_("dd")], writes=[t_("decst")])
                    yield
                    P.op("dve", lambda e: e.tensor_tensor(out=g_("dtd")[0:m, :], in0=g_("dt")[0:m, :], in1=g_("decst")[0:m, :], op=ALU.mult),
                         reads=[t_("dt"), t_("decst")], writes=[t_("dtd")])
                    P.op("dve", lambda e: e.tensor_tensor(out=g_("acsl")[0:m, :], in0=g_("acs")[0:m, :], in1=g_("lndt")[0:m, :], op=ALU.subtract),
                         reads=[t_("acs"), t_("lndt")], writes=[t_("acsl")])
                    P.op("dve", lambda e: e.tensor_tensor(
                        out=dg[0:m, :, 0:CH], in0=idl[0:m, vi, 0:CH].unsqueeze(1).to_broadcast([m, 16, CH]),
                        in1=g_("acs")[0:m, :].unsqueeze(2).to_broadcast([m, 16, CH]), op=ALU.mult), reads=[t_idl, t_("acs")], writes=t_dgq)
                    yield
            nm_ = nmls if samp else nmlp
            t_nm = t_nmls if samp else t_nmlp
            rest_conv = [(k, ch) for k in (2, 3) for ch in range(12)]
            for q in range(4):
                bq = q % 2

                def mmR(e, q=q, bq=bq):
                    e.matmul(bank(bq)[0:m, 0:4 * CH], lhsT=cst[0:m, co + 1, 0:m], rhs=dg[0:m, 4 * q:4 * q + 4, 0:CH], start=True, stop=False)
                    return e.matmul(bank(bq)[0:m, 0:4 * CH], lhsT=ident_b[0:m, 0:m], rhs=nm_[0:m, 0:4 * CH], start=False, stop=True)
                P.op("pe", mmR, reads=[t_dgq[q], t_cst, t_identb, t_nm], writes=[bankT[bq]])
                yield
                for (k, ch) in rest_conv[6 * q:6 * q + 6]:
                    conv_op(k, ch)
                yield
                Rv = bank(bq)[0:m, 0:4 * CH].rearrange("p (h i) -> p h i", h=4)
                P.op("dve", lambda e, q=q, Rv=Rv: e.tensor_tensor(
                    out=dg[0:m, 4 * q:4 * q + 4, 0:CH], in0=Rv, in1=g_("acsl")[0:m, 4 * q:4 * q + 4].unsqueeze(2).to_broadcast([m, 4, CH]), op=ALU.subtract),
                    reads=[bankT[bq], t_("acsl")], writes=[t_dgq[q]])
                yield
            P.op("act", lambda e: e.activation(out=dg[0:m, :, 0:CH], in_=dg[0:m, :, 0:CH], func=AF.Exp), reads=t_dgq, writes=t_dgq)
            yield
            for half in range(2):
                def mmz(e, half=half):
                    for k in range(8):
                        i = e.matmul(bank(half)[0:m, :], lhsT=xT[:, k, t0:t0 + m], rhs=wz[:, k, half * 512:(half + 1) * 512], start=(k == 0), stop=(k == 7))
                    return i
                P.op("pe", mmz, reads=xr + [t_wz[half]], writes=[bankT[half]])
            yield
            P.op("act", lambda e: e.activation(out=sz[0:m, :], in_=psA[0:m, 0:1024], func=AF.Silu), reads=[bankT[0], bankT[1]], writes=[t_sz])
            P.op("act", lambda e: e.activation(out=bcb[:, :, 0:m], in_=acc[:, 8:12, 0:m], func=AF.Silu), reads=t_accc[8:12], writes=[t_bcb])
            P.op("act", lambda e: e.activation(out=acc[:, 0:8, 0:m], in_=acc[:, 0:8, 0:m], func=AF.Silu), reads=t_accc[0:8], writes=[t_acc] + t_accc[0:8])
            yield
            def trb(e):
                for g in range(2):
                    i = e.transpose(out=pv2[0:m, g * 128:(g + 1) * 128], in_=bcb[:, g, 0:m], identity=ident_b[:])
                return i
            P.op("pe", trb, reads=[t_bcb, t_identb], writes=[bankT[2]])

            def mmcb(e):
                for c in range(2):
                    for g in range(2):
                        i = e.matmul(bank(3)[c * CH:(c + 1) * CH, 128 + g * 64:128 + g * 64 + CH], lhsT=bcb[:, g, c * CH:(c + 1) * CH],
                                     rhs=bcb[:, 2 + g, c * CH:(c + 1) * CH], start=True, stop=True, skip_group_check=True)
                return i
            P.op("pe", mmcb, reads=[t_bcb], writes=[bankT[3]])

            def trx(e):
                for cc in range(8):
                    i = e.transpose(out=bank(cc // 4)[0:m, (cc % 4) * 128:(cc % 4 + 1) * 128], in_=acc[:, cc, 0:m], identity=ident_f[:])
                return i
            P.op("pe", trx, reads=[t_acc, t_identf], writes=[bankT[0], bankT[1]])
            yield
            P.op("act", lambda e: e.activation(out=Btok[0:m, :, :], in_=pv2[0:m, 0:256].rearrange("p (g n) -> p g n", g=2), func=AF.Copy),
                 reads=[bankT[2]], writes=[t_Btok])
            P.op("act", lambda e: e.activation(out=x_dt[0:m, :], in_=psA[0:m, 0:1024], func=AF.Copy), reads=[bankT[0], bankT[1]], writes=[t_xdt])
            yield
            for g in range(2):
                bvx = bank(g)[0:m, :].rearrange("p (h d) -> p h d", h=8)
                P.op("dve", lambda e, g=g, bvx=bvx: e.tensor_tensor(
                    out=xd[0:m, g * 512:(g + 1) * 512].rearrange("p (h d) -> p h d", h=8), in0=bvx,
                    in1=g_("dtd")[0:m, g * 8:(g + 1) * 8].unsqueeze(2).to_broadcast([m, 8, 64]), op=ALU.mult),
                    reads=[bankT[g], t_("dtd")], writes=[t_xd])
            yield
            P.op("act", lambda e: e.activation(out=xs_sb[0:m, :], in_=psA[0:m, 0:1024], func=AF.Copy), reads=[bankT[0], bankT[1]], writes=[t_xs])
            yield
            P.op("pool", lambda e: e.tensor_tensor(
                out=xsD[0:m, :].rearrange("p (h d) -> p h d", h=16), in0=xs_sb[0:m, :].rearrange("p (h d) -> p h d", h=16),
                in1=dsk[0:m, :].unsqueeze(2).to_broadcast([m, 16, 64]), op=ALU.mult), reads=[t_xs, t_dsk], writes=[t_xsD])
            cbv = bank(3)[0:m, 128:256].rearrange("p (g i) -> p g i", g=2)[:, :, 0:CH]
            P.op("dve", lambda e: e.tensor_tensor(
                out=Mm[0:m, :, 0:CH].rearrange("p (g h) i -> p g h i", g=2), in0=dg[0:m, :, 0:CH].rearrange("p (g h) i -> p g h i", g=2),
                in1=cbv.unsqueeze(2).to_broadcast([m, 2, 8, CH]), op=ALU.mult), reads=t_dgq + [bankT[3]], writes=[t_M])
            yield

        def runBC(s):
            t0, m = SUBT[s]
            p = s % 2
            samp = (s == 16)
            CH = 32 if samp else 64
            bcb, t_bcb = bcb2[p]
            Btok, t_Btok = Btok2[p]
            indec, t_indec = indec2[p]
            cd, t_cd = cd2[p]
            xd, t_xd = xd2[p]
            sz, t_sz = sz2[p]
            xres, t_xres = sz, t_sz
            for g in range(2):
                def mmy(e, g=g):
                    e.matmul(bank(4 + g)[0:m, :], lhsT=ident_b[0:m, 0:m], rhs=xsD[0:m, g * 512:(g + 1) * 512], start=True, stop=False, skip_group_check=True)
                    for hh in range(8):
                        h = g * 8 + hh
                        for c in range(2):
                            i = e.matmul(bank(4 + g)[c * CH:(c + 1) * CH, hh * 64:(hh + 1) * 64], lhsT=Mm[c * CH:(c + 1) * CH, h, 0:CH],
                                         rhs=x_dt[c * CH:(c + 1) * CH, h * 64:(h + 1) * 64], start=False, stop=(hh == 7 and c == 1), skip_group_check=True)
                    return i
                P.op("pe", mmy, reads=[t_identb, t_xsD, t_M, t_xdt], writes=[bankT[4 + g]])
            yield
            P.op("act", lambda e: e.activation(out=ysb[0:m, :], in_=psB[0:m, 0:1024], func=AF.Copy), reads=[bankT[4], bankT[5]], writes=[t_y])
            yield
            pending_out = []
            for c in range(2):
                r0, r1 = c * CH, (c + 1) * CH
                if samp:
                    (Sx, t_Sx), (Sbx, t_Sbx) = Ss[c], Sbs[c]
                else:
                    (Sx, t_Sx), (Sbx, t_Sbx) = (S, t_S), (Sb, t_Sb)
                for g in range(2):
                    P.op("pe", lambda e, g=g, r0=r0, r1=r1, Sbx=Sbx: e.matmul(
                        bank(6 + g)[r0:r1, :], lhsT=bcb[:, 2 + g, r0:r1], rhs=Sbx[:, g * 512:(g + 1) * 512], start=True, stop=True, skip_group_check=True),
                        reads=[t_bcb, t_Sbx], writes=[bankT[6 + g]])
                    P.op("pe", lambda e, g=g, r0=r0, r1=r1: e.matmul(
                        bank(4 + g)[:, :], lhsT=Btok[r0:r1, g, :], rhs=xd[r0:r1, g * 512:(g + 1) * 512], start=True, stop=True),
                        reads=[t_Btok, t_xd], writes=[bankT[4 + g]])
                yield
                P.op("pool", lambda e, c=c, Sx=Sx: e.tensor_tensor(
                    out=Sx[:, :].rearrange("p (h d) -> p h d", h=16), in0=Sx[:, :].rearrange("p (h d) -> p h d", h=16),
                    in1=cd[:, c, :].unsqueeze(2).to_broadcast([128, 16, 64]), op=ALU.mult), reads=[t_Sx, t_cd], writes=[t_Sx])
                P.op("dve", lambda e, Sx=Sx: e.tensor_tensor(out=Sx[:, :], in0=Sx[:, :], in1=psB[:, 0:1024], op=ALU.add),
                     reads=[t_Sx, bankT[4], bankT[5]], writes=[t_Sx])
                yield
                P.op("act", lambda e, Sx=Sx, Sbx=Sbx: e.activation(out=Sbx[:, :], in_=Sx[:, :], func=AF.Copy), reads=[t_Sx], writes=[t_Sbx])
                yield
                if samp:
                    pending_out.append((Sx, t_Sx, 1 + c))
                elif s == 15 and c == 1:
                    pending_out.append((Sx, t_Sx, 0))
            for g in range(2):
                P.op("dve", lambda e, g=g: e.tensor_tensor(
                    out=junk[0:m, 0:512].rearrange("p (h d) -> p h d", h=8), in0=bank(6 + g)[0:m, :].rearrange("p (h d) -> p h d", h=8),
                    in1=indec[0:m, g * 8:(g + 1) * 8].unsqueeze(2).to_broadcast([m, 8, 64]), op=ALU.mult),
                    reads=[bankT[6 + g], t_indec], writes=[t_junk])
                P.op("pool", lambda e, g=g: e.tensor_tensor(out=ysb[0:m, g * 512:(g + 1) * 512], in0=ysb[0:m, g * 512:(g + 1) * 512], in1=junk[0:m, 0:512], op=ALU.add),
                     reads=[t_junk, t_y], writes=[t_y])
                yield
            for po in pending_out:
                state_out(*po)
                yield
            if debug:
                P.dma("sp", lambda e: e.dma_start(out=dbg_y[t0:t0 + m, :], in_=ysb[0:m, :]), reads=[t_y])
            P.op("pool", lambda e: e.tensor_tensor(out=ysb[0:m, :], in0=ysb[0:m, :], in1=sz[0:m, :], op=ALU.mult), reads=[t_y, t_sz], writes=[t_y])
            yield
            for g in range(2):
                P.op("act", lambda e, g=g: e.activation(out=junk[0:m, 0:512], in_=ysb[0:m, g * 512:(g + 1) * 512], func=AF.Square,
                                                        accum_out=ssq[0:m, g:g + 1]), reads=[t_y], writes=[t_junk, t_ssq])
            P.op("act", lambda e: e.activation(out=ssq[0:m, 2:4], in_=ssq[0:m, 0:2], func=AF.Ln, scale=1.0 / 512.0, bias=eps_t[0:m, :]),
                 reads=[t_ssq, t_eps], writes=[t_ssq])
            P.op("act", lambda e: e.activation(out=ssq[0:m, 0:2], in_=ssq[0:m, 2:4], func=AF.Exp, scale=-0.5), reads=[t_ssq], writes=[t_ssq])
            yield
            for g in range(2):
                P.op("dve", lambda e, g=g: e.scalar_tensor_tensor(
                    out=yn[0:m, g * 512:(g + 1) * 512], in0=ysb[0:m, g * 512:(g + 1) * 512], scalar=ssq[0:m, g:g + 1],
                    in1=gss[0:m, g * 512:(g + 1) * 512], op0=ALU.mult, op1=ALU.mult), reads=[t_y, t_ssq, t_gss], writes=[t_yn])
            yield
            if debug:
                P.dma("sp", lambda e: e.dma_start(out=dbg_yn[t0:t0 + m, :], in_=yn[0:m, :]), reads=[t_yn])

            def tryn(e):
                for k in range(8):
                    i = e.transpose(out=pv6[:, k * 128:k * 128 + m], in_=yn[0:m, k * 128:(k + 1) * 128], identity=ident_b[0:m, 0:m])
                return i
            P.op("pe", tryn, reads=[t_yn, t_identb], writes=[bankT[6]])
            P.dma("sp", lambda e: e.dma_start(out=xres[0:m, :], in_=x1d[t0:t0 + m, :]), reads=[x1d_T[s], t_sz], writes=[t_xres])
            yield
            P.op("act", lambda e: e.activation(out=ynT[:, :, 0:m], in_=pv6.rearrange("p (k t) -> p k t", k=8)[:, :, 0:m], func=AF.Copy),
                 reads=[bankT[6]], writes=[t_ynT])
            yield
            for half in range(2):
                def mmo(e, half=half):
                    for k in range(8):
                        i = e.matmul(bank(4 + half)[0:m, :], lhsT=ynT[:, k, 0:m], rhs=wot[:, k, half * 512:(half + 1) * 512], start=(k == 0), stop=(k == 7))
                    return i
                P.op("pe", mmo, reads=[t_ynT, t_wot[half]], writes=[bankT[4 + half]])
            yield
            P.op("dve", lambda e: e.scalar_tensor_tensor(out=r1s[0:m, :], in0=xres[0:m, :], scalar=ALPHA, in1=psB[0:m, 0:1024],
                                                         op0=ALU.mult, op1=ALU.add), reads=[t_xres, bankT[4], bankT[5]], writes=[t_r1s])
            yield
            P.dma("sp", lambda e: e.dma_start(out=r1d[t0:t0 + m, :], in_=r1s[0:m, :]), reads=[t_r1s], writes=[r1d_T[s]])
            yield

        def interleave_w(pairs):
            pairs = [[g, w] for g, w in pairs if g is not None]
            while pairs:
                for pr in list(pairs):
                    for _ in range(pr[1]):
                        try:
                            next(pr[0])
                        except StopIteration:
                            pairs.remove(pr)
                            break

        def interleave2(gens):
            gens = [g for g in gens if g is not None]
            while gens:
                for g in list(gens):
                    try:
                        next(g)
                    except StopIteration:
                        gens.remove(g)

        interleave2([prepA(0)])
        for s in range(17):
            gA = prepA(s + 1) if s + 1 < 17 else None
            if gA is not None:
                for _ in range(SSD_K):
                    next(gA)
            interleave_w([(gA, SSD_W[1]), (runBC(s), SSD_W[0])])
        A.reset(mk)

    SCALE = 192.0 ** -0.5

    def mla_phase():
        mk = A.mark()
        w_in_v = w["w_in"].rearrange("(ko ki) f -> ki ko f", ki=128)
        oT, r = A.alloc("oT", [128, 8, NT], BF16)
        oT_T = [[A.newT(r, f"oT{h}_{s}") for s in range(17)] for h in range(8)]
        cqnT, r = A.alloc("cqnT", [128, 4, NT], BF16)
        cqnT_T = [A.newT(r, f"cqnT{s}") for s in range(17)]
        latTp, r = A.alloc("latTp", [128, 4, NPROMPT], BF16)
        latTp_T = [A.newT(r, f"latTp{s}") for s in range(16)]
        latTs, r = A.alloc("latTs", [128, 4, 2, 1056], BF16)
        latTs_T = [[A.newT(r, f"latTs{j}_{a}") for a in range(9)] for j in range(2)]
        krTp, r = A.alloc("krTp", [128, NPROMPT], BF16)
        krTp_T = [A.newT(r, f"krTp{s}") for s in range(16)]
        krTs, r = A.alloc("krTs", [128, 2, 1056], BF16)
        krTs_T = [[A.newT(r, f"krTs{j}_{a}") for a in range(9)] for j in range(2)]
        mk_a = A.mark()
        wq, t_wq = al("wq_in", [128, 8, 512], BF16)
        wkv, t_wkv = al("wkv_in", [128, 8, 512], BF16)
        wkr, t_wkr = al("wkr_in", [128, 8, 64], BF16)
        P.dma("pool", lambda e: e.dma_start(out=wq[:], in_=w_in_v[:, :, 2576:3088]), writes=[t_wq])
        P.dma("pool", lambda e: e.dma_start(out=wkv[:], in_=w_in_v[:, :, 3088:3600]), writes=[t_wkv])
        P.dma("pool", lambda e: e.dma_start(out=wkr[:], in_=w_in_v[:, :, 3600:3664]), writes=[t_wkr])
        gq, t_gq = al("gq", [128, 512])
        gkv, t_gkv = al("gkv", [128, 512])
        P.dma("sp", lambda e: e.dma_start(out=gq[:], in_=gq_d[:, :]), writes=[t_gq])
        P.dma("sp", lambda e: e.dma_start(out=gkv[:], in_=gkv_d[:, :]), writes=[t_gkv])
        ctok, t_ctok = al("ctok", [128, 8, 512], BF16)
        ckr, t_ckr = al("ckr", [128, 8, 64], BF16)
        for j in range(2):
            P.dma("pool", lambda e, j=j: e.dma_start(out=ctok[:], in_=clat_d[j].rearrange("(a p) f -> p a f", p=128)), writes=[t_ctok])
            P.dma("pool", lambda e, j=j: e.dma_start(out=ckr[:], in_=ckr_d[j].rearrange("(a p) f -> p a f", p=128)), writes=[t_ckr])
            for a in range(8):
                bk = 4 + (a % 2)
                pv = bank(bk).bitcast(BF16)

                def trc(e, a=a, pv=pv):
                    for c in range(4):
                        i = e.transpose(out=pv[:, c * 128:(c + 1) * 128], in_=ctok[:, a, c * 128:(c + 1) * 128], identity=ident_b[:])
                    return e.transpose(out=pv[0:64, 512:640], in_=ckr[:, a, :], identity=ident_b[:])
                P.op("pe", trc, reads=[t_ctok, t_ckr, t_identb], writes=[bankT[bk]])
                P.op("act", lambda e, a=a, j=j, pv=pv: e.activation(out=latTs[:, :, j, a * 128:(a + 1) * 128],
                                                                  in_=pv[:, 0:512].rearrange("p (c t) -> p c t", c=4), func=AF.Copy),
                     reads=[bankT[bk]], writes=[latTs_T[j][a]])
                P.op("dve", lambda e, a=a, j=j, pv=pv: e.tensor_copy(out=krTs[0:64, j, a * 128:(a + 1) * 128], in_=pv[0:64, 512:640]),
                     reads=[bankT[bk]], writes=[krTs_T[j][a]])
        sq, t_sq = al("sq", [128, 4])
        cqn2 = [al(f"cqn{i}", [128, 512], BF16) for i in range(2)]
        latf = [al(f"latf{i}", [128, 512]) for i in range(2)]
        latb2 = [al(f"latb{i}", [128, 512], BF16) for i in range(2)]
        cs4 = [al(f"cs4_{i}", [128, 128]) for i in range(2)]
        rt, t_rt = al("rt", [128, 128])
        krf = [al(f"krf{i}", [128, 64]) for i in range(2)]
        krb2 = [al(f"krb{i}", [128, 64], BF16) for i in range(2)]
        jk, t_jk = al("jk", [128, 512])

        sq2 = [al(f"sq2_{i}", [128, 4]) for i in range(2)]

        def b2a_bk(s, i):
            return (0, 1, 2)[i] if s % 2 == 0 else (5, 6, 7)[i]

        def b2a_part1a(s):
            t0, m = SUBT[s]
            xr = [xT_T[s]]
            csh, t_cs = cs4[s % 2]
            sq, t_sq = sq2[s % 2]
            P.dma("sp", lambda e: e.dma_start(out=csh[0:m, :], in_=cs4_d[t0:t0 + m, :]), writes=[t_cs])
            for (ib, wt, t_wt, nn) in [(0, wq, t_wq, 512), (1, wkv, t_wkv, 512), (2, wkr, t_wkr, 64)]:
                bk = b2a_bk(s, ib)

                def mmp(e, bk=bk, wt=wt, nn=nn):
                    for k in range(8):
                        i = e.matmul(bank(bk)[0:m, 0:nn], lhsT=xT[:, k, t0:t0 + m], rhs=wt[:, k, :], start=(k == 0), stop=(k == 7))
                    return i
                P.op("pe", mmp, reads=xr + [t_wt], writes=[bankT[bk]])
                yield
            for i_ in range(2):
                bk = b2a_bk(s, i_)
                P.op("act", lambda e, i_=i_, bk=bk: e.activation(out=jk[0:m, :], in_=bank(bk)[0:m, :], func=AF.Square, accum_out=sq[0:m, i_:i_ + 1]),
                     reads=[bankT[bk]], writes=[t_jk, t_sq])
                yield
            P.op("act", lambda e: e.activation(out=sq[0:m, 2:4], in_=sq[0:m, 0:2], func=AF.Sqrt, scale=1.0 / 512.0, bias=eps_t[0:m, :]),
                 reads=[t_sq, t_eps], writes=[t_sq])
            yield

        def b2a_part1b(s):
            t0, m = SUBT[s]
            cqn, t_cqn = cqn2[s % 2]
            latfh, t_latf = latf[s % 2]
            csh, t_cs = cs4[s % 2]
            krfh, t_krf = krf[s % 2]
            sq, t_sq = sq2[s % 2]
            b0, b1, b2_ = b2a_bk(s, 0), b2a_bk(s, 1), b2a_bk(s, 2)
            P.op("dve", lambda e: e.tensor_tensor(out=rt[0:m, 0:64], in0=bank(b2_)[0:m, 0:64], in1=csh[0:m, 0:64], op=ALU.mult),
                 reads=[bankT[b2_], t_cs], writes=[t_rt])
            P.op("dve", lambda e: e.tensor_tensor(out=rt[0:m, 64:128], in0=bank(b2_)[0:m, 0:64], in1=csh[0:m, 64:128], op=ALU.mult),
                 reads=[bankT[b2_], t_cs], writes=[t_rt])
            yield
            P.op("dve", lambda e: e.tensor_tensor(out=krfh[0:m, 0:32], in0=rt[0:m, 0:32], in1=rt[0:m, 32:64], op=ALU.subtract), reads=[t_rt], writes=[t_krf])
            P.op("dve", lambda e: e.tensor_tensor(out=krfh[0:m, 32:64], in0=rt[0:m, 64:96], in1=rt[0:m, 96:128], op=ALU.add), reads=[t_rt], writes=[t_krf])
            yield
            P.op("dve", lambda e: e.reciprocal(out=sq[0:m, 0:2], in_=sq[0:m, 2:4]), reads=[t_sq], writes=[t_sq])
            P.op("dve", lambda e: e.scalar_tensor_tensor(out=cqn[0:m, :], in0=bank(b0)[0:m, :], scalar=sq[0:m, 0:1], in1=gq[0:m, :],
                                                         op0=ALU.mult, op1=ALU.mult), reads=[bankT[b0], t_sq, t_gq], writes=[t_cqn])
            yield
            P.op("dve", lambda e: e.scalar_tensor_tensor(out=latfh[0:m, :], in0=bank(b1)[0:m, :], scalar=sq[0:m, 1:2], in1=gkv[0:m, :],
                                                         op0=ALU.mult, op1=ALU.mult), reads=[bankT[b1], t_sq, t_gkv], writes=[t_latf])
            yield

        def b2a_part2(s):
            t0, m = SUBT[s]
            cqn, t_cqn = cqn2[s % 2]
            latfh, t_latf = latf[s % 2]
            latb, t_latb = latb2[s % 2]
            krfh, t_krf = krf[s % 2]
            krb, t_krb = krb2[s % 2]
            P.dma("sp", lambda e: e.dma_start(out=lat_o[t0:t0 + m, :], in_=latfh[0:m, :]), reads=[t_latf])
            P.dma("sp", lambda e: e.dma_start(out=kr_o[t0:t0 + m, :], in_=krfh[0:m, :]), reads=[t_krf])
            P.op("act", lambda e: e.activation(out=latb[0:m, :], in_=latfh[0:m, :], func=AF.Copy), reads=[t_latf], writes=[t_latb])
            P.op("act", lambda e: e.activation(out=krb[0:m, :], in_=krfh[0:m, :], func=AF.Copy), reads=[t_krf], writes=[t_krb])
            yield
            bk = 4
            pv = bank(bk).bitcast(BF16)

            def trq(e):
                for c in range(4):
                    i = e.transpose(out=pv[:, c * 128:c * 128 + m], in_=cqn[0:m, c * 128:(c + 1) * 128], identity=ident_b[0:m, 0:m])
                for c in range(4):
                    i = e.transpose(out=pv[:, 512 + c * 128:512 + c * 128 + m], in_=latb[0:m, c * 128:(c + 1) * 128], identity=ident_b[0:m, 0:m])
                return i
            P.op("pe", trq, reads=[t_cqn, t_latb, t_identb], writes=[bankT[bk]])
            pv3 = bank(3).bitcast(BF16)
            P.op("pe", lambda e: e.transpose(out=pv3[0:64, 0:m], in_=krb[0:m, :], identity=ident_b[0:m, 0:m]),
                 reads=[t_krb, t_identb], writes=[bankT[3]])
            yield
            pvv = pv.rearrange("p (c t) -> p c t", c=8)
            P.op("act", lambda e: e.activation(out=cqnT[:, :, t0:t0 + m], in_=pvv[:, 0:4, 0:m], func=AF.Copy), reads=[bankT[bk]], writes=[cqnT_T[s]])
            yield
            if s < 16:
                P.op("dve", lambda e: e.tensor_copy(out=latTp[:, :, t0:t0 + m], in_=pvv[:, 4:8, 0:m]), reads=[bankT[bk]], writes=[latTp_T[s]])
                P.op("dve", lambda e: e.tensor_copy(out=krTp[0:64, t0:t0 + m], in_=pv3[0:64, 0:m]), reads=[bankT[3]], writes=[krTp_T[s]])
            else:
                for j in range(2):
                    P.op("dve", lambda e, j=j: e.tensor_copy(out=latTs[:, :, j, 1024:1056], in_=pvv[:, 4:8, j * 32:(j + 1) * 32]),
                         reads=[bankT[bk]], writes=[latTs_T[j][8]])
                    P.op("dve", lambda e, j=j: e.tensor_copy(out=krTs[0:64, j, 1024:1056], in_=pv3[0:64, j * 32:(j + 1) * 32]),
                         reads=[bankT[3]], writes=[krTs_T[j][8]])
            yield

        interleave_g([b2a_part1a(0)])
        interleave_g([b2a_part1a(1), b2a_part1b(0)])
        for s in range(17):
            interleave_g([b2a_part1a(s + 2) if s + 2 < 17 else None, b2a_part1b(s + 1) if s + 1 < 17 else None, b2a_part2(s)])
        A.reset(mk_a)
        if MLA_LEVEL <= 1:
            return mk, oT, oT_T, []
        w_uq_v = w["w_uq"].rearrange("(c p) (h t) -> p c h t", p=128, t=192)
        wuq, r = A.alloc("wuq", [128, 4, 8, 192], BF16)
        t_wuq = [A.newT(r, f"wuq{c}") for c in range(4)]
        wuqr, r = A.alloc("wuqr", [128, 4, 8, 64], BF16)
        t_wuqr = A.newT(r, "wuqr")
        wukv, r = A.alloc("wukv", [128, 4, 2048], BF16)
        t_wukv = [A.newT(r, f"wukv{c}") for c in range(4)]
        w_ukv_v = w["w_ukv"].rearrange("(c p) f -> p c f", p=128)
        for c in range(4):
            P.dma("pool", lambda e, c=c: e.dma_start(out=wuq[:, c, :, :], in_=w_uq_v[:, c, :, :]), writes=[t_wuq[c]])
            P.dma("pool", lambda e, c=c: e.dma_start(out=wukv[:, c, :], in_=w_ukv_v[:, c, :]), writes=[t_wukv[c]])
            P.dma("pool", lambda e, c=c: e.dma_start(out=wuqr[:, c, :, 0:32], in_=w_uq_v[:, c, :, 160:192]), writes=[t_wuqr])
            P.dma("pool", lambda e, c=c: e.dma_start(out=wuqr[:, c, :, 32:64], in_=w_uq_v[:, c, :, 128:160]), writes=[t_wuqr])
        P.op("dve", lambda e: e.tensor_scalar(out=wuqr[:, :, :, 0:32], in0=wuqr[:, :, :, 0:32], scalar1=-1.0, scalar2=None, op0=ALU.mult),
             reads=[t_wuqr], writes=[t_wuqr])
        A2 = Arena(nc, xT_rec[0], xT_rec[1])
        xT_extra = []

        def al2(name, shape, dt=F32):
            h, _ = A2.alloc(name, shape, dt)
            t = T(name, after=list(xT_T))
            xT_extra.append(t)
            return h, t
        mq, t_mq = al("maskq", [128, 128], BF16)
        P.dma("pool", lambda e: e.dma_start(out=mq[:], in_=maskq_d[:, :]), writes=[t_mq])
        cosT, t_cosT = al2("a2cosT", [128, NT])
        sinT, t_sinT = al2("a2sinT", [128, NT])
        P.dma("sp", lambda e: e.dma_start(out=cosT[0:64, :], in_=cosT_d[:, :]), writes=[t_cosT])
        P.dma("sp", lambda e: e.dma_start(out=sinT[0:64, :], in_=sinT_d[:, :]), writes=[t_sinT])
        Pex, t_Pex = al2("a2Pex", [128, 2048])
        qt1, t_qt1 = al2("a2qt1", [128, 256])
        qt2, t_qt2 = al2("a2qt2", [128, 256])
        KTb, Vb, qnb, qrb = [], [], [], []
        for i in range(2):
            h_, r = A.alloc(f"KT{i}", [128, NPROMPT], BF16)
            KTb.append((h_, [A.newT(r, f"KT{i}_{k}") for k in range(4)]))
            h_, r = A.alloc(f"V{i}", [128, 16, 128], BF16)
            Vb.append((h_, [A.newT(r, f"V{i}_{k}") for k in range(4)]))
            h_, r = A.alloc(f"qn{i}", [128, NPROMPT], BF16)
            qnb.append((h_, [A.newT(r, f"qn{i}_{k}") for k in range(4)]))
            h_, r = A.alloc(f"qr{i}", [128, NPROMPT], BF16)
            qrb.append((h_, [A.newT(r, f"qr{i}_{k}") for k in range(4)]))
        qns, r = A.alloc("qns", [128, 8, 64], BF16)
        t_qns = [A.newT(r, f"qns{h}") for h in range(8)]
        qrs, r = A.alloc("qrs", [128, 8, 64], BF16)
        t_qrs = [A.newT(r, f"qrs{h}") for h in range(8)]
        NPN = 3
        Pn = [al(f"Pn{i}", [128, 2048], BF16) for i in range(NPN)]
        PT = [al2("a2PT0", [128, 16, 128], BF16), al("PT1", [128, 16, 128], BF16)]
        st8 = [al(f"st8_{i}", [128, 16]) for i in range(NPN)]
        ctr = dict(blk=0, rnd=0, att=0, pb=0)
        OB = 6

        def next_pb():
            ctr["pb"] += 1
            return 5 if ctr["pb"] % 2 else 7

        def proj_items(h):
            items = []
            KT, tKT = KTb[h % 2]
            V, tV = Vb[h % 2]
            qn, tqn = qnb[h % 2]
            qr, tqr = qrb[h % 2]
            for t, (t0, n) in enumerate(TILES):
                cr = [cqnT_T[s] for s in SUB_OF_TILE[t]]

                def it_qn(t=t, t0=t0, n=n, cr=cr):
                    PB = next_pb()

                    def mm(e):
                        for c in range(4):
                            i = e.matmul(bank(PB)[:, 0:n], lhsT=wuq[:, c, h, 0:128], rhs=cqnT[:, c, t0:t0 + n], start=(c == 0), stop=(c == 3))
                        return i
                    P.op("pe", mm, reads=cr + t_wuq, writes=[bankT[PB]])
                    yield
                    if t < 4:
                        P.op("act", lambda e: e.activation(out=qn[:, t0:t0 + n], in_=bank(PB)[:, 0:n], func=AF.Copy), reads=[bankT[PB]], writes=[tqn[t]])
                    else:
                        P.op("act", lambda e: e.activation(out=qns[:, h, :], in_=bank(PB)[:, 0:n], func=AF.Copy), reads=[bankT[PB]], writes=[t_qns[h]])
                items.append(it_qn)
                for hf in range(2 if t < 4 else 1):
                    n2 = 256 if t < 4 else 64
                    c0 = t0 + hf * 256

                    def it_qr(t=t, c0=c0, n2=n2, cr=cr):
                        PB = next_pb()

                        def mm(e):
                            for c in range(4):
                                e.matmul(bank(PB)[0:64, 0:n2], lhsT=wuq[:, c, h, 128:192], rhs=cqnT[:, c, c0:c0 + n2], start=(c == 0), stop=(c == 3),
                                         skip_group_check=True)
                            for c in range(4):
                                i = e.matmul(bank(PB)[0:64, 256:256 + n2], lhsT=wuqr[:, c, h, :], rhs=cqnT[:, c, c0:c0 + n2], start=False, stop=(c == 3),
                                             skip_group_check=True)
                            return i
                        P.op("pe", mm, reads=cr + t_wuq + [t_wuqr], writes=[bankT[PB]])
                        yield
                        P.op("dve", lambda e: e.tensor_tensor(out=qt1[0:64, 0:n2], in0=bank(PB)[0:64, 0:n2], in1=cosT[0:64, c0:c0 + n2], op=ALU.mult),
                             reads=[bankT[PB], t_cosT], writes=[t_qt1])
                        P.op("dve", lambda e: e.tensor_tensor(out=qt2[0:64, 0:n2], in0=bank(PB)[0:64, 256:256 + n2], in1=sinT[0:64, c0:c0 + n2], op=ALU.mult),
                             reads=[bankT[PB], t_sinT], writes=[t_qt2])
                        if t < 4:
                            P.op("pool", lambda e: e.tensor_tensor(out=qr[0:64, c0:c0 + n2], in0=qt1[0:64, 0:n2], in1=qt2[0:64, 0:n2], op=ALU.add),
                                 reads=[t_qt1, t_qt2], writes=[tqr[t]])
                        else:
                            P.op("pool", lambda e: e.tensor_tensor(out=qrs[0:64, h, :], in0=qt1[0:64, 0:n2], in1=qt2[0:64, 0:n2], op=ALU.add),
                                 reads=[t_qt1, t_qt2], writes=[t_qrs[h]])
                    items.append(it_qr)
            for kb in range(4):
                def it_k(kb=kb):
                    PB = next_pb()

                    def mm(e):
                        for c in range(4):
                            i = e.matmul(bank(PB)[:, :], lhsT=wukv[:, c, h * 256:h * 256 + 128], rhs=latTp[:, c, kb * 512:(kb + 1) * 512],
                                         start=(c == 0), stop=(c == 3))
                        return i
                    P.op("pe", mm, reads=[latTp_T[4 * kb + i] for i in range(4)] + t_wukv, writes=[bankT[PB]])
                    yield
                    P.op("act", lambda e: e.activation(out=KT[:, kb * 512:(kb + 1) * 512], in_=bank(PB)[:, :], func=AF.Copy),
                         reads=[bankT[PB]], writes=[tKT[kb]])

                def it_v(kq=kb):
                    PB = next_pb()

                    def mm(e):
                        for j in range(4):
                            kt = 4 * kq + j
                            for c in range(4):
                                i = e.matmul(bank(PB)[:, j * 128:(j + 1) * 128], lhsT=latTp[:, c, kt * 128:(kt + 1) * 128],
                                             rhs=wukv[:, c, h * 256 + 128:h * 256 + 256], start=(c == 0), stop=(c == 3), skip_group_check=True)
                        return i
                    P.op("pe", mm, reads=[latTp_T[4 * kq + i] for i in range(4)] + t_wukv, writes=[bankT[PB]])
                    yield
                    P.op("act", lambda e: e.activation(out=V[:, 4 * kq:4 * kq + 4, :], in_=bank(PB)[:, :].rearrange("p (j d) -> p j d", j=4), func=AF.Copy),
                         reads=[bankT[PB]], writes=[tV[kq]])
                items.append(it_k)
                items.append(it_v)
            return items

        def proj_items_s(h, j, slot):
            items = []
            KT, tKT = KTb[slot]
            V, tV = Vb[slot]
            for kb in range(3):
                cols = min(512, 1056 - kb * 512)

                def it_k(kb=kb, cols=cols):
                    PB = next_pb()

                    def mm(e):
                        for c in range(4):
                            i = e.matmul(bank(PB)[:, 0:cols], lhsT=wukv[:, c, h * 256:h * 256 + 128], rhs=latTs[:, c, j, kb * 512:kb * 512 + cols],
                                         start=(c == 0), stop=(c == 3))
                        return i
                    P.op("pe", mm, reads=latTs_T[j] + t_wukv, writes=[bankT[PB]])
                    yield
                    P.op("act", lambda e: e.activation(out=KT[:, kb * 512:kb * 512 + cols], in_=bank(PB)[:, 0:cols], func=AF.Copy),
                         reads=[bankT[PB]], writes=[tKT[kb]])
                items.append(it_k)
                kts = [kt for kt in range(4 * kb, min(9, 4 * kb + 4))]

                def it_v(kq=kb, kts=kts):
                    PB = next_pb()

                    def mm(e):
                        for kt in kts:
                            kk = min(128, 1056 - kt * 128)
                            for c in range(4):
                                i = e.matmul(bank(PB)[0:kk, (kt - 4 * kq) * 128:(kt - 4 * kq + 1) * 128], lhsT=latTs[:, c, j, kt * 128:kt * 128 + kk],
                                             rhs=wukv[:, c, h * 256 + 128:h * 256 + 256], start=(c == 0), stop=(c == 3), skip_group_check=True)
                        return i
                    P.op("pe", mm, reads=latTs_T[j] + t_wukv, writes=[bankT[PB]])
                    yield
                    nkt_ = len(kts)
                    P.op("dve", lambda e: e.tensor_copy(out=V[:, 4 * kq:4 * kq + nkt_, :],
                                                        in_=bank(PB)[:, 0:nkt_ * 128].rearrange("p (j d) -> p j d", j=nkt_)),
                         reads=[bankT[PB]], writes=[tV[kq]])
                items.append(it_v)
            return items

        stT = [{k: T(f"st{i}_{k}") for k in ["mx0", "mx1", "ng0", "ng1", "sm0", "sm1", "c"]} for i in range(NPN)]

        def stage1(n, h, qT, t_q, qrT_, t_qr_, qc0, mq_, KT, tKT, krT_ap, t_krT, nk, diag, bsz=1024):
            Pnh, t_Pn = Pn[n % NPN]
            sth, _ = st8[n % NPN]
            tt = stT[n % NPN]
            blocks = [(b0, min(bsz, nk - b0)) for b0 in range(0, nk, bsz)]
            for bi, (b0, bn) in enumerate(blocks):
                pr = ctr["blk"] % 2 if bsz == 1024 else 0
                ctr["blk"] += 1
                nbk = (bn + 511) // 512
                bts = [bankT[2 * pr + x] for x in range(nbk)]

                def mms(e, b0=b0, bn=bn, pr=pr, nbk=nbk):
                    for x in range(nbk):
                        k0 = b0 + x * 512
                        cols = min(512, b0 + bn - k0)
                        has_diag = diag is not None and (diag * 128) // 512 == k0 // 512
                        e.matmul(bank(2 * pr + x)[0:mq_, 0:cols], lhsT=qT[:, qc0:qc0 + mq_], rhs=KT[:, k0:k0 + cols], start=True, stop=False,
                                 skip_group_check=True)
                        i = e.matmul(bank(2 * pr + x)[0:mq_, 0:cols], lhsT=qrT_[0:64, qc0:qc0 + mq_], rhs=krT_ap[0:64, k0:k0 + cols],
                                     start=False, stop=not has_diag, skip_group_check=True)
                        if has_diag:
                            dc = (diag * 128) % 512
                            i = e.matmul(bank(2 * pr + x)[0:mq_, dc:dc + 128], lhsT=ident_b[0:mq_, 0:mq_], rhs=mq[0:mq_, :], start=False, stop=True,
                                         skip_group_check=True)
                    return i
                kbs = sorted(set((b0 + x * 512) // 512 for x in range(nbk)))
                P.op("pe", mms, reads=t_q + t_qr_ + [tKT[k] for k in kbs if k < len(tKT)] + [t_identb, t_mq] + t_krT, writes=bts)
                yield
                sc = (psA[0:mq_, 1024 * pr:1024 * pr + bn])
                P.op("dve", lambda e, sc=sc, bi=bi: e.tensor_reduce(out=sth[0:mq_, bi:bi + 1], in_=sc, op=ALU.max, axis=AX.X), reads=bts, writes=[tt[f"mx{bi}"]])
                P.op("dve", lambda e, bi=bi: e.tensor_scalar(out=sth[0:mq_, 2 + bi:3 + bi], in0=sth[0:mq_, bi:bi + 1], scalar1=-SCALE, scalar2=None, op0=ALU.mult),
                     reads=[tt[f"mx{bi}"]], writes=[tt[f"ng{bi}"]])
                yield
                P.op("act", lambda e, sc=sc, bi=bi, b0=b0, bn=bn: e.activation(out=Pex[0:mq_, b0:b0 + bn], in_=sc, func=AF.Exp, scale=SCALE,
                                                                               bias=sth[0:mq_, 2 + bi:3 + bi], accum_out=sth[0:mq_, 4 + bi:5 + bi]),
                     reads=bts + [tt[f"ng{bi}"]], writes=[t_Pex, tt[f"sm{bi}"]])
                yield
            tc = tt["c"]
            if len(blocks) == 1:
                P.op("dve", lambda e: e.reciprocal(out=sth[0:mq_, 6:7], in_=sth[0:mq_, 4:5]), reads=[tt["sm0"]], writes=[tc])
                P.op("dve", lambda e: e.tensor_scalar(out=Pnh[0:mq_, 0:nk], in0=Pex[0:mq_, 0:nk], scalar1=sth[0:mq_, 6:7], scalar2=None, op0=ALU.mult),
                     reads=[t_Pex, tc], writes=[t_Pn])
            else:
                P.op("dve", lambda e: e.tensor_tensor(out=sth[0:mq_, 8:9], in0=sth[0:mq_, 0:1], in1=sth[0:mq_, 1:2], op=ALU.max),
                     reads=[tt["mx0"], tt["mx1"]], writes=[tc])
                P.op("dve", lambda e: e.tensor_scalar(out=sth[0:mq_, 6:8], in0=sth[0:mq_, 0:2], scalar1=sth[0:mq_, 8:9], scalar2=None, op0=ALU.subtract),
                     reads=[tc, tt["mx0"], tt["mx1"]], writes=[tc])
                yield
                P.op("act", lambda e: e.activation(out=sth[0:mq_, 6:8], in_=sth[0:mq_, 6:8], func=AF.Exp, scale=SCALE), reads=[tc], writes=[tc])
                yield
                P.op("dve", lambda e: e.tensor_tensor(out=sth[0:mq_, 9:11], in0=sth[0:mq_, 6:8], in1=sth[0:mq_, 4:6], op=ALU.mult),
                     reads=[tc, tt["sm0"], tt["sm1"]], writes=[tc])
                P.op("dve", lambda e: e.tensor_tensor(out=sth[0:mq_, 11:12], in0=sth[0:mq_, 9:10], in1=sth[0:mq_, 10:11], op=ALU.add), reads=[tc], writes=[tc])
                P.op("dve", lambda e: e.reciprocal(out=sth[0:mq_, 12:13], in_=sth[0:mq_, 11:12]), reads=[tc], writes=[tc])
                P.op("dve", lambda e: e.tensor_scalar(out=sth[0:mq_, 13:15], in0=sth[0:mq_, 6:8], scalar1=sth[0:mq_, 12:13], scalar2=None, op0=ALU.mult),
                     reads=[tc], writes=[tc])
                yield
                for bi, (b0, bn) in enumerate(blocks):
                    eng_ = "dve" if bi == 0 else "pool"
                    P.op(eng_, lambda e, bi=bi, b0=b0, bn=bn: e.tensor_scalar(out=Pnh[0:mq_, b0:b0 + bn], in0=Pex[0:mq_, b0:b0 + bn],
                                                                              scalar1=sth[0:mq_, 13 + bi:14 + bi], scalar2=1.0, op0=ALU.mult, op1=ALU.mult),
                         reads=[t_Pex, tc], writes=[t_Pn])
            yield

        def stage2(n, h, qc0, mq_, nk, V, tV, s_out):
            Pnh, t_Pn = Pn[n % NPN]
            PTh, t_PT = PT[n % 2]
            nkt = (nk + 127) // 128
            use_xbar = USE_XBAR and mq_ == 128 and nk % 128 == 0
            if use_xbar:
                P.dmat("sp", lambda e: e.dma_start_transpose(out=PTh[:, 0:nkt, :], in_=Pnh[0:128, 0:nk]), reads=[t_Pn], writes=[t_PT])
                yield
            for r0 in (range(0, nkt, 8) if not use_xbar else []):
                r1 = min(nkt, r0 + 8)
                bk = 4
                pv = bank(bk).bitcast(BF16)

                def trp(e, r0=r0, r1=r1, pv=pv):
                    for kt in range(r0, r1):
                        kk = min(128, nk - kt * 128)
                        i = e.transpose(out=pv[0:kk, (kt - r0) * 128:(kt - r0) * 128 + mq_], in_=Pnh[0:mq_, kt * 128:kt * 128 + kk],
                                        identity=ident_b[0:mq_, 0:mq_])
                    return i
                P.op("pe", trp, reads=[t_Pn, t_identb], writes=[bankT[bk]])
                yield
                P.op("act", lambda e, r0=r0, r1=r1, pv=pv: e.activation(
                    out=PTh[:, r0:r1, 0:mq_], in_=pv.rearrange("p (k t) -> p k t", k=8)[:, 0:r1 - r0, 0:mq_], func=AF.Copy),
                    reads=[bankT[bk]], writes=[t_PT])
                yield

            def mmo(e):
                for kt in range(nkt):
                    kk = min(128, nk - kt * 128)
                    i = e.matmul(bank(OB)[:, 0:mq_], lhsT=V[0:kk, kt, :], rhs=PTh[0:kk, kt, 0:mq_], start=(kt == 0), stop=(kt == nkt - 1))
                return i
            P.op("pe", mmo, reads=[tV[k] for k in range(min(4, (nkt + 3) // 4))] + [t_PT], writes=[bankT[OB]])
            yield
            P.op("dve", lambda e: e.tensor_copy(out=oT[:, h, qc0:qc0 + mq_], in_=bank(OB)[:, 0:mq_]), reads=[bankT[OB]], writes=[oT_T[h][s_out]])
            yield

        def interleave_aw(pairs):
            pairs = [[g, w] for g, w in pairs if g is not None]
            while pairs:
                for pr in list(pairs):
                    for _ in range(pr[1]):
                        try:
                            next(pr[0])
                        except StopIteration:
                            pairs.remove(pr)
                            break

        def items_gen(items):
            for it in items:
                yield from it()
                yield

        def run_item(it):
            for _ in it():
                pass

        def interleave(gens):
            gens = [g for g in gens if g is not None]
            while gens:
                for g in list(gens):
                    try:
                        next(g)
                    except StopIteration:
                        gens.remove(g)

        NH = NHEADS_DBG
        DEPTH = 2
        for it in proj_items(0):
            run_item(it)
        pend_s2 = []
        for h in range(NH):
            nxt = proj_items(h + 1) if h + 1 < NH else []
            KT, tKT = KTb[h % 2]
            V, tV = Vb[h % 2]
            qn, tqn = qnb[h % 2]
            qr, tqr = qrb[h % 2]
            for i in range(16):
                n = ctr["att"]
                ctr["att"] += 1
                g1 = stage1(n, h, qn, [tqn[i // 4]], qr, [tqr[i // 4]], 128 * i, 128, KT, tKT, krTp, krTp_T[0:i + 1], 128 * (i + 1), i)
                pend_s2.append((n, h, 128 * i, 128, 128 * (i + 1), V, tV, i))
                g2 = stage2(*pend_s2.pop(0)) if len(pend_s2) > DEPTH else None
                take = []
                if i >= DEPTH:
                    for _ in range(2 if i % 2 == 0 else 1):
                        if nxt:
                            take.append(nxt.pop(0))
                interleave_aw([(g2, ATT_W[1]), (g1, ATT_W[0]), (items_gen(take), ATT_W[2])])
            while nxt:
                run_item(nxt.pop(0))
        while pend_s2:
            interleave([stage2(*pend_s2.pop(0))])
        ctxs = [(h, j) for h in range(NH) for j in range(2)]
        nctx = len(ctxs)

        def kv_items(c):
            if c >= nctx:
                return [], []
            its = proj_items_s(ctxs[c][0], ctxs[c][1], c % 2)
            return its[0::2], its[1::2]

        def s1_ctx(c, n):
            h, j = ctxs[c]
            KT, tKT = KTb[c % 2]
            return stage1(n, h, qns[:, h, :], [t_qns[h]], qrs[:, h, :], [t_qrs[h]], 32 * j, 32, KT, tKT, krTs[:, j, :], krTs_T[j], 1056, None, bsz=1536)

        def s2_ctx(c, n):
            h, j = ctxs[c]
            V, tV = Vb[c % 2]
            return stage2(n, h, NPROMPT + 32 * j, 32, 1056, V, tV, 16)

        if nctx:
            n0 = ctr["att"]
            ctr["att"] += nctx
            k0, v0 = kv_items(0)
            k1, _ = kv_items(1)
            interleave([items_gen(k0 + v0)])
            interleave([s1_ctx(0, n0), items_gen(k1)])
            for ci in range(nctx):
                k2, _ = kv_items(ci + 2)
                _, v1 = kv_items(ci + 1)
                interleave([s1_ctx(ci + 1, n0 + ci + 1) if ci + 1 < nctx else None, s2_ctx(ci, n0 + ci), items_gen(k2 + v1)])
        A.reset(mk_a)
        return mk, oT, oT_T, xT_extra

    def mix_ln_phase(mk, oT, oT_T, xT_extra):
        w_out_v = w["w_out"].rearrange("(ko ki) f -> ki ko f", ki=128)
        wob, r = A.alloc("wob", [128, 8, D], BF16)
        t_wob = [A.newT(r, f"wob{j}") for j in range(2)]
        for j in range(2):
            P.dma("pool", lambda e, j=j: e.dma_start(out=wob[:, :, j * 512:(j + 1) * 512], in_=w_out_v[:, 8:16, j * 512:(j + 1) * 512]), writes=[t_wob[j]])
        lng, t_lng = al("lng2", [128, D])
        lnb, t_lnb = al("lnb2", [128, D])
        P.dma("sp", lambda e: e.dma_start(out=lng[:], in_=lnp["ln2_g"][:, :]), writes=[t_lng])
        P.dma("sp", lambda e: e.dma_start(out=lnb[:], in_=lnp["ln2_b"][:, :]), writes=[t_lnb])
        bufs = {}
        for nm, shape, dt, nb in [("xres", [128, D], F32, 2), ("yv", [128, D], F32, 2), ("xo", [128, D], F32, 2), ("xob", [128, D], BF16, 2),
                                  ("st", [128, 12], F32, 2), ("mv", [128, 2], F32, 2), ("rs", [128, 2], F32, 2)]:
            bufs[nm] = [al(f"b3{nm}{i}", shape, dt) for i in range(nb)]
        def down3(s):
            t0, m = SUBT[s]
            xres, t_xres = bufs["xres"][s % 2]
            yv, t_yv = bufs["yv"][s % 2]
            P.dma("sp", lambda e: e.dma_start(out=xres[0:m, :], in_=r1d[t0:t0 + m, :]), reads=[r1d_T[s]], writes=[t_xres])
            for half in range(2):
                bk = 4 * (s % 2) + half

                def mmo(e, bk=bk, half=half):
                    for k in range(8):
                        i = e.matmul(bank(bk)[0:m, :], lhsT=oT[:, k, t0:t0 + m], rhs=wob[:, k, half * 512:(half + 1) * 512], start=(k == 0), stop=(k == 7))
                    return i
                P.op("pe", mmo, reads=[oT_T[h][s] for h in range(8)] + [t_wob[half]], writes=[bankT[bk]])
                yield
                P.op("dve", lambda e, bk=bk, half=half: e.tensor_tensor(
                    out=yv[0:m, half * 512:(half + 1) * 512], in0=bank(bk)[0:m, :], in1=xres[0:m, half * 512:(half + 1) * 512], op=ALU.add),
                    reads=[bankT[bk], t_xres], writes=[t_yv])
                yield

        def tail3(s):
            t0, m = SUBT[s]
            yv, t_yv = bufs["yv"][s % 2]
            return ln_tail(s, t0, m, yv, t_yv, bufs, lng, t_lng, lnb, t_lnb, x2d, x2d_T, True, xT_extra)

        interleave_g([down3(0)])
        for s in range(17):
            interleave_g([down3(s + 1) if s + 1 < 17 else None, tail3(s)])
        A.reset(mk)

    eps_t, r = A.alloc("eps_t", [128, 1], F32)
    t_eps = A.newT(r, "eps")
    P.op("dve", lambda e: e.memset(eps_t[:], EPS), writes=[t_eps])

    x1d_T = [T(f"x1d{s}") for s in range(17)]
    r1d_T = [T(f"r1d{s}") for s in range(17)]
    ffn_phase("f1", w["ffn1_w_gate"], w["ffn1_w_up"], w["ffn1_w_down"], lnp["ln1_g"], lnp["ln1_b"], xin, None, x1d, x1d_T, True)
    x2d_T = [T(f"x2d{s}") for s in range(17)]
    ssd_phase()
    if not STOP_B1:
        mix_ln_phase(*mla_phase())
        ffn_phase("f2", w["ffn2_w_gate"], w["ffn2_w_up"], w["ffn2_w_down"], lnp["ln3_g"], lnp["ln3_b"], x2d, x2d_T, y_out, None, False)

    P.build()
    nc._prog_stats = P.stats
    return nc


_NC_CACHE = {}


def _get_nc(debug=False):
    if debug not in _NC_CACHE:
        _NC_CACHE[debug] = build_program(debug)
    return _NC_CACHE[debug]


def make_in_maps(inputs):
    f32 = np.float32
    g = {k: np.asarray(v) for k, v in inputs.items()}
    shared = {}
    for nm in ["ffn1_w_gate", "ffn1_w_up", "ffn1_w_down", "ffn2_w_gate", "ffn2_w_up", "ffn2_w_down",
               "w_in", "w_uq", "w_ukv", "w_out"]:
        shared[nm] = np.ascontiguousarray(g[nm][0], dtype=f32)
    for nm in ["ln1_g", "ln1_b", "ln2_g", "ln2_b", "ln3_g", "ln3_b"]:
        shared[nm] = np.ascontiguousarray(np.broadcast_to(g[nm][0][None, :], (128, D)), dtype=f32)
    shared["ident"] = np.eye(128, dtype=f32)
    cwv = g["conv_w"][0]
    shared["cwp"] = np.ascontiguousarray(cwv.reshape(4, 12, 128).transpose(2, 1, 0), dtype=f32)
    shared["cbp"] = np.ascontiguousarray(g["conv_b"][0].reshape(12, 128).T, dtype=f32)
    for nm, key in [("dtb", "dt_bias"), ("alog", "a_log"), ("dsk", "d_skip")]:
        shared[nm] = np.ascontiguousarray(np.broadcast_to(g[key][0][None, :], (128, 16)), dtype=f32)
    shared["gss"] = np.ascontiguousarray(np.broadcast_to(g["ssd_norm_g"][0][None, :], (128, D)), dtype=f32)
    j = np.arange(128)[:, None]
    i = np.arange(128)[None, :]
    csts = np.zeros((128, 8, 128), f32)
    for v, ch in enumerate((64, 32)):
        same = (j // ch) == (i // ch)
        lim = 128 if ch == 64 else 64
        ok = (j < lim) & (i < lim)
        csts[:, 4 * v + 0, :] = (same & (j <= i) & ok)
        csts[:, 4 * v + 1, :] = (same & ok)
        csts[:, 4 * v + 2, :] = ((j // ch) == 0) & (j < lim)
        csts[:, 4 * v + 3, :] = ((j // ch) == 1) & (j < lim)
    shared["csts"] = csts
    nmp = np.where(((j // 64) == (i // 64)) & (i >= j), 0.0, -30000.0).astype(f32)
    shared["nmp"] = np.ascontiguousarray(np.tile(nmp, (1, 4)))
    nms = np.where(((j // 32) == (i // 32)) & (i >= j), 0.0, -30000.0).astype(f32)[:, 0:64]
    shared["nms"] = np.ascontiguousarray(np.tile(nms, (1, 4)))
    idl = np.zeros((128, 2, 64), f32)
    kk = np.arange(128)
    idl[kk, 0, kk % 64] = 1.0
    idl[kk[:64], 1, kk[:64] % 32] = 1.0
    shared["idl"] = idl
    il = np.arange(64)[None, :]
    nmlp = np.where(il >= (kk[:, None] % 64), 0.0, -30000.0).astype(f32)
    shared["nmlp"] = np.ascontiguousarray(np.tile(nmlp, (1, 4)))
    nmls = np.where(il[:, :32] >= (kk[:, None] % 32), 0.0, -30000.0).astype(f32)
    shared["nmls"] = np.ascontiguousarray(np.tile(nmls, (1, 4)))
    shared["gq"] = np.ascontiguousarray(np.broadcast_to(g["q_norm_g"][0][None, :], (128, 512)), dtype=f32)
    shared["gkv"] = np.ascontiguousarray(np.broadcast_to(g["kv_norm_g"][0][None, :], (128, 512)), dtype=f32)
    pos = np.concatenate([np.arange(NPROMPT), PAST + np.arange(32), PAST + np.arange(32)]).astype(np.float64)
    inv = 10000.0 ** (-np.arange(0, 64, 2, dtype=np.float64) / 64.0)
    ang = pos[:, None] * inv[None, :]
    cs_, sn_ = np.cos(ang).astype(f32), np.sin(ang).astype(f32)
    shared["cs4"] = np.ascontiguousarray(np.concatenate([cs_, sn_, sn_, cs_], axis=1))
    shared["cosT"] = np.ascontiguousarray(np.concatenate([cs_, cs_], axis=1).T)
    shared["sinT"] = np.ascontiguousarray(np.concatenate([sn_, sn_], axis=1).T)
    shared["maskq"] = np.where((j < 64) & (i >= 64), -30000.0, 0.0).astype(f32)
    maps = []
    for c in range(8):
        m = dict(shared)
        m["xin"] = np.ascontiguousarray(np.concatenate(
            [g["x_prompt"][c], g["x_sample"][2 * c], g["x_sample"][2 * c + 1]], axis=0), dtype=f32)
        m["sconv"] = np.ascontiguousarray(g["state_conv"][0, 2 * c:2 * c + 2], dtype=f32)
        m["sssm"] = np.ascontiguousarray(g["state_ssm"][0, 2 * c:2 * c + 2], dtype=f32)
        m["clat"] = np.ascontiguousarray(g["cache_latent"][0, 2 * c:2 * c + 2], dtype=f32)
        m["ckr"] = np.ascontiguousarray(g["cache_k_rope"][0, 2 * c:2 * c + 2], dtype=f32)
        maps.append(m)
    return maps


def kernel(**inputs):
    nc = _get_nc(False)
    maps = make_in_maps(inputs)
    res = run_bass_kernel_spmd(nc, maps, core_ids=list(range(8)))
    rr = res.results
    f32 = np.float32

    def prm(key, n0=NPROMPT):
        return np.stack([np.asarray(rr[c][key][0:n0], f32) for c in range(8)], axis=0)

    def smp(key):
        return np.stack([np.asarray(rr[c][key][NPROMPT + 32 * j:NPROMPT + 32 * (j + 1)], f32) for c in range(8) for j in range(2)], axis=0)

    y_p, y_s = prm("y"), smp("y")
    lat_p, lat_s = prm("lat_o")[None], smp("lat_o")[None]
    kr_p, kr_s = prm("kr_o")[None], smp("kr_o")[None]
    conv_p = np.stack([np.asarray(rr[c]["conv_o"][0], f32) for c in range(8)], axis=0)[None]
    conv_s = np.stack([np.asarray(rr[c]["conv_o"][1 + j], f32) for c in range(8) for j in range(2)], axis=0)[None]
    ssm_p = np.stack([np.asarray(rr[c]["ssm_o"][0], f32) for c in range(8)], axis=0)[None]
    ssm_s = np.stack([np.asarray(rr[c]["ssm_o"][1 + j], f32) for c in range(8) for j in range(2)], axis=0)[None]
    return (y_p, y_s, lat_p, kr_p, conv_p, ssm_p, lat_s, kr_s, conv_s, ssm_s)
```
